# Optimizing a Trainium2 kernel written in Bass

```python
import math
import jax
import jax.numpy as jnp
from jax import lax
import numpy as np

D_MODEL = 2048
BATCH = 2
SEQ = 4096
DEPTH = 4

GRID_W = 64
CTX_LEN = 256
EPS = 1e-6
ROPE_THETA = 10000.0

HEAD_DIM = 128
MIX_W = D_MODEL // 2

GDN_HEADS = MIX_W // HEAD_DIM
GDN_DK = HEAD_DIM
GDN_DV = HEAD_DIM
GDN_CHUNK = 64
SHORT_CONV = 3

SSM_HEAD_DIM = 64
SSM_HEADS = MIX_W // SSM_HEAD_DIM
SSM_GROUPS = 2
SSM_STATE = 128
SSM_CHUNK = 64
SSM_INNER = SSM_HEADS * SSM_HEAD_DIM
SSM_BC = SSM_GROUPS * SSM_STATE
SSM_XBC = SSM_INNER + 2 * SSM_BC

ATT_Q_HEADS = MIX_W // HEAD_DIM
ATT_KV_HEADS = 2
ATT_BLOCK = 128
WINDOW = 128

N_BRANCH = 3
D_FF = 5632
FFN_CONV = 3

IN_SPLITS = (
    3 * GDN_HEADS * HEAD_DIM,
    GDN_HEADS * GDN_DV,
    2 * GDN_HEADS,
    2 * GDN_HEADS,
    SSM_INNER,
    SSM_XBC,
    2 * SSM_HEADS,
    ATT_Q_HEADS * HEAD_DIM,
    2 * ATT_KV_HEADS * HEAD_DIM,
    N_BRANCH * D_MODEL,
)
P_IN = sum(IN_SPLITS)

kernel_name = 'hybrid_gdn_ssd_swa_block'

F32 = jnp.float32


def rmsnorm(x, w):
    xf = x.astype(F32)
    y = xf * lax.rsqrt(jnp.mean(xf * xf, axis=-1, keepdims=True) + EPS)
    return (y * w.astype(F32)).astype(x.dtype)


def l2norm(x):
    return x * lax.rsqrt(jnp.sum(x * x, axis=-1, keepdims=True) + EPS)


def modulate(x, shift, scale):
    return x * (1 + scale) + shift


def dwconv_centered(x, w):
    k = w.shape[0]
    pad = k // 2
    n = x.shape[1]
    xp = jnp.pad(x, ((0, 0), (pad, pad), (0, 0)))
    y = xp[:, 0:n] * w[0]
    for j in range(1, k):
        y = y + xp[:, j:j + n] * w[j]
    return y


def split_in(p):
    offs = [int(o) for o in np.cumsum(IN_SPLITS)[:-1]]
    return jnp.split(p, offs, axis=-1)


def gdn_inputs(qkv_raw, a_raw, b_raw, conv_w, a_log, dt_bias):
    bsz, n = qkv_raw.shape[:2]
    qkv = jax.nn.silu(dwconv_centered(qkv_raw, conv_w)).astype(F32)
    q, k, v = jnp.split(qkv, 3, axis=-1)
    q = l2norm(q.reshape(bsz, n, GDN_HEADS, GDN_DK)) * (GDN_DK ** -0.5)
    k = l2norm(k.reshape(bsz, n, GDN_HEADS, GDN_DK))
    v = v.reshape(bsz, n, GDN_HEADS, GDN_DV)
    a = a_raw.astype(F32).reshape(bsz, n, 2, GDN_HEADS)
    g = -jnp.exp(a_log.astype(F32)) * jax.nn.softplus(a + dt_bias.astype(F32))
    beta = jax.nn.sigmoid(b_raw.astype(F32).reshape(bsz, n, 2, GDN_HEADS))
    return q, k, v, g, beta


def gdn_chunked(q, k, v, g, beta, s0):
    bsz, n, h, _ = q.shape
    dv = v.shape[-1]
    cl = GDN_CHUNK
    nc = n // cl

    def chunks(t):
        return jnp.moveaxis(t.reshape((bsz, nc, cl) + t.shape[2:]), 3, 1)

    q, k, v, g, beta = (chunks(t) for t in (q, k, v, g, beta))
    g = jnp.cumsum(g, axis=-1)
    incl = jnp.tril(jnp.ones((cl, cl), bool))
    strict = jnp.tril(jnp.ones((cl, cl), bool), -1)
    decay = jnp.exp(jnp.where(incl, g[..., :, None] - g[..., None, :], -jnp.inf))
    kb = k * beta[..., None]
    lower = jnp.einsum('bhnid,bhnjd->bhnij', kb, k) * jnp.where(strict, decay, 0.0)
    rhs = jnp.concatenate([v * beta[..., None], kb * jnp.exp(g)[..., None]], axis=-1)
    sol = lax.linalg.triangular_solve(lower + jnp.eye(cl, dtype=lower.dtype), rhs,
                                      left_side=True, lower=True)
    u, w = sol[..., :dv], sol[..., dv:]
    attn = jnp.einsum('bhnid,bhnjd->bhnij', q, k) * decay
    q_dec = q * jnp.exp(g)[..., None]
    k_dec = k * jnp.exp(g[..., -1:] - g)[..., None]
    last = jnp.exp(g[..., -1])

    def step(s, xs):
        q_c, a_c, u_c, w_c, k_c, l_c = xs
        v_new = u_c - jnp.einsum('bhcd,bhdv->bhcv', w_c, s)
        o = jnp.einsum('bhcd,bhdv->bhcv', q_c, s) + jnp.einsum('bhij,bhjv->bhiv', a_c, v_new)
        s = s * l_c[..., None, None] + jnp.einsum('bhcd,bhcv->bhdv', k_c, v_new)
        return s, o

    xs = tuple(jnp.moveaxis(t, 2, 0) for t in (q_dec, attn, u, w, k_dec, last))
    s_final, o = lax.scan(step, s0, xs)
    o = jnp.transpose(o, (1, 0, 3, 2, 4)).reshape(bsz, n, h, dv)
    return o, s_final


def gdn_output(o, gate, norm_w):
    bsz, n = gate.shape[:2]
    y = rmsnorm(o, norm_w) * jax.nn.silu(gate.astype(F32).reshape(bsz, n, GDN_HEADS, GDN_DV))
    return y.reshape(bsz, n, GDN_HEADS * GDN_DV).astype(gate.dtype)


def ssd_inputs(xbc_raw, dt_raw, conv_w, conv_b, a_log, dt_bias):
    bsz, n = xbc_raw.shape[:2]
    xbc = jax.nn.silu(dwconv_centered(xbc_raw, conv_w) + conv_b).astype(F32)
    xs, bm, cm = jnp.split(xbc, [SSM_INNER, SSM_INNER + SSM_BC], axis=-1)
    xs = xs.reshape(bsz, n, SSM_HEADS, SSM_HEAD_DIM)
    bm = bm.reshape(bsz, n, SSM_GROUPS, SSM_STATE)
    cm = cm.reshape(bsz, n, SSM_GROUPS, SSM_STATE)
    dt = jax.nn.softplus(dt_raw.astype(F32).reshape(bsz, n, 2, SSM_HEADS) + dt_bias.astype(F32))
    log_decay = -jnp.exp(a_log.astype(F32)) * dt
    return xs, dt, log_decay, bm, cm


def ssd_chunked(x, dt, a, bm, cm, h0):
    bsz, n, h, p = x.shape
    grp = bm.shape[2]
    rep = h // grp
    cl = SSM_CHUNK
    nc = n // cl
    xdt = (x * dt[..., None]).reshape(bsz, nc, cl, grp, rep, p)
    a_cum = jnp.cumsum(a.reshape(bsz, nc, cl, grp, rep), axis=2)
    bc = bm.reshape(bsz, nc, cl, grp, -1)
    cc = cm.reshape(bsz, nc, cl, grp, -1)
    incl = jnp.tril(jnp.ones((cl, cl), bool))[:, :, None, None]
    seg = jnp.exp(jnp.where(incl, a_cum[:, :, :, None] - a_cum[:, :, None, :], -jnp.inf))
    cb = jnp.einsum('bnigs,bnjgs->bnijg', cc, bc)
    y_diag = jnp.einsum('bnijg,bnijgr,bnjgrp->bnigrp', cb, seg, xdt)
    to_end = jnp.exp(a_cum[:, :, -1:] - a_cum)
    states = jnp.einsum('bnjgs,bnjgr,bnjgrp->bngrsp', bc, to_end, xdt)
    chunk_decay = jnp.exp(a_cum[:, :, -1])

    def step(hs, xs):
        st, dec = xs
        return hs * dec[..., None, None] + st, hs

    h_final, h_prev = lax.scan(step, h0, (jnp.moveaxis(states, 1, 0), jnp.moveaxis(chunk_decay, 1, 0)))
    h_prev = jnp.moveaxis(h_prev, 0, 1)
    y_off = jnp.einsum('bnigs,bnigr,bngrsp->bnigrp', cc, jnp.exp(a_cum), h_prev)
    return (y_diag + y_off).reshape(bsz, n, h, p), h_final


def ssd_output(y, xs, d_skip, z, norm_w):
    bsz, n = z.shape[:2]
    y = y + d_skip.astype(F32)[:, None] * xs
    y = y.reshape(bsz, n, SSM_INNER) * jax.nn.silu(z.astype(F32))
    y = rmsnorm(y.reshape(bsz, n, SSM_GROUPS, SSM_INNER // SSM_GROUPS),
                norm_w.reshape(SSM_GROUPS, SSM_INNER // SSM_GROUPS))
    return y.reshape(bsz, n, SSM_INNER).astype(z.dtype)


def bidirectional_scan(scan_fn, lat_fwd, lat_bwd, ctx_fwd, ctx_bwd, state0):
    flip = lambda ts: tuple(jnp.flip(t, axis=1) for t in ts)
    oc_f, sc_f = scan_fn(*ctx_fwd, state0)
    ol_f, _ = scan_fn(*lat_fwd, sc_f)
    oc_b, sc_b = scan_fn(*flip(ctx_bwd), state0)
    ol_b, _ = scan_fn(*flip(lat_bwd), sc_b)
    return ol_f + jnp.flip(ol_b, axis=1), oc_f + jnp.flip(oc_b, axis=1)


def attn_heads(q_raw, kv_raw):
    bsz, n = q_raw.shape[:2]
    k, v = jnp.split(kv_raw, 2, axis=-1)
    return (q_raw.reshape(bsz, n, ATT_Q_HEADS, HEAD_DIM),
            k.reshape(bsz, n, ATT_KV_HEADS, HEAD_DIM),
            v.reshape(bsz, n, ATT_KV_HEADS, HEAD_DIM))


def axial_rope_tables(n_tokens):
    rows = n_tokens // GRID_W
    row = jnp.repeat(jnp.arange(rows, dtype=F32), GRID_W)
    col = jnp.tile(jnp.arange(GRID_W, dtype=F32), rows)
    n_freq = HEAD_DIM // 4
    inv_freq = ROPE_THETA ** (-jnp.arange(n_freq, dtype=F32) / n_freq)
    ang = jnp.stack([row[:, None] * inv_freq, col[:, None] * inv_freq], axis=1)
    return jnp.cos(ang), jnp.sin(ang)


def apply_axial_rope(x, cos, sin):
    bsz, n, h, d = x.shape
    xr = x.astype(F32).reshape(bsz, n, h, 2, 2, d // 4)
    x1, x2 = xr[..., 0, :], xr[..., 1, :]
    c = cos[None, :, None]
    s = sin[None, :, None]
    out = jnp.stack([x1 * c - x2 * s, x2 * c + x1 * s], axis=-2)
    return out.reshape(bsz, n, h, d).astype(x.dtype)


def softmax_with_sink(logits, sink):
    shape = (ATT_KV_HEADS, ATT_Q_HEADS // ATT_KV_HEADS) + (1,) * (logits.ndim - 3)
    sink_col = jnp.broadcast_to(sink.astype(F32).reshape(shape), logits.shape[:-1] + (1,))
    p = jax.nn.softmax(jnp.concatenate([logits, sink_col], axis=-1), axis=-1)
    return p[..., :-1]


def band_attention(q, k, v, kc, vc, sink):
    bsz, n, hq, dh = q.shape
    grp = hq // ATT_KV_HEADS
    nb = n // ATT_BLOCK
    n_ctx = kc.shape[1]
    scale = dh ** -0.5
    qb = q.reshape(bsz, nb, ATT_BLOCK, ATT_KV_HEADS, grp, dh)

    def band(t):
        tp = jnp.pad(t, ((0, 0), (ATT_BLOCK, ATT_BLOCK), (0, 0), (0, 0)))
        views = [tp[:, i * ATT_BLOCK:i * ATT_BLOCK + n].reshape(bsz, nb, ATT_BLOCK, ATT_KV_HEADS, dh)
                 for i in range(3)]
        return jnp.concatenate(views, axis=2)

    kb, vb = band(k), band(v)
    s_loc = jnp.einsum('bnqhgd,bnkhd->bhgnqk', qb, kb, preferred_element_type=F32) * scale
    s_ctx = jnp.einsum('bnqhgd,blhd->bhgnql', qb, kc, preferred_element_type=F32) * scale
    qi = jnp.arange(ATT_BLOCK)[:, None]
    kj = jnp.arange(3 * ATT_BLOCK)[None, :]
    rel = kj - ATT_BLOCK - qi
    pos = (jnp.arange(nb)[:, None, None] - 1) * ATT_BLOCK + kj[None]
    valid = (jnp.abs(rel) <= WINDOW)[None] & (pos >= 0) & (pos < n)
    s_loc = jnp.where(valid, s_loc, -jnp.inf)
    p = softmax_with_sink(jnp.concatenate([s_loc, s_ctx], axis=-1), sink)
    p_loc, p_ctx = p[..., :3 * ATT_BLOCK], p[..., 3 * ATT_BLOCK:3 * ATT_BLOCK + n_ctx]
    o = (jnp.einsum('bhgnqk,bnkhd->bnqhgd', p_loc, vb.astype(F32))
         + jnp.einsum('bhgnql,blhd->bnqhgd', p_ctx, vc.astype(F32)))
    return o.reshape(bsz, n, hq * dh).astype(q.dtype)


def context_attention(qc, kc, vc, sink):
    bsz, n_ctx, hq, dh = qc.shape
    grp = hq // ATT_KV_HEADS
    qg = qc.reshape(bsz, n_ctx, ATT_KV_HEADS, grp, dh)
    s = jnp.einsum('blhgd,bmhd->bhglm', qg, kc, preferred_element_type=F32) * (dh ** -0.5)
    p = softmax_with_sink(s, sink)
    o = jnp.einsum('bhglm,bmhd->blhgd', p, vc.astype(F32))
    return o.reshape(bsz, n_ctx, hq * dh).astype(qc.dtype)


def merge_branches(ya, yb, yc, gate_raw, w_a, w_b, w_c, w_o):
    ga, gb, gc = jnp.split(jax.nn.sigmoid(gate_raw), N_BRANCH, axis=-1)
    m = ga * (ya @ w_a) + gb * (yb @ w_b) + gc * (yc @ w_c)
    return m @ w_o


def conv_ffn(h, w_up, conv_w, conv_b, w_down):
    u, g = jnp.split(h @ w_up, 2, axis=-1)
    g = dwconv_centered(g, conv_w) + conv_b
    return (jax.nn.silu(g) * u) @ w_down


def _dt_bias(key, shape):
    u = jax.random.uniform(key, shape, F32)
    dt = jnp.exp(u * (math.log(0.1) - math.log(0.001)) + math.log(0.001))
    return dt + jnp.log(-jnp.expm1(-dt))


def setup_inputs(seed: int = 0) -> dict:
    key = jax.random.key(seed)
    ks = iter(jax.random.split(key, 40))
    L, D = DEPTH, D_MODEL

    def nrm(shape, scale):
        return jax.random.normal(next(ks), shape, F32) * scale

    def gain(shape):
        return 1.0 + nrm(shape, 0.02)

    def a_log(shape):
        return jnp.log(jax.random.uniform(next(ks), shape, F32, minval=1.0, maxval=16.0))

    return {
        'x': nrm((BATCH, SEQ, D), 1.0),
        'c': nrm((BATCH, D), 1.0),
        'ctx': nrm((BATCH, CTX_LEN, D), 1.0),
        'c_ctx': nrm((D,), 1.0),
        'w_ada': nrm((L, D, 6 * D), 0.5 * D ** -0.5),
        'b_ada': nrm((L, 6 * D), 0.02),
        'g_pre1': gain((L, D)),
        'w_in': nrm((L, D, P_IN), D ** -0.5),
        'gdn_conv_w': nrm((L, SHORT_CONV, 3 * GDN_HEADS * HEAD_DIM), SHORT_CONV ** -0.5),
        'gdn_a_log': a_log((L, 2, GDN_HEADS)),
        'gdn_dt_bias': _dt_bias(next(ks), (L, 2, GDN_HEADS)),
        'gdn_norm': gain((L, GDN_DV)),
        'ssm_conv_w': nrm((L, SHORT_CONV, SSM_XBC), SHORT_CONV ** -0.5),
        'ssm_conv_b': nrm((L, SSM_XBC), 0.02),
        'ssm_a_log': a_log((L, 2, SSM_HEADS)),
        'ssm_dt_bias': _dt_bias(next(ks), (L, 2, SSM_HEADS)),
        'ssm_d': 1.0 + nrm((L, SSM_HEADS), 0.1),
        'ssm_norm': gain((L, SSM_INNER)),
        'attn_sink': nrm((L, ATT_Q_HEADS), 0.5),
        'w_branch_a': nrm((L, GDN_HEADS * GDN_DV, D), (GDN_HEADS * GDN_DV) ** -0.5),
        'w_branch_b': nrm((L, SSM_INNER, D), SSM_INNER ** -0.5),
        'w_branch_c': nrm((L, ATT_Q_HEADS * HEAD_DIM, D), (ATT_Q_HEADS * HEAD_DIM) ** -0.5),
        'w_out': nrm((L, D, D), D ** -0.5),
        'g_post1': gain((L, D)),
        'g_pre2': gain((L, D)),
        'w_up': nrm((L, D, 2 * D_FF), D ** -0.5),
        'ffn_conv_w': nrm((L, FFN_CONV, D_FF), FFN_CONV ** -0.5),
        'ffn_conv_b': nrm((L, D_FF), 0.02),
        'w_down': nrm((L, D_FF, D), D_FF ** -0.5),
        'g_post2': gain((L, D)),
    }


def reference(x, c, ctx, c_ctx, w_ada, b_ada, g_pre1, w_in, gdn_conv_w, gdn_a_log, gdn_dt_bias,
              gdn_norm, ssm_conv_w, ssm_conv_b, ssm_a_log, ssm_dt_bias, ssm_d, ssm_norm, attn_sink,
              w_branch_a, w_branch_b, w_branch_c, w_out, g_post1, g_pre2, w_up, ffn_conv_w,
              ffn_conv_b, w_down, g_post2):
    bsz, n_lat = x.shape[:2]
    cos, sin = axial_rope_tables(n_lat)
    gdn_s0 = jnp.zeros((bsz, GDN_HEADS, GDN_DK, GDN_DV), F32)
    ssd_s0 = jnp.zeros((bsz, SSM_GROUPS, SSM_HEADS // SSM_GROUPS, SSM_STATE, SSM_HEAD_DIM), F32)
    xc = ctx
    for l in range(DEPTH):
        last = l == DEPTH - 1
        mod = jax.nn.silu(c) @ w_ada[l] + b_ada[l]
        mod_c = jax.nn.silu(c_ctx) @ w_ada[l] + b_ada[l]
        sh1, sc1, gt1, sh2, sc2, gt2 = jnp.split(mod[:, None, :], 6, axis=-1)
        sh1c, sc1c, gt1c, sh2c, sc2c, gt2c = jnp.split(mod_c, 6, axis=-1)

        h = modulate(rmsnorm(x, g_pre1[l]), sh1, sc1)
        hc = modulate(rmsnorm(xc, g_pre1[l]), sh1c, sc1c)
        (qkv, gate_a, a_in, b_in, z, xbc, dt_in, q_in, kv_in, mg_in) = split_in(h @ w_in[l])
        (qkv_c, gate_a_c, a_in_c, b_in_c, z_c, xbc_c, dt_in_c, q_in_c, kv_in_c, mg_in_c) = split_in(hc @ w_in[l])

        gq, gk, gv, gg, gb = gdn_inputs(qkv, a_in, b_in, gdn_conv_w[l], gdn_a_log[l], gdn_dt_bias[l])
        cq, ck, cv, cg, cb = gdn_inputs(qkv_c, a_in_c, b_in_c, gdn_conv_w[l], gdn_a_log[l], gdn_dt_bias[l])
        oa, oa_c = bidirectional_scan(
            gdn_chunked,
            (gq, gk, gv, gg[:, :, 0], gb[:, :, 0]), (gq, gk, gv, gg[:, :, 1], gb[:, :, 1]),
            (cq, ck, cv, cg[:, :, 0], cb[:, :, 0]), (cq, ck, cv, cg[:, :, 1], cb[:, :, 1]),
            gdn_s0)
        ya = gdn_output(oa, gate_a, gdn_norm[l])

        sx, sdt, sa, sbm, scm = ssd_inputs(xbc, dt_in, ssm_conv_w[l], ssm_conv_b[l], ssm_a_log[l], ssm_dt_bias[l])
        tx, tdt, ta, tbm, tcm = ssd_inputs(xbc_c, dt_in_c, ssm_conv_w[l], ssm_conv_b[l], ssm_a_log[l], ssm_dt_bias[l])
        ob, ob_c = bidirectional_scan(
            ssd_chunked,
            (sx, sdt[:, :, 0], sa[:, :, 0], sbm, scm), (sx, sdt[:, :, 1], sa[:, :, 1], sbm, scm),
            (tx, tdt[:, :, 0], ta[:, :, 0], tbm, tcm), (tx, tdt[:, :, 1], ta[:, :, 1], tbm, tcm),
            ssd_s0)
        yb = ssd_output(ob, sx, ssm_d[l], z, ssm_norm[l])

        aq, ak, av = attn_heads(q_in, kv_in)
        aq_c, ak_c, av_c = attn_heads(q_in_c, kv_in_c)
        yc = band_attention(apply_axial_rope(aq, cos, sin), apply_axial_rope(ak, cos, sin), av,
                            ak_c, av_c, attn_sink[l])

        mix = merge_branches(ya, yb, yc, mg_in, w_branch_a[l], w_branch_b[l], w_branch_c[l], w_out[l])
        x = x + gt1 * rmsnorm(mix, g_post1[l])
        ff = conv_ffn(modulate(rmsnorm(x, g_pre2[l]), sh2, sc2), w_up[l], ffn_conv_w[l], ffn_conv_b[l], w_down[l])
        x = x + gt2 * rmsnorm(ff, g_post2[l])

        if not last:
            ya_c = gdn_output(oa_c, gate_a_c, gdn_norm[l])
            yb_c = ssd_output(ob_c, tx, ssm_d[l], z_c, ssm_norm[l])
            yc_c = context_attention(aq_c, ak_c, av_c, attn_sink[l])
            mix_c = merge_branches(ya_c, yb_c, yc_c, mg_in_c, w_branch_a[l], w_branch_b[l], w_branch_c[l], w_out[l])
            xc = xc + gt1c * rmsnorm(mix_c, g_post1[l])
            ff_c = conv_ffn(modulate(rmsnorm(xc, g_pre2[l]), sh2c, sc2c), w_up[l], ffn_conv_w[l], ffn_conv_b[l], w_down[l])
            xc = xc + gt2c * rmsnorm(ff_c, g_post2[l])
    return x
```

```python
import numpy as np
import concourse.bass as bass
import concourse.mybir as mybir

F32 = mybir.dt.float32
BF16 = mybir.dt.bfloat16
AF = mybir.ActivationFunctionType
ALU = mybir.AluOpType
AX = mybir.AxisListType

ENGS = ("pe", "act", "dve", "pool", "sp")


class T:
    __slots__ = ("name", "last_w", "readers", "sem", "semcnt", "excl")

    def __init__(self, name):
        self.name = name
        self.last_w = None
        self.readers = []
        self.sem = None
        self.semcnt = 0
        self.excl = False


class Op:
    __slots__ = ("eng", "fn", "waits", "signal", "dma_tile", "idx")

    def __init__(self, eng, fn):
        self.eng = eng
        self.fn = fn
        self.waits = []
        self.signal = False
        self.dma_tile = None


class Sched:
    def __init__(self, same_engine_sync=True):
        self.ops = {e: [] for e in ENGS}
        self.seen = {e: {} for e in ENGS}
        self.same = same_engine_sync
        self.dma_tiles = []
        self.final_waits = []

    def _dep(self, op, d):
        if d is None:
            return
        if d[0] == 'e':
            _, eng, idx = d
            if eng == op.eng:
                if eng == "pe" or not self.same or eng == "sp":
                    return
                if idx >= len(self.ops[eng]):
                    return
            key = ('e', eng)
            cnt = idx
            self.ops[eng][idx].signal = True
        else:
            _, tile, cnt = d
            cnt = tile.semcnt
            d = ('d', tile, cnt)
            key = ('d', id(tile))
        seen = self.seen[op.eng]
        if seen.get(key, -1) >= cnt:
            return
        seen[key] = cnt
        op.waits.append(d)

    def op(self, eng, fn, reads=(), writes=()):
        o = Op(eng, fn)
        o.idx = len(self.ops[eng])
        for t in reads:
            self._dep(o, t.last_w)
            if t.excl:
                for r in t.readers:
                    if r[0] == 'e' and r[1] != eng:
                        self._dep(o, r)
        for t in writes:
            self._dep(o, t.last_w)
            for r in t.readers:
                self._dep(o, r)
        self.ops[eng].append(o)
        me = ('e', eng, o.idx)
        for t in reads:
            t.readers.append(me)
        for t in writes:
            t.last_w = me
            t.readers = []
        return o

    def dma(self, eng, fn, reads=(), writes=(), sem_tile=None):
        o = Op(eng, fn)
        o.idx = len(self.ops[eng])
        for t in reads:
            self._dep(o, t.last_w)
        for t in writes:
            self._dep(o, t.last_w)
            for r in t.readers:
                self._dep(o, r)
        st = sem_tile or (writes[0] if writes else reads[0])
        if st.sem is None:
            st.sem = True
            self.dma_tiles.append(st)
        st.semcnt += 16
        o.dma_tile = st
        self.ops[eng].append(o)
        me = ('d', st, st.semcnt)
        for t in reads:
            t.readers.append(me)
        for t in writes:
            t.last_w = me
            t.readers = []
        return o

    def finish(self, eng, tiles):
        def nopfn(e):
            return None
        o = Op(eng, None)
        o.idx = len(self.ops[eng])
        for t in tiles:
            self._dep(o, t.last_w)
            for r in t.readers:
                self._dep(o, r)
        self.ops[eng].append(o)

    def emit(self, nc, stack):
        esem = {}
        for e in ENGS:
            if any(o.signal for o in self.ops[e]):
                esem[e] = stack.enter_context(nc.semaphore("s_" + e))
        for t in self.dma_tiles:
            t.sem = stack.enter_context(nc.semaphore("d_" + t.name))
        cnt_at = {}
        for e in ENGS:
            c = 0
            for o in self.ops[e]:
                if o.signal:
                    c += 1
                    cnt_at[(e, o.idx)] = c
        block = stack.enter_context(nc.Block())

        def run(e, handle):
            for o in self.ops[e]:
                for d in o.waits:
                    if d[0] == 'e':
                        handle.wait_ge(esem[d[1]], cnt_at[(d[1], d[2])])
                    else:
                        handle.wait_ge(d[1].sem, d[2])
                if o.fn is None:
                    continue
                ins = o.fn(handle)
                if o.dma_tile is not None:
                    ins.then_inc(o.dma_tile.sem, 16)
                elif o.signal:
                    ins.then_inc(esem[e], 1)

        @block.tensor
        def _(h):
            run("pe", h)

        @block.scalar
        def _(h):
            run("act", h)

        @block.vector
        def _(h):
            run("dve", h)

        @block.gpsimd
        def _(h):
            run("pool", h)

        @block.sync
        def _(h):
            run("sp", h)


from contextlib import ExitStack
import numpy as np
from concourse.bass_utils import run_bass_kernel_spmd

D = 2048
KC = 16
EPS = 1e-6


class Ctx:
    def __init__(self):
        self.nc = bass.Bass("TRN2", target_bir_lowering=False)
        self.S = Sched()
        self.st = ExitStack()
        self.n = 0

    def sb(self, shape, dt=F32, name=None):
        self.n += 1
        name = name or f"sb{self.n}"
        t = self.st.enter_context(self.nc.sbuf_tensor(name, list(shape), dt))
        return t, T(name)

    def ps(self, shape=(128, 512), dt=F32, name=None):
        self.n += 1
        name = name or f"ps{self.n}"
        t = self.st.enter_context(self.nc.psum_tensor(name, list(shape), dt))
        tt = T(name)
        tt.excl = True
        return t, tt

    def dram_in(self, name, shape, dt=F32):
        return self.nc.dram_tensor(name, list(shape), dt, kind="ExternalInput").ap()

    def dram_out(self, name, shape, dt=F32):
        return self.nc.dram_tensor(name, list(shape), dt, kind="ExternalOutput").ap()

    def finish(self, out_tiles):
        self.S.finish("sp", out_tiles)
        self.S.emit(self.nc, self.st)
        self.st.close()
        return self.nc


def consts(C):
    ones, To = C.sb([128, 128], F32, "ones")
    C.S.op("pool", lambda e: e.memset(ones[:], 1.0), writes=[To])
    return ones, To


def norm_mod(C, xt, Tx, n, ones, To, eff, sh, Tv, h, Th, pss, Tps, scr):
    S = C.S
    for c in range(KC):
        sq, Tsq = scr['sq'][c % 2]
        S.op("act", lambda e, c=c, sq=sq: e.activation(out=sq[:, :n], in_=xt[:, c, :n], func=AF.Square),
             reads=[Tx], writes=[Tsq])
        S.op("pe", lambda e, c=c, sq=sq: e.matmul(pss[:, :n], lhsT=ones[:], rhs=sq[:, :n],
                                                  start=(c == 0), stop=(c == KC - 1)),
             reads=[Tsq, To], writes=[Tps])
    rstd, Tr = scr['rstd']
    S.op("act", lambda e: e.activation(out=rstd[:, :n], in_=pss[:, :n], func=AF.Sqrt, scale=1.0 / D, bias=scr['eps'][0][:, 0:1]),
         reads=[Tps, scr['eps'][1]], writes=[Tr])
    S.op("dve", lambda e: e.reciprocal(out=rstd[:, :n], in_=rstd[:, :n]), reads=[Tr], writes=[Tr])
    for c in range(KC):
        tmp, Tt = scr['tmp'][c % 2]
        S.op("dve", lambda e, c=c, tmp=tmp: e.tensor_tensor(out=tmp[:, :n], in0=xt[:, c, :n], in1=rstd[:, :n], op=ALU.mult),
             reads=[Tx, Tr], writes=[Tt])
        S.op("act", lambda e, c=c, tmp=tmp: e.activation(out=h[:, c, :n], in_=tmp[:, :n], func=AF.Identity,
                                                        scale=eff[:, c:c + 1], bias=sh[:, c:c + 1]),
             reads=[Tt, Tv], writes=[Th])


def make_scr(C):
    scr = {}
    scr['sq'] = [C.sb([128, 512], F32) for _ in range(2)]
    scr['tmp'] = [C.sb([128, 512], F32) for _ in range(2)]
    scr['rstd'] = C.sb([128, 512], F32)
    scr['eps'] = C.sb([128, 1], F32, "epsc")
    C.S.op("pool", lambda e: e.memset(scr['eps'][0][:], EPS), writes=[scr['eps'][1]])
    return scr


NMIX = 6208
A_TILES = [(0, 512, 0), (512, 512, 0), (1024, 64, 1)]
NT_A = 1088


def build_A():
    C = Ctx()
    nc, S = C.nc, C.S
    xT = C.dram_in("xT", [D, NT_A])
    vecs = C.dram_in("vecs", [128, 5, 16])
    wmix = C.dram_in("wmix", [D, NMIX])
    pT = C.dram_out("pT", [NMIX, NT_A])
    ones, To = consts(C)
    scr = make_scr(C)
    vt, Tv = C.sb([128, 5, 16], F32, "vecs_sb")
    eff, _ = C.sb([128, 2, 16], F32, "eff")
    S.dma("sp", lambda e: e.dma_start(out=vt[:], in_=vecs[:, :, :]), writes=[Tv])
    for s, (isc) in enumerate((1, 3)):
        S.op("dve", lambda e, s=s, isc=isc: e.scalar_tensor_tensor(out=eff[:, s, :], in0=vt[:, isc, :], scalar=1.0, in1=vt[:, 0, :],
                                                                   op0=ALU.add, op1=ALU.mult), reads=[Tv], writes=[Tv])
    h, Th = C.sb([128, KC, NT_A], BF16, "h")
    xv = xT.rearrange("(c p) t -> p c t", p=128)
    pss, Tps = C.ps(name="pss")
    xbufs = [C.sb([128, KC, 512], F32) for _ in range(2)]
    for ti, (t0, n, vs) in enumerate(A_TILES):
        xt, Tx = xbufs[ti % 2]
        S.dma("sp", lambda e, xt=xt, t0=t0, n=n: e.dma_start(out=xt[:, :, :n], in_=xv[:, :, t0:t0 + n]), writes=[Tx])
        hv = h[:, :, t0:t0 + n]
        norm_mod(C, xt, Tx, n, ones, To, eff[:, vs, :], vt[:, 2 + 2 * vs, :], Tv, hv, Th, pss, Tps, scr)
    wv = wmix.rearrange("(c p) n -> p c n", p=128)
    wb = [C.sb([128, KC, 512], BF16) for _ in range(3)]
    pbs = [C.ps() for _ in range(4)]
    obs = [C.sb([128, 512], F32) for _ in range(4)]
    nslab = (NMIX + 511) // 512
    g = 0
    for s in range(nslab):
        c0 = s * 512
        cw = min(512, NMIX - c0)
        w, Tw = wb[s % 3]
        S.dma("pool", lambda e, w=w, c0=c0, cw=cw: e.dma_start(out=w[:, :, :cw], in_=wv[:, :, c0:c0 + cw]), writes=[Tw])
        for (t0, n, vs) in A_TILES:
            for oc in range((cw + 127) // 128):
                m = min(128, cw - oc * 128)
                pb, Tp = pbs[g % 4]
                ob, Tob = obs[g % 4]
                for k in range(KC):
                    S.op("pe", lambda e, pb=pb, w=w, k=k, oc=oc, m=m, t0=t0, n=n: e.matmul(
                        pb[:m, :n], lhsT=w[:, k, oc * 128:oc * 128 + m], rhs=h[:, k, t0:t0 + n],
                        start=(k == 0), stop=(k == KC - 1)), reads=[Tw, Th], writes=[Tp])
                ev = "act" if g % 2 == 0 else "dve"
                if ev == "act":
                    S.op("act", lambda e, pb=pb, ob=ob, m=m, n=n: e.copy(out=ob[:m, :n], in_=pb[:m, :n]), reads=[Tp], writes=[Tob])
                else:
                    S.op("dve", lambda e, pb=pb, ob=ob, m=m, n=n: e.tensor_copy(out=ob[:m, :n], in_=pb[:m, :n]), reads=[Tp], writes=[Tob])
                r0 = c0 + oc * 128
                S.dma("sp", lambda e, ob=ob, r0=r0, m=m, t0=t0, n=n: e.dma_start(out=pT[r0:r0 + m, t0:t0 + n], in_=ob[:m, :n]),
                      reads=[Tob])
                g += 1
    return C.finish([t for _, t in obs])


POOLMAP = {}
def _mm(C, out, lhsT, rhs, R, W, start=True, stop=True):
    return C.S.op("pe", lambda e: e.matmul(out, lhsT=lhsT, rhs=rhs, start=start, stop=stop), reads=R, writes=W)


def _tr(C, out, in_, ident, R, W):
    return C.S.op("pe", lambda e: e.transpose(out, in_, ident), reads=R, writes=W)


def _act(C, out, in_, func, R, W, scale=None, bias=None, accum=None):
    kw = {}
    if scale is not None:
        kw["scale"] = scale
    if bias is not None:
        kw["bias"] = bias
    if accum is not None:
        kw["accum_out"] = accum
    return C.S.op("act", lambda e: e.activation(out=out, in_=in_, func=func, **kw), reads=R, writes=W)


def _tt(C, eng, out, in0, in1, op, R, W):
    eng = POOLMAP.get(eng, eng)
    return C.S.op(eng, lambda e: e.tensor_tensor(out=out, in0=in0, in1=in1, op=op), reads=R, writes=W)


def _ts(C, eng, out, in0, s1, s2, op0, op1, R, W):
    eng = POOLMAP.get(eng, eng)
    if op1 is None:
        return C.S.op(eng, lambda e: e.tensor_scalar(out=out, in0=in0, scalar1=s1, scalar2=None, op0=op0), reads=R, writes=W)
    return C.S.op(eng, lambda e: e.tensor_scalar(out=out, in0=in0, scalar1=s1, scalar2=s2, op0=op0, op1=op1), reads=R, writes=W)


def _stt(C, out, in0, scalar, in1, op0, op1, R, W):
    return C.S.op("dve", lambda e: e.scalar_tensor_tensor(out=out, in0=in0, scalar=scalar, in1=in1, op0=op0, op1=op1), reads=R, writes=W)


def _cp(C, eng, out, in_, R, W):
    if eng == "act":
        return C.S.op("act", lambda e: e.copy(out=out, in_=in_), reads=R, writes=W)
    return C.S.op(eng, lambda e: e.tensor_copy(out=out, in_=in_), reads=R, writes=W)


class Ring:
    def __init__(self, items):
        self.items = items
        self.i = 0

    def get(self):
        it = self.items[self.i % len(self.items)]
        self.i += 1
        return it


TB = 4352
NCH = 34
NTOK = 2 * TB
CI, CLO, CUP, CSLO, CSUP, CTLO, CTUP, CRM = [i * 128 for i in range(8)]
NCST = 8 * 128
MUL, ADD, SUB, MAX, MIN = ALU.mult, ALU.add, ALU.subtract, ALU.max, ALU.min


def b_consts():
    p = np.arange(128)[:, None]
    f = np.arange(128)[None, :]
    NEG = -30000.0
    cst = np.zeros((128, NCST), np.float32)
    cst[:, CI:CI + 128] = (p == f)
    cst[:, CLO:CLO + 128] = np.where(f <= p, 0.0, NEG)
    cst[:, CUP:CUP + 128] = np.where(f >= p, 0.0, NEG)
    cst[:, CSLO:CSLO + 128] = (f < p)
    cst[:, CSUP:CSUP + 128] = (f > p)
    cst[:, CTLO:CTLO + 128] = (p >= f)
    cst[:, CTUP:CTUP + 128] = (p <= f)
    rm = np.zeros((128, 128), np.float32)
    for base in (0, 64):
        for m in range(32):
            rm[base + m + 32, base + m] = -1.0
            rm[base + m, base + m + 32] = 1.0
    cst[:, CRM:CRM + 128] = rm
    t = np.arange(4096)
    row = (t // 64).astype(np.float32)
    col = (t % 64).astype(np.float32)
    inv = (10000.0 ** (-np.arange(32, dtype=np.float32) / 32)).astype(np.float32)
    cosF = np.zeros((128, 4096), np.float32)
    sinF = np.zeros((128, 4096), np.float32)
    for pp in range(128):
        pos = row if pp < 64 else col
        ang = (pos * inv[pp % 32]).astype(np.float32)
        cosF[pp] = np.cos(ang)
        sinF[pp] = np.sin(ang)
    return cst, cosF, sinF


def build_B(which=("gdn", "ssd", "attn"), batches=(0, 1)):
    extra_fns = {}
    C = Ctx()
    nc, S = C.nc, C.S
    names = ["gq", "gk", "gv", "sx", "sB", "sC", "aq", "ak", "av"]
    din = {n: C.dram_in(n, [128, NTOK]) for n in names}
    gates = C.dram_in("gates", [128, 2 * NCH, 8])
    pvd = C.dram_in("pv", [128, 40])
    cstd = C.dram_in("cst", [128, NCST])
    cosd = C.dram_in("cosF", [128, 4096])
    sind = C.dram_in("sinF", [128, 4096])
    outs = {n: C.dram_out(n, [NTOK, 128]) for n in ("oa", "ob", "oc")}

    cst, Tc = C.sb([128, NCST], F32, "cst_sb")
    S.dma("sp", lambda e: e.dma_start(out=cst[:], in_=cstd[:, :]), writes=[Tc])
    pv, Tpv = C.sb([128, 40], F32, "pv_sb")
    S.dma("sp", lambda e: e.dma_start(out=pv[:], in_=pvd[:, :]), writes=[Tpv])
    ident = cst[:, CI:CI + 128]
    ones, To = consts(C)
    epsc, Te = C.sb([128, 1], F32, "epsc")
    S.op("pool", lambda e: e.memset(epsc[:], EPS), writes=[Te])
    onec, T1 = C.sb([128, 1], F32, "onec")
    S.op("pool", lambda e: e.memset(onec[:], 1.0), writes=[T1])

    slabs = [None] + [C.sb([128, TB], F32, f"slab{i}") for i in range(1, 8)]
    slabs[0] = slabs[6]
    gt, Tg = C.sb([128, NCH, 8], F32, "gates_sb")
    banks = [C.ps(name=f"bank{i}") for i in range(8)]
    small = [(banks[bi][0][:, 0:128], banks[bi][1]) for bi in range(6)]
    PS = Ring(small)
    wide = Ring([(banks[6 + i][0], banks[6 + i][1]) for i in range(2)])

    def ringsb(n, shape=(128, 128), dt=F32):
        return Ring([C.sb(list(shape), dt) for _ in range(n)])

    def conv_silu(raw, Traw, dst, Tdst, wcol, bias_ap=None):
        for (s0, s1) in ((0, 256), (256, TB)):
            S.op("act", lambda e, s0=s0, s1=s1: e.activation(out=dst[:, s0:s1], in_=raw[:, s0:s1], func=AF.Identity,
                                                             scale=wcol[:, 1:2], **({} if bias_ap is None else {"bias": bias_ap})),
                 reads=[Traw, Tpv], writes=[Tdst])
            _stt(C, dst[:, s0 + 1:s1], raw[:, s0:s1 - 1], wcol[:, 0:1], dst[:, s0 + 1:s1], MUL, ADD, [Traw, Tpv, Tdst], [Tdst])
            _stt(C, dst[:, s0:s1 - 1], raw[:, s0 + 1:s1], wcol[:, 2:3], dst[:, s0:s1 - 1], MUL, ADD, [Traw, Tpv, Tdst], [Tdst])
        _act(C, dst[:, :], dst[:, :], AF.Silu, [Tdst], [Tdst])

    sqr = ringsb(2, (128, 512))

    def l2norm(dst, Tdst, mul):
        for c0 in range(0, TB, 512):
            n = min(512, TB - c0)
            sq, Tsq = sqr.get()
            _act(C, sq[:, :n], dst[:, c0:c0 + n], AF.Square, [Tdst], [Tsq])
            pw, Tpw = wide.get()
            _mm(C, pw[:, :n], ones[:], sq[:, :n], [Tsq, To], [Tpw])
            _act(C, sq[:, :n], pw[:, :n], AF.Sqrt, [Tpw, Te], [Tsq], bias=epsc[:, 0:1])
            S.op("dve", lambda e, sq=sq, n=n: e.reciprocal(out=sq[:, :n], in_=sq[:, :n]), reads=[Tsq], writes=[Tsq])
            _stt(C, dst[:, c0:c0 + n], dst[:, c0:c0 + n], float(mul), sq[:, :n], MUL, MUL, [Tdst, Tsq], [Tdst])

    def to_tm(src, Tsrc, dst, Tdst, chunks=range(NCH)):
        for ch in chunks:
            pt, Tp = PS.get()
            _tr(C, pt, src[:, ch * 128:(ch + 1) * 128], ident, [Tsrc, Tc], [Tp])
            _cp(C, "act" if ch % 2 else "dve", dst[:, ch * 128:(ch + 1) * 128], pt, [Tp], [Tdst])

    def load(name, b, dst, Tdst):
        S.dma("sp", lambda e: e.dma_start(out=dst[:, :], in_=din[name][:, b * TB:(b + 1) * TB]), writes=[Tdst])

    def store(oname, b, src, Tsrc):
        ov = outs[oname].rearrange("(c p) d -> p c d", p=128)
        sv = src[:, :].rearrange("p (c d) -> p c d", d=128)
        for c0 in range(0, NCH, 6):
            c1 = min(NCH, c0 + 6)
            S.dma("sp", lambda e, c0=c0, c1=c1: e.dma_start(out=ov[:, b * NCH + c0:b * NCH + c1, :], in_=sv[:, c0:c1, :]),
                  reads=[Tsrc])

    tabcache = {}
    sbcache = {}

    def sbc(shape, name):
        if name not in sbcache:
            sbcache[name] = C.sb(shape, F32, name)
        return sbcache[name]

    def tab(name):
        if name not in tabcache:
            tabcache[name] = C.sb([128, NCH], F32, name)
        return tabcache[name]

    def cum_tabs(gsrc_ap, Tsrc, d, names):
        tri = cst[:, CTUP:CTUP + 128] if d == 0 else cst[:, CTLO:CTLO + 128]
        r = {}
        pc, Tpc = PS.get()
        _mm(C, pc[:, :NCH], tri, gsrc_ap, [Tc, Tsrc], [Tpc])
        r['gc'] = tab(names + "gc")
        _cp(C, "dve", r['gc'][0][:, :], pc[:, :NCH], [Tpc], [r['gc'][1]])
        pl, Tpl = PS.get()
        _mm(C, pl[:, :NCH], ones[:], gsrc_ap, [To, Tsrc], [Tpl])
        r['gl'] = tab(names + "gl")
        _cp(C, "dve", r['gl'][0][:, :], pl[:, :NCH], [Tpl], [r['gl'][1]])
        r['ngc'] = tab(names + "ngc")
        _ts(C, "dve", r['ngc'][0][:, :], r['gc'][0][:, :], -1.0, None, MUL, None, [r['gc'][1]], [r['ngc'][1]])
        r['eg'] = tab(names + "eg")
        _act(C, r['eg'][0][:, :], r['gc'][0][:, :], AF.Exp, [r['gc'][1]], [r['eg'][1]])
        r['el'] = tab(names + "el")
        _act(C, r['el'][0][:, :], r['gl'][0][:, :], AF.Exp, [r['gl'][1]], [r['el'][1]])
        r['kd'] = tab(names + "kd")
        _tt(C, "dve", r['kd'][0][:, :], r['gl'][0][:, :], r['gc'][0][:, :], SUB, [r['gl'][1], r['gc'][1]], [r['kd'][1]])
        _act(C, r['kd'][0][:, :], r['kd'][0][:, :], AF.Exp, [r['kd'][1]], [r['kd'][1]])
        return r

    R = {k: ringsb(2) for k in ("diag", "t1", "t2", "vb", "kbg", "vnew")}
    R.update({k: ringsb(3) for k in ("Dm", "DTm", "EG", "Nm", "NmT", "P", "PT", "XT")})
    R.update({k: ringsb(5) for k in ("u", "wT", "attnT", "qdT", "kdec")})

    def grow_mats(gc_tab, ch, d, want_D, want_EG=True):
        gc, Tgc = gc_tab['gc']
        ngc, Tngc = gc_tab['ngc']
        LO = cst[:, CLO:CLO + 128]
        UP = cst[:, CUP:CUP + 128]
        negm, negmT = (LO, UP) if d == 0 else (UP, LO)
        dg, Tdg = R["diag"].get()
        _ts(C, "pool", dg[:, :], ident, gc[:, ch:ch + 1], None, MUL, None, [Tc, Tgc], [Tdg])
        pg, Tpg = PS.get()
        _mm(C, pg, ones[:], dg[:, :], [To, Tdg], [Tpg])
        out = {}
        t2, Tt2 = R["t2"].get()
        _tt(C, "dve", t2[:, :], pg, negmT, ADD, [Tpg, Tc], [Tt2])
        DTm, TDT = R["DTm"].get()
        _act(C, DTm[:, :], t2[:, :], AF.Exp, [Tt2, Tngc], [TDT], bias=ngc[:, ch:ch + 1])
        out['DTm'] = (DTm, TDT)
        if want_D:
            t1, Tt1 = R["t1"].get()
            _stt(C, t1[:, :], pg, -1.0, negm, MUL, ADD, [Tpg, Tc], [Tt1])
            Dm, TD = R["Dm"].get()
            _act(C, Dm[:, :], t1[:, :], AF.Exp, [Tt1, Tgc], [TD], bias=gc[:, ch:ch + 1])
            out['Dm'] = (Dm, TD)
        if want_EG:
            EG, TEG = R["EG"].get()
            _act(C, EG[:, :], pg, AF.Exp, [Tpg], [TEG])
            out['EG'] = (EG, TEG)
        return out

    def gdn(b):
        raw, Traw = slabs[0]
        qf, Tq = slabs[1]
        kf, Tk = slabs[2]
        vf, Tv = slabs[3]
        ktm, Tktm = slabs[4]
        vtm, Tvtm = slabs[5]
        obs = [slabs[6], slabs[7]]
        S.dma("sp", lambda e: e.dma_start(out=gt[:], in_=gates[:, b * NCH:(b + 1) * NCH, :]), writes=[Tg])
        for i, (nm, (dst, Td)) in enumerate((("gq", slabs[1]), ("gk", slabs[2]), ("gv", slabs[3]))):
            load(nm, b, raw, Traw)
            conv_silu(raw, Traw, dst, Td, pv[:, 3 * i:3 * i + 3])
        l2norm(qf, Tq, 128 ** -0.5)
        l2norm(kf, Tk, 1.0)
        to_tm(kf, Tk, ktm, Tktm)
        to_tm(vf, Tv, vtm, Tvtm)
        nA, TnA = sbc([128, 2], "gdn_nA")
        _act(C, nA[:, :], pv[:, 18:20], AF.Exp, [Tpv], [TnA])
        _ts(C, "dve", nA[:, :], nA[:, :], -1.0, None, MUL, None, [TnA], [TnA])
        tabs = []
        betas = []
        for d in range(2):
            g, Tgd = tab(f"gdn_g{d}")
            _act(C, g[:, :], gt[:, :, d], AF.Exp, [Tg, Tpv], [Tgd], bias=pv[:, 20 + d:21 + d])
            _act(C, g[:, :], g[:, :], AF.Ln, [Tgd, T1], [Tgd], bias=onec[:, 0:1])
            _ts(C, "dve", g[:, :], g[:, :], nA[:, d:d + 1], None, MUL, None, [Tgd, TnA], [Tgd])
            tb = cum_tabs(g[:, :], Tgd, d, f"gdn{d}")
            be, Tbe = tab(f"gdn_beta{d}")
            _act(C, be[:, :], gt[:, :, 2 + d], AF.Sigmoid, [Tg], [Tbe])
            nb_, Tnb = tab(f"gdn_nbeta{d}")
            _ts(C, "dve", nb_[:, :], be[:, :], -1.0, None, MUL, None, [Tbe], [Tnb])
            bg, Tbg = tab(f"gdn_bg{d}")
            _tt(C, "dve", bg[:, :], be[:, :], tb['eg'][0][:, :], MUL, [Tbe, tb['eg'][1]], [Tbg])
            tb['beta'] = (be, Tbe)
            tb['nbeta'] = (nb_, Tnb)
            tb['bg'] = (bg, Tbg)
            tabs.append(tb)
        Sst = [sbc([128, 128], f"gdnS{d}") for d in range(2)]
        for d in range(2):
            S.op("pool", lambda e, d=d: e.memset(Sst[d][0][:], 0.0), writes=[Sst[d][1]])

        def pre(d, ch):
            tb = tabs[d]
            cs = slice(ch * 128, (ch + 1) * 128)
            gm = grow_mats(tb, ch, d, want_D=True)
            Dm, TD = gm['Dm']
            DTm, TDT = gm['DTm']
            EG, TEG = gm['EG']
            strict = cst[:, CSLO:CSLO + 128] if d == 0 else cst[:, CSUP:CSUP + 128]
            _tt(C, "pool", Dm[:, :], Dm[:, :], strict, MUL, [TD, Tc], [TD])
            pk, Tpk = PS.get()
            _mm(C, pk, kf[:, cs], kf[:, cs], [Tk], [Tpk])
            Nm, TN = R["Nm"].get()
            _stt(C, Nm[:, :], pk, tb['nbeta'][0][:, ch:ch + 1], Dm[:, :], MUL, MUL, [Tpk, tb['nbeta'][1], TD], [TN])
            pt, Tpt = PS.get()
            _tr(C, pt, Nm[:, :], ident, [TN, Tc], [Tpt])
            NmT, TNT = R["NmT"].get()
            _cp(C, "act", NmT[:, :], pt, [Tpt], [TNT])
            XT, TX = R["XT"].get()
            _tt(C, "dve", XT[:, :], pt, ident, ADD, [Tpt, Tc], [TX])
            P, TP, PT, TPT = Nm, TN, NmT, TNT
            for s in range(1, 7):
                pp, Tpp = PS.get()
                _mm(C, pp, PT[:, :], P[:, :], [TPT, TP], [Tpp])
                Pn, TPn = R["P"].get()
                _cp(C, "act", Pn[:, :], pp, [Tpp], [TPn])
                if s < 6:
                    pq, Tpq = PS.get()
                    _mm(C, pq, P[:, :], PT[:, :], [TP, TPT], [Tpq])
                    PTn, TPTn = R["PT"].get()
                    _cp(C, "dve", PTn[:, :], pq, [Tpq], [TPTn])
                px, Tpx = PS.get()
                _mm(C, px, Pn[:, :], XT[:, :], [TPn, TX], [Tpx])
                _tt(C, "dve", XT[:, :], px, XT[:, :], ADD, [Tpx, TX], [TX])
                P, TP = Pn, TPn
                if s < 6:
                    PT, TPT = PTn, TPTn
            vb, Tvb = R["vb"].get()
            _ts(C, "pool", vb[:, :], vtm[:, cs], tb['beta'][0][:, ch:ch + 1], None, MUL, None, [Tvtm, tb['beta'][1]], [Tvb])
            kbg, Tkbg = R["kbg"].get()
            _ts(C, "pool", kbg[:, :], ktm[:, cs], tb['bg'][0][:, ch:ch + 1], None, MUL, None, [Tktm, tb['bg'][1]], [Tkbg])
            pu, Tpu = PS.get()
            _mm(C, pu, XT[:, :], vb[:, :], [TX, Tvb], [Tpu])
            u, Tu = R["u"].get()
            _cp(C, "act", u[:, :], pu, [Tpu], [Tu])
            pw, Tpw = PS.get()
            _mm(C, pw, kbg[:, :], XT[:, :], [Tkbg, TX], [Tpw])
            wT, TwT = R["wT"].get()
            _cp(C, "dve", wT[:, :], pw, [Tpw], [TwT])
            pa, Tpa = PS.get()
            _mm(C, pa, kf[:, cs], qf[:, cs], [Tk, Tq], [Tpa])
            attnT, TaT = R["attnT"].get()
            _tt(C, "dve", attnT[:, :], pa, DTm[:, :], MUL, [Tpa, TDT], [TaT])
            qdT, TqdT = R["qdT"].get()
            _tt(C, "pool", qdT[:, :], qf[:, cs], EG[:, :], MUL, [Tq, TEG], [TqdT])
            kdec, Tkd = R["kdec"].get()
            _ts(C, "pool", kdec[:, :], ktm[:, cs], tb['kd'][0][:, ch:ch + 1], None, MUL, None, [Tktm, tb['kd'][1]], [Tkd])
            return dict(u=(u, Tu), wT=(wT, TwT), attnT=(attnT, TaT), qdT=(qdT, TqdT), kdec=(kdec, Tkd))

        def seq(d, ch, m):
            tb = tabs[d]
            St, TS = Sst[d]
            ob, Tob = obs[d]
            cs = slice(ch * 128, (ch + 1) * 128)
            p1, Tp1 = PS.get()
            _mm(C, p1, m['wT'][0][:, :], St[:, :], [m['wT'][1], TS], [Tp1])
            vn, Tvn = R["vnew"].get()
            _tt(C, "dve", vn[:, :], m['u'][0][:, :], p1, SUB, [m['u'][1], Tp1], [Tvn])
            p2, Tp2 = PS.get()
            _mm(C, p2, m['qdT'][0][:, :], St[:, :], [m['qdT'][1], TS], [Tp2], start=True, stop=False)
            _mm(C, p2, m['attnT'][0][:, :], vn[:, :], [m['attnT'][1], Tvn], [Tp2], start=False, stop=True)
            _cp(C, "act", ob[:, cs], p2, [Tp2], [Tob])
            p3, Tp3 = PS.get()
            _mm(C, p3, m['kdec'][0][:, :], vn[:, :], [m['kdec'][1], Tvn], [Tp3])
            _stt(C, St[:, :], St[:, :], tb['el'][0][:, ch:ch + 1], p3, MUL, ADD, [TS, tb['el'][1], Tp3], [TS])

        order = [[0, 1] + list(range(2, NCH)), [1, 0] + list(range(NCH - 1, 1, -1))]
        pend = [None, None]
        NST = NCH
        S.counting = True
        for step in range(NST + 1):
            cur = [None, None]
            if step < NST:
                for d in range(2):
                    cur[d] = (order[d][step], pre(d, order[d][step]))
            if step > 0:
                for d in range(2):
                    seq(d, pend[d][0], pend[d][1])
            pend = cur
        S.counting = False
        _tt(C, "pool", obs[0][0][:, :], obs[0][0][:, :], obs[1][0][:, :], ADD, [obs[0][1], obs[1][1]], [obs[0][1]])
        store("oa", b, obs[0][0], obs[0][1])

    R.update({k: ringsb(3) for k in ("CBs", "MT", "CdT")})
    R.update({k: ringsb(4, (128, 64)) for k in ("xdt", "xw")})

    def ssd(b):
        raw, Traw = slabs[0]
        xf, Txf = slabs[1]
        Bf, TBf = slabs[2]
        Cf, TCf = slabs[3]
        xtm, Txtm = slabs[4]
        Btm, TBtm = slabs[5]
        ybs = [slabs[6], slabs[7]]
        S.dma("sp", lambda e: e.dma_start(out=gt[:], in_=gates[:, b * NCH:(b + 1) * NCH, :]), writes=[Tg])
        for i, (nm, (dst, Td)) in enumerate((("sx", slabs[1]), ("sB", slabs[2]), ("sC", slabs[3]))):
            load(nm, b, raw, Traw)
            conv_silu(raw, Traw, dst, Td, pv[:, 9 + 3 * i:12 + 3 * i], bias_ap=pv[:, 22 + i:23 + i])
        to_tm(xf, Txf, xtm, Txtm)
        to_tm(Bf, TBf, Btm, TBtm)
        nA, TnA = sbc([128, 4], "ssd_nA")
        _act(C, nA[:, :], pv[:, 25:29], AF.Exp, [Tpv], [TnA])
        _ts(C, "dve", nA[:, :], nA[:, :], -1.0, None, MUL, None, [TnA], [TnA])
        tabs = {}
        for d in range(2):
            for hh in range(2):
                k = 2 * d + hh
                dt_, Tdt = tab(f"ssd_dt{k}")
                _act(C, dt_[:, :], gt[:, :, 4 + k], AF.Exp, [Tg, Tpv], [Tdt], bias=pv[:, 29 + k:30 + k])
                _act(C, dt_[:, :], dt_[:, :], AF.Ln, [Tdt, T1], [Tdt], bias=onec[:, 0:1])
                a_, Ta = tab(f"ssd_a{k}")
                _ts(C, "dve", a_[:, :], dt_[:, :], nA[:, k:k + 1], None, MUL, None, [Tdt, TnA], [Ta])
                tb = cum_tabs(a_[:, :], Ta, d, f"ssd{k}")
                tb['dt'] = (dt_, Tdt)
                tabs[(d, hh)] = tb
        hst = [sbc([128, 128], f"ssdH{d}") for d in range(2)]
        for d in range(2):
            S.op("pool", lambda e, d=d: e.memset(hst[d][0][:], 0.0), writes=[hst[d][1]])

        def step(d, ch):
            cs = slice(ch * 128, (ch + 1) * 128)
            H, TH = hst[d]
            yb, Tyb = ybs[d]
            pcb, Tpcb = PS.get()
            _mm(C, pcb, Bf[:, cs], Cf[:, cs], [TBf, TCf], [Tpcb])
            CBs, TCB = R["CBs"].get()
            _cp(C, "act", CBs[:, :], pcb, [Tpcb], [TCB])
            hd = []
            for hh in range(2):
                tb = tabs[(d, hh)]
                gm = grow_mats(tb, ch, d, want_D=False)
                DTm, TDT = gm['DTm']
                EG, TEG = gm['EG']
                MT, TMT = R["MT"].get()
                _tt(C, "pool", MT[:, :], CBs[:, :], DTm[:, :], MUL, [TCB, TDT], [TMT])
                CdT, TCd = R["CdT"].get()
                _tt(C, "pool", CdT[:, :], Cf[:, cs], EG[:, :], MUL, [TCf, TEG], [TCd])
                xdt, Txd = R["xdt"].get()
                _ts(C, "dve", xdt[:, :], xtm[:, ch * 128 + hh * 64:ch * 128 + hh * 64 + 64], tb['dt'][0][:, ch:ch + 1], None, MUL, None,
                    [Txtm, tb['dt'][1]], [Txd])
                xw, Txw = R["xw"].get()
                _ts(C, "pool", xw[:, :], xdt[:, :], tb['kd'][0][:, ch:ch + 1], None, MUL, None, [Txd, tb['kd'][1]], [Txw])
                hd.append((MT, TMT, CdT, TCd, xdt, Txd, xw, Txw, tb))
            py, Tpy = PS.get()
            for hh in range(2):
                MT, TMT, CdT, TCd, xdt, Txd, xw, Txw, tb = hd[hh]
                hc = slice(hh * 64, hh * 64 + 64)
                _mm(C, py[:, hc], MT[:, :], xdt[:, :], [TMT, Txd], [Tpy], start=True, stop=False)
                _mm(C, py[:, hc], CdT[:, :], H[:, hc], [TCd, TH], [Tpy], start=False, stop=True)
            _cp(C, "act", yb[:, cs], py, [Tpy], [Tyb])
            ph, Tph = PS.get()
            for hh in range(2):
                xw, Txw = hd[hh][6], hd[hh][7]
                hc = slice(hh * 64, hh * 64 + 64)
                _mm(C, ph[:, hc], Btm[:, cs], xw[:, :], [TBtm, Txw], [Tph])
            for hh in range(2):
                tb = hd[hh][8]
                hc = slice(hh * 64, hh * 64 + 64)
                _stt(C, H[:, hc], H[:, hc], tb['el'][0][:, ch:ch + 1], ph[:, hc], MUL, ADD, [TH, tb['el'][1], Tph], [TH])

        order = [[0, 1] + list(range(2, NCH)), [1, 0] + list(range(NCH - 1, 1, -1))]
        for st_ in range(NCH):
            for d in range(2):
                step(d, order[d][st_])
        yv = ybs[0][0][:, :].rearrange("p (c h q) -> p c h q", h=2, q=64)
        xv = xtm[:, :].rearrange("p (c h q) -> p c h q", h=2, q=64)
        for hh in range(2):
            _stt(C, yv[:, :, hh, :], xv[:, :, hh, :], pv[:, 33 + hh:34 + hh], yv[:, :, hh, :], MUL, ADD, [Txtm, Tpv, ybs[0][1]], [ybs[0][1]])
        _tt(C, "pool", ybs[0][0][:, :], ybs[0][0][:, :], ybs[1][0][:, :], ADD, [ybs[0][1], ybs[1][1]], [ybs[0][1]])
        store("ob", b, ybs[0][0], ybs[0][1])

    amask, Tam = C.sb([128, 384], F32, "amask")
    _cp(C, "pool", amask[:, 0:128], cst[:, CUP:CUP + 128], [Tc], [Tam])
    S.op("pool", lambda e: e.memset(amask[:, 128:256], 0.0), writes=[Tam])
    _cp(C, "pool", amask[:, 256:384], cst[:, CLO:CLO + 128], [Tc], [Tam])
    nsk, Tnsk = C.sb([128, 1], F32, "nsink")
    _ts(C, "dve", nsk[:, :], pv[:, 35:36], -1.0, None, MUL, None, [Tpv], [Tnsk])
    csr = ringsb(2, (128, 512))
    snr = ringsb(2, (128, 512))
    rtmp = sqr
    scr_ = ringsb(2, (128, 640))
    pTr = ringsb(6)
    sm = Ring([tuple(C.sb([128, 1], F32) for _ in range(5)) for _ in range(3)])
    SCALE = 128 ** -0.5

    def attn(b):
        qf, Tq = slabs[1]
        kf, Tk = slabs[2]
        vf, Tv = slabs[3]
        vtm, Tvtm = slabs[4]
        ob, Tob = slabs[5]
        load("aq", b, qf, Tq)
        load("ak", b, kf, Tk)
        load("av", b, vf, Tv)
        to_tm(vf, Tv, vtm, Tvtm)
        for p0 in range(0, 4096, 512):
            cs_, Tcs = csr.get()
            sn_, Tsn = snr.get()
            S.dma("sp", lambda e, cs_=cs_, p0=p0: e.dma_start(out=cs_[:, :], in_=cosd[:, p0:p0 + 512]), writes=[Tcs])
            S.dma("sp", lambda e, sn_=sn_, p0=p0: e.dma_start(out=sn_[:, :], in_=sind[:, p0:p0 + 512]), writes=[Tsn])
            for (x, Tx) in ((qf, Tq), (kf, Tk)):
                xs = x[:, 256 + p0:256 + p0 + 512]
                pr, Tpr = wide.get()
                _mm(C, pr[:, :], cst[:, CRM:CRM + 128], xs, [Tc, Tx], [Tpr])
                tm_, Ttm = rtmp.get()
                _tt(C, "dve", tm_[:, :], pr[:, :], sn_[:, :], MUL, [Tpr, Tsn], [Ttm])
                _tt(C, "pool", xs, xs, cs_[:, :], MUL, [Tx, Tcs], [Tx])
                _tt(C, "dve", xs, xs, tm_[:, :], ADD, [Tx, Ttm], [Tx])
        for qc in range(NCH):
            qcs = slice(qc * 128, (qc + 1) * 128)
            sc, Tsc = scr_.get()
            mx, nm, rs, es, rd = [t for t in sm.get()]
            if qc < 2:
                W = 0
                kchunks = [0, 1]
            else:
                n = qc - 2
                lo, hi = max(n - 1, 0), min(n + 1, 31)
                W = (hi - lo + 1) * 128
                m0 = 0 if lo == n - 1 else 128
                pa, Tpa = wide.get()
                _mm(C, pa[:, :W], qf[:, qcs], kf[:, 256 + lo * 128:256 + lo * 128 + W], [Tq, Tk], [Tpa])
                _tt(C, "dve", sc[:, :W], pa[:, :W], amask[:, m0:m0 + W], ADD, [Tpa, Tam], [Tsc])
                kchunks = [2 + lo + i for i in range(hi - lo + 1)] + [0, 1]
            pb_, Tpb = wide.get()
            _mm(C, pb_[:, :256], qf[:, qcs], kf[:, 0:256], [Tq, Tk], [Tpb])
            _cp(C, "act", sc[:, W:W + 256], pb_[:, :256], [Tpb], [Tsc])
            WT = W + 256
            S.op("dve", lambda e, sc=sc, mx=mx, WT=WT: e.tensor_reduce(out=mx[0][:, :], in_=sc[:, :WT], axis=AX.X, op=MAX), reads=[Tsc], writes=[mx[1]])
            _ts(C, "dve", nm[0][:, :], mx[0][:, :], -SCALE, nsk[:, 0:1], MUL, MIN, [mx[1], Tnsk], [nm[1]])
            _act(C, sc[:, :WT], sc[:, :WT], AF.Exp, [Tsc, nm[1]], [Tsc, rs[1]], scale=SCALE, bias=nm[0][:, 0:1], accum=rs[0][:, 0:1])
            _act(C, es[0][:, :], pv[:, 35:36], AF.Exp, [Tpv, nm[1]], [es[1]], bias=nm[0][:, 0:1])
            _tt(C, "dve", rd[0][:, :], rs[0][:, :], es[0][:, :], ADD, [rs[1], es[1]], [rd[1]])
            S.op("dve", lambda e, rd=rd: e.reciprocal(out=rd[0][:, :], in_=rd[0][:, :]), reads=[rd[1]], writes=[rd[1]])
            po, Tpo = PS.get()
            nk = len(kchunks)
            pts = []
            for i in range(nk):
                pt, Tpt = PS.get()
                _tr(C, pt, sc[:, i * 128:(i + 1) * 128], ident, [Tsc, Tc], [Tpt])
                pT, TpT = pTr.get()
                _cp(C, "act" if i % 2 else "dve", pT[:, :], pt, [Tpt], [TpT])
                pts.append((pT, TpT))
            for i, kc in enumerate(kchunks):
                pT, TpT = pts[i]
                _mm(C, po, pT[:, :], vtm[:, kc * 128:(kc + 1) * 128], [TpT, Tvtm], [Tpo], start=(i == 0), stop=(i == nk - 1))
            _act(C, ob[:, qcs], po, AF.Identity, [Tpo, rd[1]], [Tob], scale=rd[0][:, 0:1])
        store("oc", b, ob, Tob)

    extra_fns = dict(ssd=ssd, attn=attn)
    fns = dict(gdn=gdn)
    fns.update(extra_fns)
    for nm in which:
        for b in batches:
            fns[nm](b)
    return C.finish([t for _, t in slabs[1:]])


NT_C = 1092
NGATE = 8192
DFF = 5632
FC = 44
P1_TILES = [(0, 342, 0), (342, 342, 0), (684, 342, 0), (1026, 66, 1)]
P2_TILES = [(1, 343, 0, 0), (343, 685, 0, 342), (685, 1025, 0, 684), (1027, 1091, 1, 1024)]
NW = 344


def build_C():
    C = Ctx()
    nc, S = C.nc, C.S
    xT = C.dram_in("xT", [D, NT_C])
    oT = C.dram_in("oT", [3072, NT_C])
    hmd = C.dram_in("hmask", [128, NT_C])
    vecs = C.dram_in("vecs", [128, 16, 16])
    pvd = C.dram_in("pvc", [128, 185])
    wg = C.dram_in("wg", [D, NGATE])
    wbr = C.dram_in("wbr", [3072, D])
    wout = C.dram_in("wout", [D, D])
    wup = C.dram_in("wup", [D, 2 * DFF])
    wdn = C.dram_in("wdn", [DFF, D])
    xo = C.dram_out("xoT", [D, 1088])
    ones, To = consts(C)
    scr = make_scr(C)
    vt, Tv = C.sb([128, 16, 16], F32, "vecs_sb")
    S.dma("sp", lambda e: e.dma_start(out=vt[:], in_=vecs[:, :, :]), writes=[Tv])
    pv, Tpv = C.sb([128, 185], F32, "pvc_sb")
    S.dma("sp", lambda e: e.dma_start(out=pv[:], in_=pvd[:, :]), writes=[Tpv])
    hm, Thm = C.sb([128, NT_C], F32, "hm_sb")
    S.dma("sp", lambda e: e.dma_start(out=hm[:], in_=hmd[:, :]), writes=[Thm])
    ef, Tef = C.sb([128, 2, 4, 16], F32, "eff")
    for s in range(2):
        b0 = 4 + 6 * s
        S.op("dve", lambda e, s=s, b0=b0: e.scalar_tensor_tensor(out=ef[:, s, 0, :], in0=vt[:, b0 + 1, :], scalar=1.0, in1=vt[:, 0, :], op0=ADD, op1=MUL),
             reads=[Tv], writes=[Tef])
        _tt(C, "dve", ef[:, s, 1, :], vt[:, b0 + 2, :], vt[:, 1, :], MUL, [Tv], [Tef])
        S.op("dve", lambda e, s=s, b0=b0: e.scalar_tensor_tensor(out=ef[:, s, 2, :], in0=vt[:, b0 + 4, :], scalar=1.0, in1=vt[:, 2, :], op0=ADD, op1=MUL),
             reads=[Tv], writes=[Tef])
        _tt(C, "dve", ef[:, s, 3, :], vt[:, b0 + 5, :], vt[:, 3, :], MUL, [Tv], [Tef])
    xs, Txs = C.sb([128, KC, NT_C], F32, "xres")
    xv = xT.rearrange("(c p) t -> p c t", p=128)
    for c0 in range(0, KC, 4):
        S.dma("sp", lambda e, c0=c0: e.dma_start(out=xs[:, c0:c0 + 4, :], in_=xv[:, c0:c0 + 4, :]), writes=[Txs])
    A16, TA16 = C.sb([128, KC, NW], BF16, "A16")
    ACT_, TACT = C.sb([128, FC, NW], BF16, "ACTb")
    F32A, TF32 = C.sb([128, KC, NW], F32, "F32A")
    M16, TM16 = C.sb([128, KC, NW], BF16, "M16")
    wsl = Ring([C.sb([128, KC, 512], BF16) for _ in range(2)])
    pss, Tps = C.ps(name="pss")
    PB = Ring([C.ps() for _ in range(6)])
    st_in = Ring([C.sb([128, NW], F32) for _ in range(3)])
    st4 = [C.sb([128, NW], F32) for _ in range(4)]
    tmpr = Ring([C.sb([128, NW], F32) for _ in range(3)])
    rsm, Trsm = C.sb([128, 512], F32, "rsm")
    ov = oT.rearrange("(c p) t -> p c t", p=128)

    def wload(dview, c0, cw, kcn):
        w, Tw = wsl.get()
        S.dma("pool", lambda e: e.dma_start(out=w[:, :kcn, :cw], in_=dview[:, :, c0:c0 + cw]), writes=[Tw])
        return w, Tw

    wgv = wg.rearrange("(c p) n -> p c n", p=128)
    wbv = [wbr[br * 1024:(br + 1) * 1024, :].rearrange("(c p) n -> p c n", p=128) for br in range(3)]
    wov = wout.rearrange("(c p) n -> p c n", p=128)
    wuv = wup.rearrange("(c p) n -> p c n", p=128)
    wdv = [wdn[fg * 11 * 128:(fg + 1) * 11 * 128, :].rearrange("(c p) n -> p c n", p=128) for fg in range(4)]

    def stats(src_fn, Tsrc, nchunks, n, div):
        for c in range(nchunks):
            sq, Tsq = scr['sq'][c % 2]
            _act(C, sq[:, :n], src_fn(c), AF.Square, [Tsrc], [Tsq])
            _mm(C, pss[:, :n], ones[:], sq[:, :n], [Tsq, To], [Tps], start=(c == 0), stop=(c == nchunks - 1))
        _act(C, rsm[:, :n], pss[:, :n], AF.Sqrt, [Tps, scr['eps'][1]], [Trsm], scale=1.0 / div, bias=scr['eps'][0][:, 0:1])
        S.op("dve", lambda e: e.reciprocal(out=rsm[:, :n], in_=rsm[:, :n]), reads=[Trsm], writes=[Trsm])

    def gemm_chunk(w, Tw, wc, kcn, rhs_fn, Trhs, n):
        pb, Tp = PB.get()
        for k in range(kcn):
            _mm(C, pb[:, :n], w[:, k, wc:wc + 128], rhs_fn(k), [Tw, Trhs], [Tp], start=(k == 0), stop=(k == kcn - 1))
        return pb, Tp

    for (t0, n, vs) in P1_TILES:
        norm_mod(C, xs[:, :, t0:t0 + n], Txs, n, ones, To, ef[:, vs, 0, :], vt[:, 4 + 6 * vs, :], Tef, A16, TA16, pss, Tps, scr)
        hfn = lambda k: A16[:, k, :n]
        for sgrp in range(2):
            w, Tw = wload(wgv, sgrp * 512, 512, KC)
            for cc in range(4):
                c = sgrp * 4 + cc
                oin, Toin = st_in.get()
                S.dma("sp", lambda e, oin=oin, c=c, t0=t0, n=n: e.dma_start(out=oin[:, :n], in_=ov[:, c, t0:t0 + n]), writes=[Toin])
                stats(lambda _c, oin=oin: oin[:, :n], Toin, 1, n, 128.0)
                pb, Tp = gemm_chunk(w, Tw, cc * 128, KC, hfn, TA16, n)
                sg, Tsg = tmpr.get()
                _act(C, sg[:, :n], pb[:, :n], AF.Silu, [Tp], [Tsg])
                _tt(C, "dve", oin[:, :n], oin[:, :n], rsm[:, :n], MUL, [Toin, Trsm], [Toin])
                _stt(C, ACT_[:, c, :n], oin[:, :n], pv[:, 0:1], sg[:, :n], MUL, MUL, [Toin, Tpv, Tsg], [TACT])
        for sgrp in range(2):
            w, Tw = wload(wgv, 1024 + sgrp * 512, 512, KC)
            for cc in range(4):
                c = sgrp * 4 + cc
                oin, Toin = st4[cc]
                S.dma("sp", lambda e, oin=oin, c=c, t0=t0, n=n: e.dma_start(out=oin[:, :n], in_=ov[:, 8 + c, t0:t0 + n]), writes=[Toin])
                pb, Tp = gemm_chunk(w, Tw, cc * 128, KC, hfn, TA16, n)
                sg, Tsg = tmpr.get()
                _act(C, sg[:, :n], pb[:, :n], AF.Silu, [Tp], [Tsg])
                _tt(C, "dve", oin[:, :n], oin[:, :n], sg[:, :n], MUL, [Toin, Tsg], [Toin])
            for c_ in range(4):
                sq, Tsq = scr['sq'][c_ % 2]
                _act(C, sq[:, :n], st4[c_][0][:, :n], AF.Square, [st4[c_][1]], [Tsq])
                _mm(C, pss[:, :n], ones[:], sq[:, :n], [Tsq, To], [Tps], start=(c_ == 0), stop=(c_ == 3))
            _act(C, rsm[:, :n], pss[:, :n], AF.Sqrt, [Tps, scr['eps'][1]], [Trsm], scale=1.0 / 512.0, bias=scr['eps'][0][:, 0:1])
            S.op("dve", lambda e, n=n: e.reciprocal(out=rsm[:, :n], in_=rsm[:, :n]), reads=[Trsm], writes=[Trsm])
            for cc in range(4):
                c = sgrp * 4 + cc
                oin, Toin = st4[cc]
                _tt(C, "dve", oin[:, :n], oin[:, :n], rsm[:, :n], MUL, [Toin, Trsm], [Toin])
                _ts(C, "dve", ACT_[:, 8 + c, :n], oin[:, :n], pv[:, 1 + c:2 + c], None, MUL, None, [Toin, Tpv], [TACT])
        for c in range(8):
            oin, Toin = st_in.get()
            S.dma("sp", lambda e, oin=oin, c=c, t0=t0, n=n: e.dma_start(out=oin[:, :n], in_=ov[:, 16 + c, t0:t0 + n]), writes=[Toin])
            _cp(C, "act", ACT_[:, 16 + c, :n], oin[:, :n], [Toin], [TACT])
        for og in range(4):
            for br in range(3):
                wb_, Twb = wload(wbv[br], og * 512, 512, 8)
                wm_, Twm = wload(wgv, 2048 + br * 2048 + og * 512, 512, KC)
                for oc in range(4):
                    o = og * 4 + oc
                    p1, Tp1 = gemm_chunk(wb_, Twb, oc * 128, 8, lambda k, br=br: ACT_[:, br * 8 + k, :n], TACT, n)
                    p2, Tp2 = gemm_chunk(wm_, Twm, oc * 128, KC, hfn, TA16, n)
                    gt_, Tgt = tmpr.get()
                    _act(C, gt_[:, :n], p2[:, :n], AF.Sigmoid, [Tp2], [Tgt])
                    if br == 0:
                        _tt(C, "dve", F32A[:, o, :n], p1[:, :n], gt_[:, :n], MUL, [Tp1, Tgt], [TF32])
                    else:
                        _tt(C, "dve", gt_[:, :n], p1[:, :n], gt_[:, :n], MUL, [Tp1, Tgt], [Tgt])
                        _tt(C, "dve", F32A[:, o, :n], F32A[:, o, :n], gt_[:, :n], ADD, [TF32, Tgt], [TF32])
        for o in range(KC):
            _cp(C, "act", M16[:, o, :n], F32A[:, o, :n], [TF32], [TM16])
        for og in range(4):
            w, Tw = wload(wov, og * 512, 512, KC)
            for oc in range(4):
                o = og * 4 + oc
                pb, Tp = gemm_chunk(w, Tw, oc * 128, KC, lambda k: M16[:, k, :n], TM16, n)
                _cp(C, "act" if o % 2 else "dve", F32A[:, o, :n], pb[:, :n], [Tp], [TF32])
        stats(lambda c: F32A[:, c, :n], TF32, KC, n, float(D))
        for c in range(KC):
            tm_, Ttm = tmpr.get()
            _tt(C, "dve", tm_[:, :n], F32A[:, c, :n], rsm[:, :n], MUL, [TF32, Trsm], [Ttm])
            _stt(C, xs[:, c, t0:t0 + n], tm_[:, :n], ef[:, vs, 1, c:c + 1], xs[:, c, t0:t0 + n], MUL, ADD, [Ttm, Tef, Txs], [Txs])

    cw_ = pv[:, 9:9 + 132]
    for (a, b, vs, oc0) in P2_TILES:
        n2 = b - a + 2
        ni = b - a
        norm_mod(C, xs[:, :, a - 1:b + 1], Txs, n2, ones, To, ef[:, vs, 2, :], vt[:, 4 + 6 * vs + 3, :], Tef, A16, TA16, pss, Tps, scr)
        hfn = lambda k: A16[:, k, :n2]
        for fg in range(11):
            wu, Twu = wload(wuv, fg * 512, 512, KC)
            wgt, Twgt = wload(wuv, DFF + fg * 512, 512, KC)
            for fc in range(4):
                f = fg * 4 + fc
                pu, Tpu = gemm_chunk(wu, Twu, fc * 128, KC, hfn, TA16, n2)
                pg, Tpg = gemm_chunk(wgt, Twgt, fc * 128, KC, hfn, TA16, n2)
                gm, Tgm = tmpr.get()
                _tt(C, "dve", gm[:, :n2], pg[:, :n2], hm[:, a - 1:b + 1], MUL, [Tpg, Thm], [Tgm])
                gc, Tgc = tmpr.get()
                _act(C, gc[:, :ni], gm[:, 1:1 + ni], AF.Identity, [Tgm, Tpv], [Tgc], scale=pv[:, 9 + 3 * f + 1:9 + 3 * f + 2], bias=pv[:, 141 + f:142 + f])
                _stt(C, gc[:, :ni], gm[:, 0:ni], pv[:, 9 + 3 * f:9 + 3 * f + 1], gc[:, :ni], MUL, ADD, [Tgm, Tpv, Tgc], [Tgc])
                _stt(C, gc[:, :ni], gm[:, 2:2 + ni], pv[:, 9 + 3 * f + 2:9 + 3 * f + 3], gc[:, :ni], MUL, ADD, [Tgm, Tpv, Tgc], [Tgc])
                _act(C, gc[:, :ni], gc[:, :ni], AF.Silu, [Tgc], [Tgc])
                _tt(C, "dve", ACT_[:, f, :ni], pu[:, 1:1 + ni], gc[:, :ni], MUL, [Tpu, Tgc], [TACT])
        for og in range(4):
            pbs = [PB.get() for _ in range(4)]
            for fg in range(4):
                w, Tw = wload(wdv[fg], og * 512, 512, 11)
                for oc in range(4):
                    pb, Tp = pbs[oc]
                    for k in range(11):
                        _mm(C, pb[:, :ni], w[:, k, oc * 128:(oc + 1) * 128], ACT_[:, fg * 11 + k, :ni], [Tw, TACT], [Tp],
                            start=(fg == 0 and k == 0), stop=(fg == 3 and k == 10))
            for oc in range(4):
                o = og * 4 + oc
                _cp(C, "act" if o % 2 else "dve", F32A[:, o, :ni], pbs[oc][0][:, :ni], [pbs[oc][1]], [TF32])
        stats(lambda c: F32A[:, c, :ni], TF32, KC, ni, float(D))
        for c in range(KC):
            _tt(C, "dve", F32A[:, c, :ni], F32A[:, c, :ni], rsm[:, :ni], MUL, [TF32, Trsm], [TF32])
            _stt(C, F32A[:, c, :ni], F32A[:, c, :ni], ef[:, vs, 3, c:c + 1], xs[:, c, a:b], MUL, ADD, [TF32, Tef, Txs], [TF32])
        xov = xo.rearrange("(c p) t -> p c t", p=128)
        for c0 in range(0, KC, 4):
            S.dma("sp", lambda e, c0=c0, ni=ni, oc0=oc0: e.dma_start(out=xov[:, c0:c0 + 4, oc0:oc0 + ni], in_=F32A[:, c0:c0 + 4, :ni]), reads=[TF32])
    return C.finish([TF32])


MCOLS = 1536


def build_M():
    C = Ctx()
    nc, S = C.nc, C.S
    wada = C.dram_in("wada", [4 * D, MCOLS])
    bada = C.dram_in("bada", [128, 48])
    cT = C.dram_in("cT", [128, 16, 3])
    out = C.dram_out("modT", [128, 144])
    ct, Tct = C.sb([128, 16, 3], F32, "ct_sb")
    S.dma("sp", lambda e: e.dma_start(out=ct[:], in_=cT[:, :, :]), writes=[Tct])
    bt, Tbt = C.sb([128, 48], F32, "bt_sb")
    S.dma("sp", lambda e: e.dma_start(out=bt[:], in_=bada[:, :]), writes=[Tbt])
    _act(C, ct[:, :, :], ct[:, :, :], AF.Silu, [Tct], [Tct])
    ot, Tot = C.sb([128, 144], F32, "ot_sb")
    wb = Ring([C.sb([128, KC, 768], F32) for _ in range(2)])
    PB = Ring([C.ps() for _ in range(4)])
    for l in range(4):
        wv = wada[l * D:(l + 1) * D, :].rearrange("(c p) n -> p c n", p=128)
        for hf in range(2):
            w, Tw = wb.get()
            for c0 in range(0, KC, 4):
                S.dma("sp", lambda e, w=w, wv=wv, hf=hf, c0=c0: e.dma_start(out=w[:, c0:c0 + 4, :], in_=wv[:, c0:c0 + 4, hf * 768:(hf + 1) * 768]), writes=[Tw])
            for jj in range(6):
                j = hf * 6 + jj
                pb, Tp = PB.get()
                for k in range(KC):
                    _mm(C, pb[:, :3], w[:, k, jj * 128:(jj + 1) * 128], ct[:, k, :], [Tw, Tct], [Tp], start=(k == 0), stop=(k == KC - 1))
                col = (l * 12 + j)
                _act(C, ot[:, col * 3:col * 3 + 3], pb[:, :3], AF.Identity, [Tp, Tbt], [Tot], bias=bt[:, col:col + 1])
    S.dma("sp", lambda e: e.dma_start(out=out[:, :], in_=ot[:, :]), reads=[Tot])
    return C.finish([Tot])


IN_SPLITS = (3072, 1024, 16, 16, 1024, 1536, 32, 1024, 512, 6144)
OFFS = np.cumsum((0,) + IN_SPLITS)
MIX_PARTS = (0, 2, 3, 5, 6, 7, 8)
MIXCOLS = np.concatenate([np.arange(OFFS[i], OFFS[i + 1]) for i in MIX_PARTS])
_mo = np.cumsum([0] + [IN_SPLITS[i] for i in MIX_PARTS])
M_QKV, M_A, M_B, M_XBC, M_DT, M_Q, M_KV = [int(v) for v in _mo[:7]]
GATECOLS = np.concatenate([np.arange(OFFS[i], OFFS[i + 1]) for i in (1, 4, 9)])
def b_inputs(PT, j, prm):
    g = j // 4
    sl = lambda r0: np.ascontiguousarray(PT[r0:r0 + 128])
    d = {}
    d["gq"] = sl(M_QKV + j * 128)
    d["gk"] = sl(M_QKV + 1024 + j * 128)
    d["gv"] = sl(M_QKV + 2048 + j * 128)
    d["sx"] = sl(M_XBC + j * 128)
    d["sB"] = sl(M_XBC + 1024 + g * 128)
    d["sC"] = sl(M_XBC + 1024 + 256 + g * 128)
    d["aq"] = sl(M_Q + j * 128)
    d["ak"] = sl(M_KV + g * 128)
    d["av"] = sl(M_KV + 256 + g * 128)
    rows = [M_A + j, M_A + 8 + j, M_B + j, M_B + 8 + j,
            M_DT + 2 * j, M_DT + 2 * j + 1, M_DT + 16 + 2 * j, M_DT + 16 + 2 * j + 1]
    gt = PT[rows]
    d["gates"] = np.ascontiguousarray(gt.reshape(8, 68, 128).transpose(2, 1, 0))
    pv = np.zeros((128, 40), np.float32)
    cw = prm["gdn_conv_w"]
    for i in range(3):
        pv[:, 3 * i:3 * i + 3] = cw[:, i * 1024 + j * 128:i * 1024 + (j + 1) * 128].T
    sw = prm["ssm_conv_w"]
    sb = prm["ssm_conv_b"]
    for i, c0 in enumerate((j * 128, 1024 + g * 128, 1280 + g * 128)):
        pv[:, 9 + 3 * i:12 + 3 * i] = sw[:, c0:c0 + 128].T
        pv[:, 22 + i] = sb[c0:c0 + 128]
    pv[:, 18] = prm["gdn_a_log"][0, j]; pv[:, 19] = prm["gdn_a_log"][1, j]
    pv[:, 20] = prm["gdn_dt_bias"][0, j]; pv[:, 21] = prm["gdn_dt_bias"][1, j]
    k = 0
    for dd in range(2):
        for hh in range(2):
            pv[:, 25 + k] = prm["ssm_a_log"][dd, 2 * j + hh]
            pv[:, 29 + k] = prm["ssm_dt_bias"][dd, 2 * j + hh]
            k += 1
    pv[:, 33] = prm["ssm_d"][2 * j]; pv[:, 34] = prm["ssm_d"][2 * j + 1]
    pv[:, 35] = prm["attn_sink"][j]
    d["pv"] = pv
    return d


def vec16(v):
    return np.ascontiguousarray(np.asarray(v, np.float32).reshape(16, 128).T)


def halo_slice(arr, lo, hi):
    T = arr.shape[0]
    out = np.zeros((hi - lo, arr.shape[1]), arr.dtype)
    m = np.zeros(hi - lo, np.float32)
    a, b = max(lo, 0), min(hi, T)
    out[a - lo:b - lo] = arr[a:b]
    m[a - lo:b - lo] = 1.0
    return out, m


def c_inputs(i, X, XC, OL, OC, mod, modc, prm):
    b, q = i // 4, i % 4
    xl, ml = halo_slice(X[b], q * 1024 - 1, (q + 1) * 1024 + 1)
    xc, mc = halo_slice(XC[b], q * 64 - 1, (q + 1) * 64 + 1)
    ol, _ = halo_slice(OL[b], q * 1024 - 1, (q + 1) * 1024 + 1)
    oc, _ = halo_slice(OC[b], q * 64 - 1, (q + 1) * 64 + 1)
    d = {}
    d["xT"] = np.ascontiguousarray(np.concatenate([xl, xc], 0).T)
    d["oT"] = np.ascontiguousarray(np.concatenate([ol, oc], 0).T)
    d["hmask"] = np.ascontiguousarray(np.broadcast_to(np.concatenate([ml, mc])[None, :], (128, 1092))).astype(np.float32)
    vs = [prm["g_pre1"], prm["g_post1"], prm["g_pre2"], prm["g_post2"]]
    for m in (mod[b], modc):
        vs += [m[k * 2048:(k + 1) * 2048] for k in range(6)]
    d["vecs"] = np.ascontiguousarray(np.stack([vec16(v) for v in vs], 1))
    pv = np.zeros((128, 185), np.float32)
    pv[:, 0] = prm["gdn_norm"]
    pv[:, 1:9] = prm["ssm_norm"].reshape(8, 128).T
    pv[:, 9:141] = prm["ffn_conv_w"].reshape(3, 44, 128).transpose(2, 1, 0).reshape(128, 132)
    pv[:, 141:185] = prm["ffn_conv_b"].reshape(44, 128).T
    d["pvc"] = pv
    return d


def c_weights(W):
    return dict(wg=np.ascontiguousarray(W["w_in"][:, GATECOLS]),
                wbr=np.ascontiguousarray(np.concatenate([W["w_branch_a"], W["w_branch_b"], W["w_branch_c"]], 0)),
                wout=W["w_out"], wup=W["w_up"], wdn=W["w_down"])


def _run(nc, in_maps):
    res = run_bass_kernel_spmd(nc, in_maps, core_ids=list(range(8)))
    return res.results


def kernel(**inp):
    inp = {k: np.asarray(v) for k, v in inp.items()}
    x, ctx, c, c_ctx = inp["x"], inp["ctx"], inp["c"], inp["c_ctx"]
    f32 = np.float32
    ncM = build_M()
    cT = np.ascontiguousarray(np.stack([c[0], c[1], c_ctx], 0).reshape(3, 16, 128).transpose(2, 1, 0)).astype(f32)
    maps = []
    for i in range(8):
        cs = slice(i * 1536, (i + 1) * 1536)
        maps.append(dict(wada=np.ascontiguousarray(inp["w_ada"][:, :, cs].reshape(4 * 2048, 1536)),
                         bada=np.ascontiguousarray(inp["b_ada"][:, cs].reshape(4, 12, 128).transpose(2, 0, 1).reshape(128, 48)),
                         cT=cT))
    res = _run(ncM, maps)
    mod_all = np.concatenate([r["modT"].reshape(128, 4, 12, 3).transpose(1, 3, 2, 0).reshape(4, 3, 1536) for r in res], -1)
    ncA, ncB, ncC = build_A(), build_B(), build_C()
    cst, cosF, sinF = b_consts()
    X = np.array(x, f32)
    XC = np.array(ctx, f32)
    pnames = ["gdn_conv_w", "gdn_a_log", "gdn_dt_bias", "gdn_norm", "ssm_conv_w", "ssm_conv_b", "ssm_a_log", "ssm_dt_bias", "ssm_d", "ssm_norm",
              "attn_sink", "g_pre1", "g_post1", "g_pre2", "g_post2", "ffn_conv_w", "ffn_conv_b", "w_in", "w_branch_a", "w_branch_b", "w_branch_c",
              "w_out", "w_up", "w_down"]
    for l in range(4):
        prm = {k: inp[k][l] for k in pnames}
        mod, modc = mod_all[l, 0:2], mod_all[l, 2]
        wmix = np.ascontiguousarray(prm["w_in"][:, MIXCOLS])
        maps = []
        for i in range(8):
            b, q = i // 4, i % 4
            xs = np.concatenate([X[b, q * 1024:(q + 1) * 1024], XC[b, q * 64:(q + 1) * 64]], 0)
            vecs = np.stack([vec16(prm["g_pre1"]), vec16(mod[b, 2048:4096]), vec16(mod[b, 0:2048]), vec16(modc[2048:4096]), vec16(modc[0:2048])], 1)
            maps.append(dict(xT=np.ascontiguousarray(xs.T), vecs=np.ascontiguousarray(vecs), wmix=wmix))
        res = _run(ncA, maps)
        PT = np.empty((6208, 8704), f32)
        for i in range(8):
            b, q = i // 4, i % 4
            PT[:, b * 4352 + 256 + q * 1024:b * 4352 + 256 + (q + 1) * 1024] = res[i]["pT"][:, :1024]
            PT[:, b * 4352 + q * 64:b * 4352 + (q + 1) * 64] = res[i]["pT"][:, 1024:]
        maps = []
        for j in range(8):
            m = b_inputs(PT, j, prm)
            m.update(cst=cst, cosF=cosF, sinF=sinF)
            maps.append(m)
        res = _run(ncB, maps)
        OL = np.empty((2, 4096, 3072), f32)
        OC = np.empty((2, 256, 3072), f32)
        for j in range(8):
            for mi, nm in enumerate(("oa", "ob", "oc")):
                o = res[j][nm]
                for b in range(2):
                    OC[b, :, mi * 1024 + j * 128:mi * 1024 + (j + 1) * 128] = o[b * 4352:b * 4352 + 256]
                    OL[b, :, mi * 1024 + j * 128:mi * 1024 + (j + 1) * 128] = o[b * 4352 + 256:(b + 1) * 4352]
        wts = c_weights(prm)
        maps = []
        for i in range(8):
            m = c_inputs(i, X, XC, OL, OC, mod, modc, prm)
            m.update(wts)
            maps.append(m)
        res = _run(ncC, maps)
        Xn = np.empty_like(X)
        XCn = np.empty_like(XC)
        for i in range(8):
            b, q = i // 4, i % 4
            xo = res[i]["xoT"]
            Xn[b, q * 1024:(q + 1) * 1024] = xo[:, :1024].T
            XCn[b, q * 64:(q + 1) * 64] = xo[:, 1024:].T
        X, XC = Xn, XCn
    return X.astype(f32)
```

```python
import numpy as np
import concourse.bass as bass
import concourse.mybir as mybir

F32 = mybir.dt.float32
BF16 = mybir.dt.bfloat16
AF = mybir.ActivationFunctionType
ALU = mybir.AluOpType
AX = mybir.AxisListType

ENGS = ("pe", "act", "dve", "pool", "sp")


class T:
    __slots__ = ("name", "last_w", "readers", "sem", "semcnt", "excl")

    def __init__(self, name):
        self.name = name
        self.last_w = None
        self.readers = []
        self.sem = None
        self.semcnt = 0
        self.excl = False


class Op:
    __slots__ = ("eng", "fn", "waits", "signal", "dma_tile", "idx", "inc")

    def __init__(self, eng, fn):
        self.eng = eng
        self.fn = fn
        self.waits = []
        self.signal = False
        self.dma_tile = None


class Sched:
    def __init__(self, same_engine_sync=True):
        self.ops = {e: [] for e in ENGS}
        self.seen = {e: {} for e in ENGS}
        self.same = same_engine_sync
        self.dma_tiles = []
        self.final_waits = []

    def _dep(self, op, d):
        if d is None:
            return
        if d[0] == 'e':
            _, eng, idx = d
            if eng == op.eng:
                if eng == "pe" or not self.same or eng == "sp":
                    return
                if idx >= len(self.ops[eng]):
                    return
            key = ('e', eng)
            cnt = idx
            self.ops[eng][idx].signal = True
        else:
            _, tile, cnt = d
            cnt = tile.semcnt
            d = ('d', tile, cnt)
            key = ('d', id(tile))
        seen = self.seen[op.eng]
        if seen.get(key, -1) >= cnt:
            return
        seen[key] = cnt
        op.waits.append(d)

    def op(self, eng, fn, reads=(), writes=()):
        o = Op(eng, fn)
        o.idx = len(self.ops[eng])
        for t in reads:
            self._dep(o, t.last_w)
            if t.excl:
                for r in t.readers:
                    if r[0] == 'e' and r[1] != eng:
                        self._dep(o, r)
        for t in writes:
            self._dep(o, t.last_w)
            for r in t.readers:
                self._dep(o, r)
        self.ops[eng].append(o)
        me = ('e', eng, o.idx)
        for t in reads:
            t.readers.append(me)
        for t in writes:
            t.last_w = me
            t.readers = []
        return o

    def dma(self, eng, fn, reads=(), writes=(), sem_tile=None, inc=16):
        o = Op(eng, fn)
        o.idx = len(self.ops[eng])
        for t in reads:
            self._dep(o, t.last_w)
        for t in writes:
            self._dep(o, t.last_w)
            for r in t.readers:
                self._dep(o, r)
        st = sem_tile or (writes[0] if writes else reads[0])
        if st.sem is None:
            st.sem = True
            self.dma_tiles.append(st)
        st.semcnt += inc
        o.dma_tile = st
        o.inc = inc
        self.ops[eng].append(o)
        me = ('d', st, st.semcnt)
        for t in reads:
            t.readers.append(me)
        for t in writes:
            t.last_w = me
            t.readers = []
        return o

    def barrier(self):
        last = {e: len(self.ops[e]) - 1 for e in ENGS if e != "sp" and self.ops[e]}
        dts = [(t, t.semcnt) for t in self.dma_tiles]
        for e in ENGS:
            o = Op(e, None)
            o.idx = len(self.ops[e])
            for pe_, idx in last.items():
                if pe_ != e:
                    self._dep(o, ('e', pe_, idx))
            for t, c in dts:
                self._dep(o, ('d', t, c))
            self.ops[e].append(o)

    def finish(self, eng, tiles):
        def nopfn(e):
            return None
        o = Op(eng, None)
        o.idx = len(self.ops[eng])
        for t in tiles:
            self._dep(o, t.last_w)
            for r in t.readers:
                self._dep(o, r)
        self.ops[eng].append(o)

    def emit(self, nc, stack):
        esem = {}
        for e in ENGS:
            if any(o.signal for o in self.ops[e]):
                esem[e] = stack.enter_context(nc.semaphore("s_" + e))
        for t in self.dma_tiles:
            t.sem = stack.enter_context(nc.semaphore("d_" + t.name))
        cnt_at = {}
        for e in ENGS:
            c = 0
            for o in self.ops[e]:
                if o.signal:
                    c += 1
                    cnt_at[(e, o.idx)] = c
        block = stack.enter_context(nc.Block())

        def run(e, handle):
            for o in self.ops[e]:
                for d in o.waits:
                    if d[0] == 'e':
                        handle.wait_ge(esem[d[1]], cnt_at[(d[1], d[2])])
                    else:
                        handle.wait_ge(d[1].sem, d[2])
                if o.fn is None:
                    continue
                ins = o.fn(handle)
                if o.dma_tile is not None:
                    ins.then_inc(o.dma_tile.sem, o.inc)
                elif o.signal:
                    ins.then_inc(esem[e], 1)

        @block.tensor
        def _(h):
            run("pe", h)

        @block.scalar
        def _(h):
            run("act", h)

        @block.vector
        def _(h):
            run("dve", h)

        @block.gpsimd
        def _(h):
            run("pool", h)

        @block.sync
        def _(h):
            run("sp", h)


from contextlib import ExitStack
import numpy as np
from concourse.bass_utils import run_bass_kernel_spmd

D = 2048
KC = 16
EPS = 1e-6


class Ctx:
    def __init__(self):
        self.nc = bass.Bass("TRN2", target_bir_lowering=False)
        self.S = Sched()
        self.st = ExitStack()
        self.n = 0

    def sb(self, shape, dt=F32, name=None):
        self.n += 1
        name = name or f"sb{self.n}"
        t = self.st.enter_context(self.nc.sbuf_tensor(name, list(shape), dt))
        return t, T(name)

    def ps(self, shape=(128, 512), dt=F32, name=None):
        self.n += 1
        name = name or f"ps{self.n}"
        t = self.st.enter_context(self.nc.psum_tensor(name, list(shape), dt))
        tt = T(name)
        tt.excl = True
        return t, tt

    def dram_in(self, name, shape, dt=F32):
        return self.nc.dram_tensor(name, list(shape), dt, kind="ExternalInput").ap()

    def dram_out(self, name, shape, dt=F32):
        return self.nc.dram_tensor(name, list(shape), dt, kind="ExternalOutput").ap()

    def finish(self, out_tiles):
        self.S.finish("sp", out_tiles)
        self.S.emit(self.nc, self.st)
        self.st.close()
        return self.nc


def consts(C):
    ones, To = C.sb([128, 128], F32, "ones")
    C.S.op("pool", lambda e: e.memset(ones[:], 1.0), writes=[To])
    return ones, To


def norm_mod(C, xt, Tx, n, ones, To, eff, sh, Tv, h, Th, pss, Tps, scr):
    S = C.S
    for c in range(KC):
        sq, Tsq = scr['sq'][c % 2]
        S.op("act", lambda e, c=c, sq=sq: e.activation(out=sq[:, :n], in_=xt[:, c, :n], func=AF.Square),
             reads=[Tx], writes=[Tsq])
        S.op("pe", lambda e, c=c, sq=sq: e.matmul(pss[:, :n], lhsT=ones[:], rhs=sq[:, :n],
                                                  start=(c == 0), stop=(c == KC - 1)),
             reads=[Tsq, To], writes=[Tps])
    rstd, Tr = scr['rstd']
    S.op("act", lambda e: e.activation(out=rstd[:, :n], in_=pss[:, :n], func=AF.Sqrt, scale=1.0 / D, bias=scr['eps'][0][:, 0:1]),
         reads=[Tps, scr['eps'][1]], writes=[Tr])
    S.op("dve", lambda e: e.reciprocal(out=rstd[:, :n], in_=rstd[:, :n]), reads=[Tr], writes=[Tr])
    for c in range(KC):
        tmp, Tt = scr['tmp'][c % 2]
        S.op("dve", lambda e, c=c, tmp=tmp: e.tensor_tensor(out=tmp[:, :n], in0=xt[:, c, :n], in1=rstd[:, :n], op=ALU.mult),
             reads=[Tx, Tr], writes=[Tt])
        S.op("act", lambda e, c=c, tmp=tmp: e.activation(out=h[:, c, :n], in_=tmp[:, :n], func=AF.Identity,
                                                        scale=eff[:, c:c + 1], bias=sh[:, c:c + 1]),
             reads=[Tt, Tv], writes=[Th])


def norm_mod2(C, xt, Tx, n, ones, To, segs, Tv, h, Th, pss, Tps, scr):
    S = C.S
    for c in range(KC):
        sq, Tsq = scr['sq'][c % 2]
        S.op("act", lambda e, c=c, sq=sq: e.activation(out=sq[:, :n], in_=xt[:, c, :n], func=AF.Square), reads=[Tx], writes=[Tsq])
        S.op("pe", lambda e, c=c, sq=sq: e.matmul(pss[:, :n], lhsT=ones[:], rhs=sq[:, :n], start=(c == 0), stop=(c == KC - 1)),
             reads=[Tsq, To], writes=[Tps])
    rstd, Tr = scr['rstd']
    S.op("act", lambda e: e.activation(out=rstd[:, :n], in_=pss[:, :n], func=AF.Sqrt, scale=1.0 / D, bias=scr['eps'][0][:, 0:1]),
         reads=[Tps, scr['eps'][1]], writes=[Tr])
    S.op("dve", lambda e: e.reciprocal(out=rstd[:, :n], in_=rstd[:, :n]), reads=[Tr], writes=[Tr])
    for c in range(KC):
        tmp, Tt = scr['tmp'][c % 2]
        S.op("dve", lambda e, c=c, tmp=tmp: e.tensor_tensor(out=tmp[:, :n], in0=xt[:, c, :n], in1=rstd[:, :n], op=ALU.mult),
             reads=[Tx, Tr], writes=[Tt])
        for (c0, c1, eff, sh) in segs:
            S.op("act", lambda e, c=c, tmp=tmp, c0=c0, c1=c1, eff=eff, sh=sh: e.activation(out=h[:, c, c0:c1], in_=tmp[:, c0:c1], func=AF.Identity,
                                                                                          scale=eff[:, c:c + 1], bias=sh[:, c:c + 1]),
                 reads=[Tt, Tv], writes=[Th])


def make_scr(C):
    scr = {}
    scr['sq'] = [C.sb([128, 512], F32) for _ in range(2)]
    scr['tmp'] = [C.sb([128, 512], F32) for _ in range(2)]
    scr['rstd'] = C.sb([128, 512], F32)
    scr['eps'] = C.sb([128, 1], F32, "epsc")
    C.S.op("pool", lambda e: e.memset(scr['eps'][0][:], EPS), writes=[scr['eps'][1]])
    return scr


NMIX = 6208
A_TILES = [(0, 512, 0), (512, 512, 0), (1024, 64, 1)]
NT_A = 1088


def build_A():
    C = Ctx()
    nc, S = C.nc, C.S
    xT = C.dram_in("xT", [D, NT_A])
    vecs = C.dram_in("vecs", [128, 5, 16])
    wmix = C.dram_in("wmix", [D, NMIX])
    pT = C.dram_out("pT", [NMIX, NT_A])
    ones, To = consts(C)
    scr = make_scr(C)
    vt, Tv = C.sb([128, 5, 16], F32, "vecs_sb")
    eff, _ = C.sb([128, 2, 16], F32, "eff")
    S.dma("sp", lambda e: e.dma_start(out=vt[:], in_=vecs[:, :, :]), writes=[Tv])
    for s, (isc) in enumerate((1, 3)):
        S.op("dve", lambda e, s=s, isc=isc: e.scalar_tensor_tensor(out=eff[:, s, :], in0=vt[:, isc, :], scalar=1.0, in1=vt[:, 0, :],
                                                                   op0=ALU.add, op1=ALU.mult), reads=[Tv], writes=[Tv])
    h, Th = C.sb([128, KC, NT_A], BF16, "h")
    xv = xT.rearrange("(c p) t -> p c t", p=128)
    pss, Tps = C.ps(name="pss")
    xbufs = [C.sb([128, KC, 512], F32) for _ in range(2)]
    for ti, (t0, n, vs) in enumerate(A_TILES):
        xt, Tx = xbufs[ti % 2]
        S.dma("sp", lambda e, xt=xt, t0=t0, n=n: e.dma_start(out=xt[:, :, :n], in_=xv[:, :, t0:t0 + n]), writes=[Tx])
        hv = h[:, :, t0:t0 + n]
        norm_mod(C, xt, Tx, n, ones, To, eff[:, vs, :], vt[:, 2 + 2 * vs, :], Tv, hv, Th, pss, Tps, scr)
    wv = wmix.rearrange("(c p) n -> p c n", p=128)
    wb = [C.sb([128, KC, 512], BF16) for _ in range(3)]
    pbs = [C.ps() for _ in range(4)]
    obs = [C.sb([128, 512], F32) for _ in range(4)]
    nslab = (NMIX + 511) // 512
    g = 0
    for s in range(nslab):
        c0 = s * 512
        cw = min(512, NMIX - c0)
        w, Tw = wb[s % 3]
        S.dma("pool", lambda e, w=w, c0=c0, cw=cw: e.dma_start(out=w[:, :, :cw], in_=wv[:, :, c0:c0 + cw]), writes=[Tw])
        for (t0, n, vs) in A_TILES:
            for oc in range((cw + 127) // 128):
                m = min(128, cw - oc * 128)
                pb, Tp = pbs[g % 4]
                ob, Tob = obs[g % 4]
                for k in range(KC):
                    S.op("pe", lambda e, pb=pb, w=w, k=k, oc=oc, m=m, t0=t0, n=n: e.matmul(
                        pb[:m, :n], lhsT=w[:, k, oc * 128:oc * 128 + m], rhs=h[:, k, t0:t0 + n],
                        start=(k == 0), stop=(k == KC - 1)), reads=[Tw, Th], writes=[Tp])
                ev = "act" if g % 2 == 0 else "dve"
                if ev == "act":
                    S.op("act", lambda e, pb=pb, ob=ob, m=m, n=n: e.copy(out=ob[:m, :n], in_=pb[:m, :n]), reads=[Tp], writes=[Tob])
                else:
                    S.op("dve", lambda e, pb=pb, ob=ob, m=m, n=n: e.tensor_copy(out=ob[:m, :n], in_=pb[:m, :n]), reads=[Tp], writes=[Tob])
                r0 = c0 + oc * 128
                S.dma("sp", lambda e, ob=ob, r0=r0, m=m, t0=t0, n=n: e.dma_start(out=pT[r0:r0 + m, t0:t0 + n], in_=ob[:m, :n]),
                      reads=[Tob])
                g += 1
    return C.finish([t for _, t in obs])


POOLMAP = {}
def _mm(C, out, lhsT, rhs, R, W, start=True, stop=True):
    return C.S.op("pe", lambda e: e.matmul(out, lhsT=lhsT, rhs=rhs, start=start, stop=stop), reads=R, writes=W)


def _tr(C, out, in_, ident, R, W):
    return C.S.op("pe", lambda e: e.transpose(out, in_, ident), reads=R, writes=W)


def _act(C, out, in_, func, R, W, scale=None, bias=None, accum=None):
    kw = {}
    if scale is not None:
        kw["scale"] = scale
    if bias is not None:
        kw["bias"] = bias
    if accum is not None:
        kw["accum_out"] = accum
    return C.S.op("act", lambda e: e.activation(out=out, in_=in_, func=func, **kw), reads=R, writes=W)


def _tt(C, eng, out, in0, in1, op, R, W):
    eng = POOLMAP.get(eng, eng)
    return C.S.op(eng, lambda e: e.tensor_tensor(out=out, in0=in0, in1=in1, op=op), reads=R, writes=W)


def _ts(C, eng, out, in0, s1, s2, op0, op1, R, W):
    eng = POOLMAP.get(eng, eng)
    if op1 is None:
        return C.S.op(eng, lambda e: e.tensor_scalar(out=out, in0=in0, scalar1=s1, scalar2=None, op0=op0), reads=R, writes=W)
    return C.S.op(eng, lambda e: e.tensor_scalar(out=out, in0=in0, scalar1=s1, scalar2=s2, op0=op0, op1=op1), reads=R, writes=W)


def _stt(C, out, in0, scalar, in1, op0, op1, R, W):
    return C.S.op("dve", lambda e: e.scalar_tensor_tensor(out=out, in0=in0, scalar=scalar, in1=in1, op0=op0, op1=op1), reads=R, writes=W)


def _cp(C, eng, out, in_, R, W):
    if eng == "act":
        return C.S.op("act", lambda e: e.copy(out=out, in_=in_), reads=R, writes=W)
    return C.S.op(eng, lambda e: e.tensor_copy(out=out, in_=in_), reads=R, writes=W)


class Ring:
    def __init__(self, items):
        self.items = items
        self.i = 0

    def get(self):
        it = self.items[self.i % len(self.items)]
        self.i += 1
        return it


TB = 4352
NCH = 34
NTOK = 2 * TB
CI, CLO, CUP, CSLO, CSUP, CTLO, CTUP, CRM = [i * 128 for i in range(8)]
NCST = 8 * 128
MUL, ADD, SUB, MAX, MIN = ALU.mult, ALU.add, ALU.subtract, ALU.max, ALU.min


def b_consts():
    p = np.arange(128)[:, None]
    f = np.arange(128)[None, :]
    NEG = -30000.0
    cst = np.zeros((128, NCST), np.float32)
    cst[:, CI:CI + 128] = (p == f)
    cst[:, CLO:CLO + 128] = np.where(f <= p, 0.0, NEG)
    cst[:, CUP:CUP + 128] = np.where(f >= p, 0.0, NEG)
    cst[:, CSLO:CSLO + 128] = (f < p)
    cst[:, CSUP:CSUP + 128] = (f > p)
    cst[:, CTLO:CTLO + 128] = (p >= f)
    cst[:, CTUP:CTUP + 128] = (p <= f)
    rm = np.zeros((128, 128), np.float32)
    for base in (0, 64):
        for m in range(32):
            rm[base + m + 32, base + m] = -1.0
            rm[base + m, base + m + 32] = 1.0
    cst[:, CRM:CRM + 128] = rm
    t = np.arange(4096)
    row = (t // 64).astype(np.float32)
    col = (t % 64).astype(np.float32)
    inv = (10000.0 ** (-np.arange(32, dtype=np.float32) / 32)).astype(np.float32)
    cosF = np.zeros((128, 4096), np.float32)
    sinF = np.zeros((128, 4096), np.float32)
    for pp in range(128):
        pos = row if pp < 64 else col
        ang = (pos * inv[pp % 32]).astype(np.float32)
        cosF[pp] = np.cos(ang)
        sinF[pp] = np.sin(ang)
    return cst, cosF, sinF


def build_B(which=("gdn", "ssd", "attn"), batches=(0, 1)):
    extra_fns = {}
    C = Ctx()
    nc, S = C.nc, C.S
    names = ["gq", "gk", "gv", "sx", "sB", "sC", "aq", "ak", "av"]
    din = {n: C.dram_in(n, [128, NTOK]) for n in names}
    gates = C.dram_in("gates", [128, 2 * NCH, 8])
    pvd = C.dram_in("pv", [128, 40])
    cstd = C.dram_in("cst", [128, NCST])
    cosd = C.dram_in("cosF", [128, 4096])
    sind = C.dram_in("sinF", [128, 4096])
    outs = {n: C.dram_out(n, [NTOK, 128]) for n in ("oa", "ob", "oc")}

    cst, Tc = C.sb([128, NCST], F32, "cst_sb")
    S.dma("sp", lambda e: e.dma_start(out=cst[:], in_=cstd[:, :]), writes=[Tc])
    pv, Tpv = C.sb([128, 40], F32, "pv_sb")
    S.dma("sp", lambda e: e.dma_start(out=pv[:], in_=pvd[:, :]), writes=[Tpv])
    ident = cst[:, CI:CI + 128]
    ones, To = consts(C)
    epsc, Te = C.sb([128, 1], F32, "epsc")
    S.op("pool", lambda e: e.memset(epsc[:], EPS), writes=[Te])
    onec, T1 = C.sb([128, 1], F32, "onec")
    S.op("pool", lambda e: e.memset(onec[:], 1.0), writes=[T1])

    slabs = [None] + [C.sb([128, TB], F32, f"slab{i}") for i in range(1, 8)]
    slabs[0] = slabs[6]
    gt, Tg = C.sb([128, NCH, 8], F32, "gates_sb")
    banks = [C.ps(name=f"bank{i}") for i in range(8)]
    small = [(banks[bi][0][:, 0:128], banks[bi][1]) for bi in range(8)]
    PS = Ring(small)
    wide = Ring([(banks[6 + i][0], banks[6 + i][1]) for i in range(2)])

    def ringsb(n, shape=(128, 128), dt=F32):
        return Ring([C.sb(list(shape), dt) for _ in range(n)])

    def conv_silu(raw, Traw, dst, Tdst, wcol, bias_ap=None):
        for (s0, s1) in ((0, 256), (256, TB)):
            S.op("act", lambda e, s0=s0, s1=s1: e.activation(out=dst[:, s0:s1], in_=raw[:, s0:s1], func=AF.Identity,
                                                             scale=wcol[:, 1:2], **({} if bias_ap is None else {"bias": bias_ap})),
                 reads=[Traw, Tpv], writes=[Tdst])
            _stt(C, dst[:, s0 + 1:s1], raw[:, s0:s1 - 1], wcol[:, 0:1], dst[:, s0 + 1:s1], MUL, ADD, [Traw, Tpv, Tdst], [Tdst])
            _stt(C, dst[:, s0:s1 - 1], raw[:, s0 + 1:s1], wcol[:, 2:3], dst[:, s0:s1 - 1], MUL, ADD, [Traw, Tpv, Tdst], [Tdst])
        _act(C, dst[:, :], dst[:, :], AF.Silu, [Tdst], [Tdst])

    sqr = ringsb(2, (128, 512))

    def l2norm(dst, Tdst, mul):
        for c0 in range(0, TB, 512):
            n = min(512, TB - c0)
            sq, Tsq = sqr.get()
            _act(C, sq[:, :n], dst[:, c0:c0 + n], AF.Square, [Tdst], [Tsq])
            pw, Tpw = wide.get()
            _mm(C, pw[:, :n], ones[:], sq[:, :n], [Tsq, To], [Tpw])
            _act(C, sq[:, :n], pw[:, :n], AF.Sqrt, [Tpw, Te], [Tsq], bias=epsc[:, 0:1])
            S.op("dve", lambda e, sq=sq, n=n: e.reciprocal(out=sq[:, :n], in_=sq[:, :n]), reads=[Tsq], writes=[Tsq])
            _stt(C, dst[:, c0:c0 + n], dst[:, c0:c0 + n], float(mul), sq[:, :n], MUL, MUL, [Tdst, Tsq], [Tdst])

    def to_tm(src, Tsrc, dst, Tdst, chunks=range(NCH)):
        for ch in chunks:
            pt, Tp = PS.get()
            _tr(C, pt, src[:, ch * 128:(ch + 1) * 128], ident, [Tsrc, Tc], [Tp])
            _cp(C, "act" if ch % 2 else "dve", dst[:, ch * 128:(ch + 1) * 128], pt, [Tp], [Tdst])

    def load(name, b, dst, Tdst):
        S.dma("sp", lambda e: e.dma_start(out=dst[:, :], in_=din[name][:, b * TB:(b + 1) * TB]), writes=[Tdst])

    def store(oname, b, src, Tsrc):
        ov = outs[oname].rearrange("(c p) d -> p c d", p=128)
        sv = src[:, :].rearrange("p (c d) -> p c d", d=128)
        for c0 in range(0, NCH, 6):
            c1 = min(NCH, c0 + 6)
            S.dma("sp", lambda e, c0=c0, c1=c1: e.dma_start(out=ov[:, b * NCH + c0:b * NCH + c1, :], in_=sv[:, c0:c1, :]),
                  reads=[Tsrc])

    tabcache = {}
    sbcache = {}

    def sbc(shape, name):
        if name not in sbcache:
            sbcache[name] = C.sb(shape, F32, name)
        return sbcache[name]

    def tab(name):
        if name not in tabcache:
            tabcache[name] = C.sb([128, NCH], F32, name)
        return tabcache[name]

    def cum_tabs(gsrc_ap, Tsrc, d, names):
        tri = cst[:, CTUP:CTUP + 128] if d == 0 else cst[:, CTLO:CTLO + 128]
        r = {}
        pc, Tpc = PS.get()
        _mm(C, pc[:, :NCH], tri, gsrc_ap, [Tc, Tsrc], [Tpc])
        r['gc'] = tab(names + "gc")
        _cp(C, "dve", r['gc'][0][:, :], pc[:, :NCH], [Tpc], [r['gc'][1]])
        pl, Tpl = PS.get()
        _mm(C, pl[:, :NCH], ones[:], gsrc_ap, [To, Tsrc], [Tpl])
        r['gl'] = tab(names + "gl")
        _cp(C, "dve", r['gl'][0][:, :], pl[:, :NCH], [Tpl], [r['gl'][1]])
        r['ngc'] = tab(names + "ngc")
        _ts(C, "dve", r['ngc'][0][:, :], r['gc'][0][:, :], -1.0, None, MUL, None, [r['gc'][1]], [r['ngc'][1]])
        r['eg'] = tab(names + "eg")
        _act(C, r['eg'][0][:, :], r['gc'][0][:, :], AF.Exp, [r['gc'][1]], [r['eg'][1]])
        r['el'] = tab(names + "el")
        _act(C, r['el'][0][:, :], r['gl'][0][:, :], AF.Exp, [r['gl'][1]], [r['el'][1]])
        r['kd'] = tab(names + "kd")
        _tt(C, "dve", r['kd'][0][:, :], r['gl'][0][:, :], r['gc'][0][:, :], SUB, [r['gl'][1], r['gc'][1]], [r['kd'][1]])
        _act(C, r['kd'][0][:, :], r['kd'][0][:, :], AF.Exp, [r['kd'][1]], [r['kd'][1]])
        return r

    R = {k: ringsb(3) for k in ("diag", "t1", "t2", "vb", "kbg", "vnew")}
    R.update({k: ringsb(4) for k in ("Dm", "DTm", "EG", "Nm", "NmT", "P", "PT", "XT")})
    R.update({k: ringsb(5) for k in ("u", "wT", "attnT", "qdT", "kdec")})

    def lockstep(gens):
        gens = list(gens)
        while gens:
            nxt = []
            for g in gens:
                try:
                    next(g)
                    nxt.append(g)
                except StopIteration:
                    pass
            gens = nxt

    def grow_mats(gc_tab, ch, d, want_D, out, want_EG=True):
        gc, Tgc = gc_tab['gc']
        ngc, Tngc = gc_tab['ngc']
        LO = cst[:, CLO:CLO + 128]
        UP = cst[:, CUP:CUP + 128]
        negm, negmT = (LO, UP) if d == 0 else (UP, LO)
        dg, Tdg = R["diag"].get()
        _ts(C, "pool", dg[:, :], ident, gc[:, ch:ch + 1], None, MUL, None, [Tc, Tgc], [Tdg])
        yield
        pg, Tpg = PS.get()
        _mm(C, pg, ones[:], dg[:, :], [To, Tdg], [Tpg])
        yield
        t2, Tt2 = R["t2"].get()
        _tt(C, "dve", t2[:, :], pg, negmT, ADD, [Tpg, Tc], [Tt2])
        if want_D:
            t1, Tt1 = R["t1"].get()
            _stt(C, t1[:, :], pg, -1.0, negm, MUL, ADD, [Tpg, Tc], [Tt1])
        yield
        DTm, TDT = R["DTm"].get()
        _act(C, DTm[:, :], t2[:, :], AF.Exp, [Tt2, Tngc], [TDT], bias=ngc[:, ch:ch + 1])
        out['DTm'] = (DTm, TDT)
        if want_D:
            Dm, TD = R["Dm"].get()
            _act(C, Dm[:, :], t1[:, :], AF.Exp, [Tt1, Tgc], [TD], bias=gc[:, ch:ch + 1])
            out['Dm'] = (Dm, TD)
        if want_EG:
            EG, TEG = R["EG"].get()
            _act(C, EG[:, :], pg, AF.Exp, [Tpg], [TEG])
            out['EG'] = (EG, TEG)
        yield

    def gdn(b):
        raw, Traw = slabs[0]
        qf, Tq = slabs[1]
        kf, Tk = slabs[2]
        vf, Tv = slabs[3]
        ktm, Tktm = slabs[4]
        vtm, Tvtm = slabs[5]
        obs = [slabs[6], slabs[7]]
        S.dma("sp", lambda e: e.dma_start(out=gt[:], in_=gates[:, b * NCH:(b + 1) * NCH, :]), writes=[Tg])
        for i, (nm, (dst, Td)) in enumerate((("gq", slabs[1]), ("gk", slabs[2]), ("gv", slabs[3]))):
            load(nm, b, raw, Traw)
            conv_silu(raw, Traw, dst, Td, pv[:, 3 * i:3 * i + 3])
        l2norm(qf, Tq, 128 ** -0.5)
        l2norm(kf, Tk, 1.0)
        to_tm(kf, Tk, ktm, Tktm)
        to_tm(vf, Tv, vtm, Tvtm)
        nA, TnA = sbc([128, 2], "gdn_nA")
        _act(C, nA[:, :], pv[:, 18:20], AF.Exp, [Tpv], [TnA])
        _ts(C, "dve", nA[:, :], nA[:, :], -1.0, None, MUL, None, [TnA], [TnA])
        tabs = []
        betas = []
        for d in range(2):
            g, Tgd = tab(f"gdn_g{d}")
            _act(C, g[:, :], gt[:, :, d], AF.Exp, [Tg, Tpv], [Tgd], bias=pv[:, 20 + d:21 + d])
            _act(C, g[:, :], g[:, :], AF.Ln, [Tgd, T1], [Tgd], bias=onec[:, 0:1])
            _ts(C, "dve", g[:, :], g[:, :], nA[:, d:d + 1], None, MUL, None, [Tgd, TnA], [Tgd])
            tb = cum_tabs(g[:, :], Tgd, d, f"gdn{d}")
            be, Tbe = tab(f"gdn_beta{d}")
            _act(C, be[:, :], gt[:, :, 2 + d], AF.Sigmoid, [Tg], [Tbe])
            nb_, Tnb = tab(f"gdn_nbeta{d}")
            _ts(C, "dve", nb_[:, :], be[:, :], -1.0, None, MUL, None, [Tbe], [Tnb])
            bg, Tbg = tab(f"gdn_bg{d}")
            _tt(C, "dve", bg[:, :], be[:, :], tb['eg'][0][:, :], MUL, [Tbe, tb['eg'][1]], [Tbg])
            tb['beta'] = (be, Tbe)
            tb['nbeta'] = (nb_, Tnb)
            tb['bg'] = (bg, Tbg)
            tabs.append(tb)
        Sst = [sbc([128, 128], f"gdnS{d}") for d in range(2)]
        for d in range(2):
            S.op("pool", lambda e, d=d: e.memset(Sst[d][0][:], 0.0), writes=[Sst[d][1]])

        def pre(d, ch, m):
            tb = tabs[d]
            cs = slice(ch * 128, (ch + 1) * 128)
            gm = {}
            yield from grow_mats(tb, ch, d, True, gm)
            Dm, TD = gm['Dm']
            DTm, TDT = gm['DTm']
            EG, TEG = gm['EG']
            strict = cst[:, CSLO:CSLO + 128] if d == 0 else cst[:, CSUP:CSUP + 128]
            _tt(C, "pool", Dm[:, :], Dm[:, :], strict, MUL, [TD, Tc], [TD])
            pk, Tpk = PS.get()
            _mm(C, pk, kf[:, cs], kf[:, cs], [Tk], [Tpk])
            yield
            Nm, TN = R["Nm"].get()
            _stt(C, Nm[:, :], pk, tb['nbeta'][0][:, ch:ch + 1], Dm[:, :], MUL, MUL, [Tpk, tb['nbeta'][1], TD], [TN])
            yield
            pt, Tpt = PS.get()
            _tr(C, pt, Nm[:, :], ident, [TN, Tc], [Tpt])
            yield
            NmT, TNT = R["NmT"].get()
            _cp(C, "act", NmT[:, :], pt, [Tpt], [TNT])
            yield
            XT, TX = R["XT"].get()
            _tt(C, "dve", XT[:, :], NmT[:, :], ident, ADD, [TNT, Tc], [TX])
            P, TP, PT, TPT = Nm, TN, NmT, TNT
            for s_ in range(1, 7):
                pp, Tpp = PS.get()
                _mm(C, pp, PT[:, :], P[:, :], [TPT, TP], [Tpp])
                if s_ < 6:
                    pq, Tpq = PS.get()
                    _mm(C, pq, P[:, :], PT[:, :], [TP, TPT], [Tpq])
                yield
                Pn, TPn = R["P"].get()
                _cp(C, "act", Pn[:, :], pp, [Tpp], [TPn])
                if s_ < 6:
                    PTn, TPTn = R["PT"].get()
                    _cp(C, "dve", PTn[:, :], pq, [Tpq], [TPTn])
                yield
                px, Tpx = PS.get()
                _mm(C, px, Pn[:, :], XT[:, :], [TPn, TX], [Tpx])
                yield
                _tt(C, "dve", XT[:, :], px, XT[:, :], ADD, [Tpx, TX], [TX])
                P, TP = Pn, TPn
                if s_ < 6:
                    PT, TPT = PTn, TPTn
                yield
            vb, Tvb = R["vb"].get()
            _ts(C, "pool", vb[:, :], vtm[:, cs], tb['beta'][0][:, ch:ch + 1], None, MUL, None, [Tvtm, tb['beta'][1]], [Tvb])
            kbg, Tkbg = R["kbg"].get()
            _ts(C, "pool", kbg[:, :], ktm[:, cs], tb['bg'][0][:, ch:ch + 1], None, MUL, None, [Tktm, tb['bg'][1]], [Tkbg])
            pa, Tpa = PS.get()
            _mm(C, pa, kf[:, cs], qf[:, cs], [Tk, Tq], [Tpa])
            yield
            pu, Tpu = PS.get()
            _mm(C, pu, XT[:, :], vb[:, :], [TX, Tvb], [Tpu])
            pw, Tpw = PS.get()
            _mm(C, pw, kbg[:, :], XT[:, :], [Tkbg, TX], [Tpw])
            attnT, TaT = R["attnT"].get()
            _tt(C, "dve", attnT[:, :], pa, DTm[:, :], MUL, [Tpa, TDT], [TaT])
            qdT, TqdT = R["qdT"].get()
            _tt(C, "pool", qdT[:, :], qf[:, cs], EG[:, :], MUL, [Tq, TEG], [TqdT])
            kdec, Tkd = R["kdec"].get()
            _ts(C, "pool", kdec[:, :], ktm[:, cs], tb['kd'][0][:, ch:ch + 1], None, MUL, None, [Tktm, tb['kd'][1]], [Tkd])
            yield
            u, Tu = R["u"].get()
            _cp(C, "act", u[:, :], pu, [Tpu], [Tu])
            wT, TwT = R["wT"].get()
            _cp(C, "dve", wT[:, :], pw, [Tpw], [TwT])
            m.update(u=(u, Tu), wT=(wT, TwT), attnT=(attnT, TaT), qdT=(qdT, TqdT), kdec=(kdec, Tkd))

        def seq(d, ch, m):
            tb = tabs[d]
            St, TS = Sst[d]
            ob, Tob = obs[d]
            cs = slice(ch * 128, (ch + 1) * 128)
            p1, Tp1 = PS.get()
            _mm(C, p1, m['wT'][0][:, :], St[:, :], [m['wT'][1], TS], [Tp1])
            p2, Tp2 = PS.get()
            _mm(C, p2, m['qdT'][0][:, :], St[:, :], [m['qdT'][1], TS], [Tp2], start=True, stop=False)
            yield
            vn, Tvn = R["vnew"].get()
            _tt(C, "dve", vn[:, :], m['u'][0][:, :], p1, SUB, [m['u'][1], Tp1], [Tvn])
            yield
            _mm(C, p2, m['attnT'][0][:, :], vn[:, :], [m['attnT'][1], Tvn], [Tp2], start=False, stop=True)
            p3, Tp3 = PS.get()
            _mm(C, p3, m['kdec'][0][:, :], vn[:, :], [m['kdec'][1], Tvn], [Tp3])
            yield
            _stt(C, St[:, :], St[:, :], tb['el'][0][:, ch:ch + 1], p3, MUL, ADD, [TS, tb['el'][1], Tp3], [TS])
            _cp(C, "act", ob[:, cs], p2, [Tp2], [Tob])

        order = [[0, 1] + list(range(2, NCH)), [1, 0] + list(range(NCH - 1, 1, -1))]
        pend = [None, None]
        for step in range(NCH + 1):
            cur = [None, None]
            gens = []
            if step < NCH:
                for d in range(2):
                    cur[d] = (order[d][step], {})
                    gens.append(pre(d, cur[d][0], cur[d][1]))
            if step > 0:
                for d in range(2):
                    gens.append(seq(d, pend[d][0], pend[d][1]))
            lockstep(gens)
            pend = cur
        S.counting = False
        _tt(C, "pool", obs[0][0][:, :], obs[0][0][:, :], obs[1][0][:, :], ADD, [obs[0][1], obs[1][1]], [obs[0][1]])
        store("oa", b, obs[0][0], obs[0][1])

    R.update({k: ringsb(4) for k in ("CBs",)})
    R.update({k: ringsb(5) for k in ("MT", "CdT")})
    R.update({k: ringsb(6, (128, 64)) for k in ("xdt", "xw")})

    def ssd(b):
        raw, Traw = slabs[0]
        xf, Txf = slabs[1]
        Bf, TBf = slabs[2]
        Cf, TCf = slabs[3]
        xtm, Txtm = slabs[4]
        Btm, TBtm = slabs[5]
        ybs = [slabs[6], slabs[7]]
        S.dma("sp", lambda e: e.dma_start(out=gt[:], in_=gates[:, b * NCH:(b + 1) * NCH, :]), writes=[Tg])
        for i, (nm, (dst, Td)) in enumerate((("sx", slabs[1]), ("sB", slabs[2]), ("sC", slabs[3]))):
            load(nm, b, raw, Traw)
            conv_silu(raw, Traw, dst, Td, pv[:, 9 + 3 * i:12 + 3 * i], bias_ap=pv[:, 22 + i:23 + i])
        to_tm(xf, Txf, xtm, Txtm)
        to_tm(Bf, TBf, Btm, TBtm)
        nA, TnA = sbc([128, 4], "ssd_nA")
        _act(C, nA[:, :], pv[:, 25:29], AF.Exp, [Tpv], [TnA])
        _ts(C, "dve", nA[:, :], nA[:, :], -1.0, None, MUL, None, [TnA], [TnA])
        tabs = {}
        for d in range(2):
            for hh in range(2):
                k = 2 * d + hh
                dt_, Tdt = tab(f"ssd_dt{k}")
                _act(C, dt_[:, :], gt[:, :, 4 + k], AF.Exp, [Tg, Tpv], [Tdt], bias=pv[:, 29 + k:30 + k])
                _act(C, dt_[:, :], dt_[:, :], AF.Ln, [Tdt, T1], [Tdt], bias=onec[:, 0:1])
                a_, Ta = tab(f"ssd_a{k}")
                _ts(C, "dve", a_[:, :], dt_[:, :], nA[:, k:k + 1], None, MUL, None, [Tdt, TnA], [Ta])
                tb = cum_tabs(a_[:, :], Ta, d, f"ssd{k}")
                tb['dt'] = (dt_, Tdt)
                tabs[(d, hh)] = tb
        hst = [sbc([128, 128], f"ssdH{d}") for d in range(2)]
        for d in range(2):
            S.op("pool", lambda e, d=d: e.memset(hst[d][0][:], 0.0), writes=[hst[d][1]])

        def step(d, ch):
            cs = slice(ch * 128, (ch + 1) * 128)
            H, TH = hst[d]
            yb, Tyb = ybs[d]
            pcb, Tpcb = PS.get()
            _mm(C, pcb, Bf[:, cs], Cf[:, cs], [TBf, TCf], [Tpcb])
            gms = [{}, {}]
            yield from grow_mats(tabs[(d, 0)], ch, d, False, gms[0])
            CBs, TCB = R["CBs"].get()
            _cp(C, "act", CBs[:, :], pcb, [Tpcb], [TCB])
            yield from grow_mats(tabs[(d, 1)], ch, d, False, gms[1])
            hd = []
            for hh in range(2):
                tb = tabs[(d, hh)]
                DTm, TDT = gms[hh]['DTm']
                EG, TEG = gms[hh]['EG']
                MT, TMT = R["MT"].get()
                _tt(C, "pool", MT[:, :], CBs[:, :], DTm[:, :], MUL, [TCB, TDT], [TMT])
                CdT, TCd = R["CdT"].get()
                _tt(C, "pool", CdT[:, :], Cf[:, cs], EG[:, :], MUL, [TCf, TEG], [TCd])
                xdt, Txd = R["xdt"].get()
                _ts(C, "dve", xdt[:, :], xtm[:, ch * 128 + hh * 64:ch * 128 + hh * 64 + 64], tb['dt'][0][:, ch:ch + 1], None, MUL, None,
                    [Txtm, tb['dt'][1]], [Txd])
                xw, Txw = R["xw"].get()
                _ts(C, "dve", xw[:, :], xdt[:, :], tb['kd'][0][:, ch:ch + 1], None, MUL, None, [Txd, tb['kd'][1]], [Txw])
                hd.append((MT, TMT, CdT, TCd, xdt, Txd, xw, Txw, tb))
            yield
            py, Tpy = PS.get()
            for hh in range(2):
                MT, TMT, CdT, TCd, xdt, Txd, xw, Txw, tb = hd[hh]
                hc = slice(hh * 64, hh * 64 + 64)
                _mm(C, py[:, hc], MT[:, :], xdt[:, :], [TMT, Txd], [Tpy], start=True, stop=False)
                _mm(C, py[:, hc], CdT[:, :], H[:, hc], [TCd, TH], [Tpy], start=False, stop=True)
            ph, Tph = PS.get()
            for hh in range(2):
                xw, Txw = hd[hh][6], hd[hh][7]
                hc = slice(hh * 64, hh * 64 + 64)
                _mm(C, ph[:, hc], Btm[:, cs], xw[:, :], [TBtm, Txw], [Tph])
            yield
            _cp(C, "act", yb[:, cs], py, [Tpy], [Tyb])
            for hh in range(2):
                tb = hd[hh][8]
                hc = slice(hh * 64, hh * 64 + 64)
                _stt(C, H[:, hc], H[:, hc], tb['el'][0][:, ch:ch + 1], ph[:, hc], MUL, ADD, [TH, tb['el'][1], Tph], [TH])

        order = [[0, 1] + list(range(2, NCH)), [1, 0] + list(range(NCH - 1, 1, -1))]
        for st_ in range(NCH):
            lockstep([step(d, order[d][st_]) for d in range(2)])
        yv = ybs[0][0][:, :].rearrange("p (c h q) -> p c h q", h=2, q=64)
        xv = xtm[:, :].rearrange("p (c h q) -> p c h q", h=2, q=64)
        for hh in range(2):
            _stt(C, yv[:, :, hh, :], xv[:, :, hh, :], pv[:, 33 + hh:34 + hh], yv[:, :, hh, :], MUL, ADD, [Txtm, Tpv, ybs[0][1]], [ybs[0][1]])
        _tt(C, "pool", ybs[0][0][:, :], ybs[0][0][:, :], ybs[1][0][:, :], ADD, [ybs[0][1], ybs[1][1]], [ybs[0][1]])
        store("ob", b, ybs[0][0], ybs[0][1])

    amask, Tam = C.sb([128, 384], F32, "amask")
    _cp(C, "pool", amask[:, 0:128], cst[:, CUP:CUP + 128], [Tc], [Tam])
    S.op("pool", lambda e: e.memset(amask[:, 128:256], 0.0), writes=[Tam])
    _cp(C, "pool", amask[:, 256:384], cst[:, CLO:CLO + 128], [Tc], [Tam])
    nsk, Tnsk = C.sb([128, 1], F32, "nsink")
    _ts(C, "dve", nsk[:, :], pv[:, 35:36], -1.0, None, MUL, None, [Tpv], [Tnsk])
    csr = ringsb(2, (128, 512))
    snr = ringsb(2, (128, 512))
    rtmp = sqr
    scr_ = ringsb(2, (128, 640))
    pTr = ringsb(6)
    sm = Ring([tuple(C.sb([128, 1], F32) for _ in range(5)) for _ in range(3)])
    SCALE = 128 ** -0.5

    def attn(b):
        qf, Tq = slabs[1]
        kf, Tk = slabs[2]
        vf, Tv = slabs[3]
        vtm, Tvtm = slabs[4]
        ob, Tob = slabs[5]
        load("aq", b, qf, Tq)
        load("ak", b, kf, Tk)
        load("av", b, vf, Tv)
        to_tm(vf, Tv, vtm, Tvtm)
        for p0 in range(0, 4096, 512):
            cs_, Tcs = csr.get()
            sn_, Tsn = snr.get()
            S.dma("sp", lambda e, cs_=cs_, p0=p0: e.dma_start(out=cs_[:, :], in_=cosd[:, p0:p0 + 512]), writes=[Tcs])
            S.dma("sp", lambda e, sn_=sn_, p0=p0: e.dma_start(out=sn_[:, :], in_=sind[:, p0:p0 + 512]), writes=[Tsn])
            for (x, Tx) in ((qf, Tq), (kf, Tk)):
                xs = x[:, 256 + p0:256 + p0 + 512]
                pr, Tpr = wide.get()
                _mm(C, pr[:, :], cst[:, CRM:CRM + 128], xs, [Tc, Tx], [Tpr])
                tm_, Ttm = rtmp.get()
                _tt(C, "dve", tm_[:, :], pr[:, :], sn_[:, :], MUL, [Tpr, Tsn], [Ttm])
                _tt(C, "pool", xs, xs, cs_[:, :], MUL, [Tx, Tcs], [Tx])
                _tt(C, "dve", xs, xs, tm_[:, :], ADD, [Tx, Ttm], [Tx])
        for qc in range(NCH):
            qcs = slice(qc * 128, (qc + 1) * 128)
            sc, Tsc = scr_.get()
            mx, nm, rs, es, rd = [t for t in sm.get()]
            if qc < 2:
                W = 0
                kchunks = [0, 1]
            else:
                n = qc - 2
                lo, hi = max(n - 1, 0), min(n + 1, 31)
                W = (hi - lo + 1) * 128
                m0 = 0 if lo == n - 1 else 128
                pa, Tpa = wide.get()
                _mm(C, pa[:, :W], qf[:, qcs], kf[:, 256 + lo * 128:256 + lo * 128 + W], [Tq, Tk], [Tpa])
                _tt(C, "dve", sc[:, :W], pa[:, :W], amask[:, m0:m0 + W], ADD, [Tpa, Tam], [Tsc])
                kchunks = [2 + lo + i for i in range(hi - lo + 1)] + [0, 1]
            pb_, Tpb = wide.get()
            _mm(C, pb_[:, :256], qf[:, qcs], kf[:, 0:256], [Tq, Tk], [Tpb])
            _cp(C, "act", sc[:, W:W + 256], pb_[:, :256], [Tpb], [Tsc])
            WT = W + 256
            S.op("dve", lambda e, sc=sc, mx=mx, WT=WT: e.tensor_reduce(out=mx[0][:, :], in_=sc[:, :WT], axis=AX.X, op=MAX), reads=[Tsc], writes=[mx[1]])
            _ts(C, "dve", nm[0][:, :], mx[0][:, :], -SCALE, nsk[:, 0:1], MUL, MIN, [mx[1], Tnsk], [nm[1]])
            _act(C, sc[:, :WT], sc[:, :WT], AF.Exp, [Tsc, nm[1]], [Tsc, rs[1]], scale=SCALE, bias=nm[0][:, 0:1], accum=rs[0][:, 0:1])
            _act(C, es[0][:, :], pv[:, 35:36], AF.Exp, [Tpv, nm[1]], [es[1]], bias=nm[0][:, 0:1])
            _tt(C, "dve", rd[0][:, :], rs[0][:, :], es[0][:, :], ADD, [rs[1], es[1]], [rd[1]])
            S.op("dve", lambda e, rd=rd: e.reciprocal(out=rd[0][:, :], in_=rd[0][:, :]), reads=[rd[1]], writes=[rd[1]])
            po, Tpo = PS.get()
            nk = len(kchunks)
            pts = []
            for i in range(nk):
                pt, Tpt = PS.get()
                _tr(C, pt, sc[:, i * 128:(i + 1) * 128], ident, [Tsc, Tc], [Tpt])
                pT, TpT = pTr.get()
                _cp(C, "act" if i % 2 else "dve", pT[:, :], pt, [Tpt], [TpT])
                pts.append((pT, TpT))
            for i, kc in enumerate(kchunks):
                pT, TpT = pts[i]
                _mm(C, po, pT[:, :], vtm[:, kc * 128:(kc + 1) * 128], [TpT, Tvtm], [Tpo], start=(i == 0), stop=(i == nk - 1))
            _act(C, ob[:, qcs], po, AF.Identity, [Tpo, rd[1]], [Tob], scale=rd[0][:, 0:1])
        store("oc", b, ob, Tob)

    extra_fns = dict(ssd=ssd, attn=attn)
    fns = dict(gdn=gdn)
    fns.update(extra_fns)
    for nm in which:
        for b in batches:
            fns[nm](b)
    return C.finish([t for _, t in slabs[1:]])


NT_C = 1092
NGATE = 8192
DFF = 5632
FC = 44
P1_TILES = [(0, 342, [(0, 342, 0)]), (342, 342, [(0, 342, 0)]), (684, 408, [(0, 342, 0), (342, 408, 1)])]
P2_TILES = [(1, 343, [(0, 342, 0, 0)]), (343, 685, [(0, 342, 0, 342)]), (685, 1091, [(0, 340, 0, 684), (342, 406, 1, 1024)])]
NW = 410


def build_C():
    C = Ctx()
    nc, S = C.nc, C.S
    xT = C.dram_in("xT", [D, NT_C])
    oT = C.dram_in("oT", [3072, NT_C])
    hmd = C.dram_in("hmask", [128, NT_C])
    vecs = C.dram_in("vecs", [128, 16, 16])
    pvd = C.dram_in("pvc", [128, 185])
    wg = C.dram_in("wg", [D, NGATE])
    wbr = C.dram_in("wbr", [3072, D])
    wout = C.dram_in("wout", [D, D])
    wup = C.dram_in("wup", [D, 2 * DFF])
    wdn = C.dram_in("wdn", [DFF, D])
    xo = C.dram_out("xoT", [D, 1088])
    ones, To = consts(C)
    scr = make_scr(C)
    vt, Tv = C.sb([128, 16, 16], F32, "vecs_sb")
    S.dma("sp", lambda e: e.dma_start(out=vt[:], in_=vecs[:, :, :]), writes=[Tv])
    pv, Tpv = C.sb([128, 185], F32, "pvc_sb")
    S.dma("sp", lambda e: e.dma_start(out=pv[:], in_=pvd[:, :]), writes=[Tpv])
    hm, Thm = C.sb([128, NT_C], F32, "hm_sb")
    S.dma("sp", lambda e: e.dma_start(out=hm[:], in_=hmd[:, :]), writes=[Thm])
    ef, Tef = C.sb([128, 2, 4, 16], F32, "eff")
    for s in range(2):
        b0 = 4 + 6 * s
        S.op("dve", lambda e, s=s, b0=b0: e.scalar_tensor_tensor(out=ef[:, s, 0, :], in0=vt[:, b0 + 1, :], scalar=1.0, in1=vt[:, 0, :], op0=ADD, op1=MUL),
             reads=[Tv], writes=[Tef])
        _tt(C, "dve", ef[:, s, 1, :], vt[:, b0 + 2, :], vt[:, 1, :], MUL, [Tv], [Tef])
        S.op("dve", lambda e, s=s, b0=b0: e.scalar_tensor_tensor(out=ef[:, s, 2, :], in0=vt[:, b0 + 4, :], scalar=1.0, in1=vt[:, 2, :], op0=ADD, op1=MUL),
             reads=[Tv], writes=[Tef])
        _tt(C, "dve", ef[:, s, 3, :], vt[:, b0 + 5, :], vt[:, 3, :], MUL, [Tv], [Tef])
    xs, Txs = C.sb([128, KC, NT_C], F32, "xres")
    xv = xT.rearrange("(c p) t -> p c t", p=128)
    for c0 in range(0, KC, 4):
        S.dma("sp", lambda e, c0=c0: e.dma_start(out=xs[:, c0:c0 + 4, :], in_=xv[:, c0:c0 + 4, :]), writes=[Txs])
    A16, TA16 = C.sb([128, KC, NW], BF16, "A16")
    ACT_, TACT = C.sb([128, FC, NW], BF16, "ACTb")
    F32A, TF32 = C.sb([128, KC, NW], F32, "F32A")
    M16, TM16 = A16, TA16
    wsl = Ring([C.sb([128, KC, 512], BF16) for _ in range(2)])
    pss, Tps = C.ps(name="pss")
    PB = Ring([C.ps() for _ in range(6)])
    st_in = Ring([C.sb([128, NW], F32) for _ in range(2)])
    st4 = [C.sb([128, NW], F32) for _ in range(4)]
    tmpr = Ring([C.sb([128, NW], F32) for _ in range(3)])
    rsm, Trsm = C.sb([128, 512], F32, "rsm")
    ov = oT.rearrange("(c p) t -> p c t", p=128)

    def wload(dview, c0, cw, kcn):
        w, Tw = wsl.get()
        S.dma("pool", lambda e: e.dma_start(out=w[:, :kcn, :cw], in_=dview[:, :, c0:c0 + cw]), writes=[Tw])
        return w, Tw

    wgv = wg.rearrange("(c p) n -> p c n", p=128)
    wbv = [wbr[br * 1024:(br + 1) * 1024, :].rearrange("(c p) n -> p c n", p=128) for br in range(3)]
    wov = wout.rearrange("(c p) n -> p c n", p=128)
    wuv = wup.rearrange("(c p) n -> p c n", p=128)
    wdv = [wdn[fg * 11 * 128:(fg + 1) * 11 * 128, :].rearrange("(c p) n -> p c n", p=128) for fg in range(4)]

    def stats(src_fn, Tsrc, nchunks, n, div):
        for c in range(nchunks):
            sq, Tsq = scr['sq'][c % 2]
            _act(C, sq[:, :n], src_fn(c), AF.Square, [Tsrc], [Tsq])
            _mm(C, pss[:, :n], ones[:], sq[:, :n], [Tsq, To], [Tps], start=(c == 0), stop=(c == nchunks - 1))
        _act(C, rsm[:, :n], pss[:, :n], AF.Sqrt, [Tps, scr['eps'][1]], [Trsm], scale=1.0 / div, bias=scr['eps'][0][:, 0:1])
        S.op("dve", lambda e: e.reciprocal(out=rsm[:, :n], in_=rsm[:, :n]), reads=[Trsm], writes=[Trsm])

    def gemm_chunk(w, Tw, wc, kcn, rhs_fn, Trhs, n):
        pb, Tp = PB.get()
        for k in range(kcn):
            _mm(C, pb[:, :n], w[:, k, wc:wc + 128], rhs_fn(k), [Tw, Trhs], [Tp], start=(k == 0), stop=(k == kcn - 1))
        return pb, Tp

    for (t0, n, segs1) in P1_TILES:
        norm_mod2(C, xs[:, :, t0:t0 + n], Txs, n, ones, To, [(c0, c1, ef[:, vs, 0, :], vt[:, 4 + 6 * vs, :]) for (c0, c1, vs) in segs1],
                  Tef, A16, TA16, pss, Tps, scr)
        hfn = lambda k: A16[:, k, :n]
        for sgrp in range(2):
            w, Tw = wload(wgv, sgrp * 512, 512, KC)
            for cc in range(4):
                c = sgrp * 4 + cc
                oin, Toin = st_in.get()
                S.dma("sp", lambda e, oin=oin, c=c, t0=t0, n=n: e.dma_start(out=oin[:, :n], in_=ov[:, c, t0:t0 + n]), writes=[Toin])
                stats(lambda _c, oin=oin: oin[:, :n], Toin, 1, n, 128.0)
                pb, Tp = gemm_chunk(w, Tw, cc * 128, KC, hfn, TA16, n)
                sg, Tsg = tmpr.get()
                _act(C, sg[:, :n], pb[:, :n], AF.Silu, [Tp], [Tsg])
                _tt(C, "dve", oin[:, :n], oin[:, :n], rsm[:, :n], MUL, [Toin, Trsm], [Toin])
                _stt(C, ACT_[:, c, :n], oin[:, :n], pv[:, 0:1], sg[:, :n], MUL, MUL, [Toin, Tpv, Tsg], [TACT])
        for sgrp in range(2):
            w, Tw = wload(wgv, 1024 + sgrp * 512, 512, KC)
            for cc in range(4):
                c = sgrp * 4 + cc
                oin, Toin = st4[cc]
                S.dma("sp", lambda e, oin=oin, c=c, t0=t0, n=n: e.dma_start(out=oin[:, :n], in_=ov[:, 8 + c, t0:t0 + n]), writes=[Toin])
                pb, Tp = gemm_chunk(w, Tw, cc * 128, KC, hfn, TA16, n)
                sg, Tsg = tmpr.get()
                _act(C, sg[:, :n], pb[:, :n], AF.Silu, [Tp], [Tsg])
                _tt(C, "dve", oin[:, :n], oin[:, :n], sg[:, :n], MUL, [Toin, Tsg], [Toin])
            for c_ in range(4):
                sq, Tsq = scr['sq'][c_ % 2]
                _act(C, sq[:, :n], st4[c_][0][:, :n], AF.Square, [st4[c_][1]], [Tsq])
                _mm(C, pss[:, :n], ones[:], sq[:, :n], [Tsq, To], [Tps], start=(c_ == 0), stop=(c_ == 3))
            _act(C, rsm[:, :n], pss[:, :n], AF.Sqrt, [Tps, scr['eps'][1]], [Trsm], scale=1.0 / 512.0, bias=scr['eps'][0][:, 0:1])
            S.op("dve", lambda e, n=n: e.reciprocal(out=rsm[:, :n], in_=rsm[:, :n]), reads=[Trsm], writes=[Trsm])
            for cc in range(4):
                c = sgrp * 4 + cc
                oin, Toin = st4[cc]
                _tt(C, "dve", oin[:, :n], oin[:, :n], rsm[:, :n], MUL, [Toin, Trsm], [Toin])
                _ts(C, "dve", ACT_[:, 8 + c, :n], oin[:, :n], pv[:, 1 + c:2 + c], None, MUL, None, [Toin, Tpv], [TACT])
        for c in range(8):
            oin, Toin = st_in.get()
            S.dma("sp", lambda e, oin=oin, c=c, t0=t0, n=n: e.dma_start(out=oin[:, :n], in_=ov[:, 16 + c, t0:t0 + n]), writes=[Toin])
            _cp(C, "act", ACT_[:, 16 + c, :n], oin[:, :n], [Toin], [TACT])
        for og in range(4):
            for br in range(3):
                wb_, Twb = wload(wbv[br], og * 512, 512, 8)
                wm_, Twm = wload(wgv, 2048 + br * 2048 + og * 512, 512, KC)
                for oc in range(4):
                    o = og * 4 + oc
                    p1, Tp1 = gemm_chunk(wb_, Twb, oc * 128, 8, lambda k, br=br: ACT_[:, br * 8 + k, :n], TACT, n)
                    p2, Tp2 = gemm_chunk(wm_, Twm, oc * 128, KC, hfn, TA16, n)
                    gt_, Tgt = tmpr.get()
                    _act(C, gt_[:, :n], p2[:, :n], AF.Sigmoid, [Tp2], [Tgt])
                    if br == 0:
                        _tt(C, "dve", F32A[:, o, :n], p1[:, :n], gt_[:, :n], MUL, [Tp1, Tgt], [TF32])
                    else:
                        _tt(C, "dve", gt_[:, :n], p1[:, :n], gt_[:, :n], MUL, [Tp1, Tgt], [Tgt])
                        _tt(C, "dve", F32A[:, o, :n], F32A[:, o, :n], gt_[:, :n], ADD, [TF32, Tgt], [TF32])
        for o in range(KC):
            _cp(C, "act", M16[:, o, :n], F32A[:, o, :n], [TF32], [TM16])
        for og in range(4):
            w, Tw = wload(wov, og * 512, 512, KC)
            for oc in range(4):
                o = og * 4 + oc
                pb, Tp = gemm_chunk(w, Tw, oc * 128, KC, lambda k: M16[:, k, :n], TM16, n)
                _cp(C, "act" if o % 2 else "dve", F32A[:, o, :n], pb[:, :n], [Tp], [TF32])
        stats(lambda c: F32A[:, c, :n], TF32, KC, n, float(D))
        for c in range(KC):
            tm_, Ttm = tmpr.get()
            _tt(C, "dve", tm_[:, :n], F32A[:, c, :n], rsm[:, :n], MUL, [TF32, Trsm], [Ttm])
            for (c0, c1, vs) in segs1:
                _stt(C, xs[:, c, t0 + c0:t0 + c1], tm_[:, c0:c1], ef[:, vs, 1, c:c + 1], xs[:, c, t0 + c0:t0 + c1], MUL, ADD, [Ttm, Tef, Txs], [Txs])

    cw_ = pv[:, 9:9 + 132]
    for (a, b, segs2) in P2_TILES:
        n2 = b - a + 2
        ni = b - a
        nsegs = [(i0, i1 + 2, ef[:, vs, 2, :], vt[:, 4 + 6 * vs + 3, :]) for (i0, i1, vs, _o) in segs2]
        norm_mod2(C, xs[:, :, a - 1:b + 1], Txs, n2, ones, To, nsegs, Tef, A16, TA16, pss, Tps, scr)
        hfn = lambda k: A16[:, k, :n2]
        for fg in range(11):
            wu, Twu = wload(wuv, fg * 512, 512, KC)
            wgt, Twgt = wload(wuv, DFF + fg * 512, 512, KC)
            for fc in range(4):
                f = fg * 4 + fc
                pu, Tpu = gemm_chunk(wu, Twu, fc * 128, KC, hfn, TA16, n2)
                pg, Tpg = gemm_chunk(wgt, Twgt, fc * 128, KC, hfn, TA16, n2)
                gm, Tgm = tmpr.get()
                _tt(C, "dve", gm[:, :n2], pg[:, :n2], hm[:, a - 1:b + 1], MUL, [Tpg, Thm], [Tgm])
                gc, Tgc = tmpr.get()
                _act(C, gc[:, :ni], gm[:, 1:1 + ni], AF.Identity, [Tgm, Tpv], [Tgc], scale=pv[:, 9 + 3 * f + 1:9 + 3 * f + 2], bias=pv[:, 141 + f:142 + f])
                _stt(C, gc[:, :ni], gm[:, 0:ni], pv[:, 9 + 3 * f:9 + 3 * f + 1], gc[:, :ni], MUL, ADD, [Tgm, Tpv, Tgc], [Tgc])
                _stt(C, gc[:, :ni], gm[:, 2:2 + ni], pv[:, 9 + 3 * f + 2:9 + 3 * f + 3], gc[:, :ni], MUL, ADD, [Tgm, Tpv, Tgc], [Tgc])
                _act(C, gc[:, :ni], gc[:, :ni], AF.Silu, [Tgc], [Tgc])
                _tt(C, "dve", ACT_[:, f, :ni], pu[:, 1:1 + ni], gc[:, :ni], MUL, [Tpu, Tgc], [TACT])
        for og in range(4):
            pbs = [PB.get() for _ in range(4)]
            for fg in range(4):
                w, Tw = wload(wdv[fg], og * 512, 512, 11)
                for oc in range(4):
                    pb, Tp = pbs[oc]
                    for k in range(11):
                        _mm(C, pb[:, :ni], w[:, k, oc * 128:(oc + 1) * 128], ACT_[:, fg * 11 + k, :ni], [Tw, TACT], [Tp],
                            start=(fg == 0 and k == 0), stop=(fg == 3 and k == 10))
            for oc in range(4):
                o = og * 4 + oc
                _cp(C, "act" if o % 2 else "dve", F32A[:, o, :ni], pbs[oc][0][:, :ni], [pbs[oc][1]], [TF32])
        stats(lambda c: F32A[:, c, :ni], TF32, KC, ni, float(D))
        for c in range(KC):
            _tt(C, "dve", F32A[:, c, :ni], F32A[:, c, :ni], rsm[:, :ni], MUL, [TF32, Trsm], [TF32])
            for (i0, i1, vs, _o) in segs2:
                _stt(C, F32A[:, c, i0:i1], F32A[:, c, i0:i1], ef[:, vs, 3, c:c + 1], xs[:, c, a + i0:a + i1], MUL, ADD, [TF32, Tef, Txs], [TF32])
        xov = xo.rearrange("(c p) t -> p c t", p=128)
        for (i0, i1, vs, oc0) in segs2:
            for c0 in range(0, KC, 4):
                S.dma("sp", lambda e, c0=c0, i0=i0, i1=i1, oc0=oc0: e.dma_start(out=xov[:, c0:c0 + 4, oc0:oc0 + (i1 - i0)], in_=F32A[:, c0:c0 + 4, i0:i1]), reads=[TF32])
    return C.finish([TF32])


MCOLS = 1536


def build_M():
    C = Ctx()
    nc, S = C.nc, C.S
    wada = C.dram_in("wada", [4 * D, MCOLS])
    bada = C.dram_in("bada", [128, 48])
    cT = C.dram_in("cT", [128, 16, 3])
    out = C.dram_out("modT", [128, 144])
    ct, Tct = C.sb([128, 16, 3], F32, "ct_sb")
    S.dma("sp", lambda e: e.dma_start(out=ct[:], in_=cT[:, :, :]), writes=[Tct])
    bt, Tbt = C.sb([128, 48], F32, "bt_sb")
    S.dma("sp", lambda e: e.dma_start(out=bt[:], in_=bada[:, :]), writes=[Tbt])
    _act(C, ct[:, :, :], ct[:, :, :], AF.Silu, [Tct], [Tct])
    ot, Tot = C.sb([128, 144], F32, "ot_sb")
    wb = Ring([C.sb([128, KC, 768], F32) for _ in range(2)])
    PB = Ring([C.ps() for _ in range(4)])
    for l in range(4):
        wv = wada[l * D:(l + 1) * D, :].rearrange("(c p) n -> p c n", p=128)
        for hf in range(2):
            w, Tw = wb.get()
            for c0 in range(0, KC, 4):
                S.dma("sp", lambda e, w=w, wv=wv, hf=hf, c0=c0: e.dma_start(out=w[:, c0:c0 + 4, :], in_=wv[:, c0:c0 + 4, hf * 768:(hf + 1) * 768]), writes=[Tw])
            for jj in range(6):
                j = hf * 6 + jj
                pb, Tp = PB.get()
                for k in range(KC):
                    _mm(C, pb[:, :3], w[:, k, jj * 128:(jj + 1) * 128], ct[:, k, :], [Tw, Tct], [Tp], start=(k == 0), stop=(k == KC - 1))
                col = (l * 12 + j)
                _act(C, ot[:, col * 3:col * 3 + 3], pb[:, :3], AF.Identity, [Tp, Tbt], [Tot], bias=bt[:, col:col + 1])
    S.dma("sp", lambda e: e.dma_start(out=out[:, :], in_=ot[:, :]), reads=[Tot])
    return C.finish([Tot])


IN_SPLITS = (3072, 1024, 16, 16, 1024, 1536, 32, 1024, 512, 6144)
OFFS = np.cumsum((0,) + IN_SPLITS)
MIX_PARTS = (0, 2, 3, 5, 6, 7, 8)
MIXCOLS = np.concatenate([np.arange(OFFS[i], OFFS[i + 1]) for i in MIX_PARTS])
_mo = np.cumsum([0] + [IN_SPLITS[i] for i in MIX_PARTS])
M_QKV, M_A, M_B, M_XBC, M_DT, M_Q, M_KV = [int(v) for v in _mo[:7]]
GATECOLS = np.concatenate([np.arange(OFFS[i], OFFS[i + 1]) for i in (1, 4, 9)])
def b_inputs(PT, j, prm):
    g = j // 4
    sl = lambda r0: np.ascontiguousarray(PT[r0:r0 + 128])
    d = {}
    d["gq"] = sl(M_QKV + j * 128)
    d["gk"] = sl(M_QKV + 1024 + j * 128)
    d["gv"] = sl(M_QKV + 2048 + j * 128)
    d["sx"] = sl(M_XBC + j * 128)
    d["sB"] = sl(M_XBC + 1024 + g * 128)
    d["sC"] = sl(M_XBC + 1024 + 256 + g * 128)
    d["aq"] = sl(M_Q + j * 128)
    d["ak"] = sl(M_KV + g * 128)
    d["av"] = sl(M_KV + 256 + g * 128)
    rows = [M_A + j, M_A + 8 + j, M_B + j, M_B + 8 + j,
            M_DT + 2 * j, M_DT + 2 * j + 1, M_DT + 16 + 2 * j, M_DT + 16 + 2 * j + 1]
    gt = PT[rows]
    d["gates"] = np.ascontiguousarray(gt.reshape(8, 68, 128).transpose(2, 1, 0))
    pv = np.zeros((128, 40), np.float32)
    cw = prm["gdn_conv_w"]
    for i in range(3):
        pv[:, 3 * i:3 * i + 3] = cw[:, i * 1024 + j * 128:i * 1024 + (j + 1) * 128].T
    sw = prm["ssm_conv_w"]
    sb = prm["ssm_conv_b"]
    for i, c0 in enumerate((j * 128, 1024 + g * 128, 1280 + g * 128)):
        pv[:, 9 + 3 * i:12 + 3 * i] = sw[:, c0:c0 + 128].T
        pv[:, 22 + i] = sb[c0:c0 + 128]
    pv[:, 18] = prm["gdn_a_log"][0, j]; pv[:, 19] = prm["gdn_a_log"][1, j]
    pv[:, 20] = prm["gdn_dt_bias"][0, j]; pv[:, 21] = prm["gdn_dt_bias"][1, j]
    k = 0
    for dd in range(2):
        for hh in range(2):
            pv[:, 25 + k] = prm["ssm_a_log"][dd, 2 * j + hh]
            pv[:, 29 + k] = prm["ssm_dt_bias"][dd, 2 * j + hh]
            k += 1
    pv[:, 33] = prm["ssm_d"][2 * j]; pv[:, 34] = prm["ssm_d"][2 * j + 1]
    pv[:, 35] = prm["attn_sink"][j]
    d["pv"] = pv
    return d


def vec16(v):
    return np.ascontiguousarray(np.asarray(v, np.float32).reshape(16, 128).T)


def halo_slice(arr, lo, hi):
    T = arr.shape[0]
    out = np.zeros((hi - lo, arr.shape[1]), arr.dtype)
    m = np.zeros(hi - lo, np.float32)
    a, b = max(lo, 0), min(hi, T)
    out[a - lo:b - lo] = arr[a:b]
    m[a - lo:b - lo] = 1.0
    return out, m


def c_inputs(i, X, XC, OL, OC, mod, modc, prm):
    b, q = i // 4, i % 4
    xl, ml = halo_slice(X[b], q * 1024 - 1, (q + 1) * 1024 + 1)
    xc, mc = halo_slice(XC[b], q * 64 - 1, (q + 1) * 64 + 1)
    ol, _ = halo_slice(OL[b], q * 1024 - 1, (q + 1) * 1024 + 1)
    oc, _ = halo_slice(OC[b], q * 64 - 1, (q + 1) * 64 + 1)
    d = {}
    d["xT"] = np.ascontiguousarray(np.concatenate([xl, xc], 0).T)
    d["oT"] = np.ascontiguousarray(np.concatenate([ol, oc], 0).T)
    d["hmask"] = np.ascontiguousarray(np.broadcast_to(np.concatenate([ml, mc])[None, :], (128, 1092))).astype(np.float32)
    vs = [prm["g_pre1"], prm["g_post1"], prm["g_pre2"], prm["g_post2"]]
    for m in (mod[b], modc):
        vs += [m[k * 2048:(k + 1) * 2048] for k in range(6)]
    d["vecs"] = np.ascontiguousarray(np.stack([vec16(v) for v in vs], 1))
    pv = np.zeros((128, 185), np.float32)
    pv[:, 0] = prm["gdn_norm"]
    pv[:, 1:9] = prm["ssm_norm"].reshape(8, 128).T
    pv[:, 9:141] = prm["ffn_conv_w"].reshape(3, 44, 128).transpose(2, 1, 0).reshape(128, 132)
    pv[:, 141:185] = prm["ffn_conv_b"].reshape(44, 128).T
    d["pvc"] = pv
    return d


def c_weights(W):
    return dict(wg=np.ascontiguousarray(W["w_in"][:, GATECOLS]),
                wbr=np.ascontiguousarray(np.concatenate([W["w_branch_a"], W["w_branch_b"], W["w_branch_c"]], 0)),
                wout=W["w_out"], wup=W["w_up"], wdn=W["w_down"])


def _run(nc, in_maps):
    res = run_bass_kernel_spmd(nc, in_maps, core_ids=list(range(8)))
    return res.results


def kernel(**inp):
    inp = {k: np.asarray(v) for k, v in inp.items()}
    x, ctx, c, c_ctx = inp["x"], inp["ctx"], inp["c"], inp["c_ctx"]
    f32 = np.float32
    ncM = build_M()
    cT = np.ascontiguousarray(np.stack([c[0], c[1], c_ctx], 0).reshape(3, 16, 128).transpose(2, 1, 0)).astype(f32)
    maps = []
    for i in range(8):
        cs = slice(i * 1536, (i + 1) * 1536)
        maps.append(dict(wada=np.ascontiguousarray(inp["w_ada"][:, :, cs].reshape(4 * 2048, 1536)),
                         bada=np.ascontiguousarray(inp["b_ada"][:, cs].reshape(4, 12, 128).transpose(2, 0, 1).reshape(128, 48)),
                         cT=cT))
    res = _run(ncM, maps)
    mod_all = np.concatenate([r["modT"].reshape(128, 4, 12, 3).transpose(1, 3, 2, 0).reshape(4, 3, 1536) for r in res], -1)
    ncA, ncB, ncC = build_A(), build_B(), build_C()
    cst, cosF, sinF = b_consts()
    X = np.array(x, f32)
    XC = np.array(ctx, f32)
    pnames = ["gdn_conv_w", "gdn_a_log", "gdn_dt_bias", "gdn_norm", "ssm_conv_w", "ssm_conv_b", "ssm_a_log", "ssm_dt_bias", "ssm_d", "ssm_norm",
              "attn_sink", "g_pre1", "g_post1", "g_pre2", "g_post2", "ffn_conv_w", "ffn_conv_b", "w_in", "w_branch_a", "w_branch_b", "w_branch_c",
              "w_out", "w_up", "w_down"]
    for l in range(4):
        prm = {k: inp[k][l] for k in pnames}
        mod, modc = mod_all[l, 0:2], mod_all[l, 2]
        wmix = np.ascontiguousarray(prm["w_in"][:, MIXCOLS])
        maps = []
        for i in range(8):
            b, q = i // 4, i % 4
            xs = np.concatenate([X[b, q * 1024:(q + 1) * 1024], XC[b, q * 64:(q + 1) * 64]], 0)
            vecs = np.stack([vec16(prm["g_pre1"]), vec16(mod[b, 2048:4096]), vec16(mod[b, 0:2048]), vec16(modc[2048:4096]), vec16(modc[0:2048])], 1)
            maps.append(dict(xT=np.ascontiguousarray(xs.T), vecs=np.ascontiguousarray(vecs), wmix=wmix))
        res = _run(ncA, maps)
        PT = np.empty((6208, 8704), f32)
        for i in range(8):
            b, q = i // 4, i % 4
            PT[:, b * 4352 + 256 + q * 1024:b * 4352 + 256 + (q + 1) * 1024] = res[i]["pT"][:, :1024]
            PT[:, b * 4352 + q * 64:b * 4352 + (q + 1) * 64] = res[i]["pT"][:, 1024:]
        maps = []
        for j in range(8):
            m = b_inputs(PT, j, prm)
            m.update(cst=cst, cosF=cosF, sinF=sinF)
            maps.append(m)
        res = _run(ncB, maps)
        OL = np.empty((2, 4096, 3072), f32)
        OC = np.empty((2, 256, 3072), f32)
        for j in range(8):
            for mi, nm in enumerate(("oa", "ob", "oc")):
                o = res[j][nm]
                for b in range(2):
                    OC[b, :, mi * 1024 + j * 128:mi * 1024 + (j + 1) * 128] = o[b * 4352:b * 4352 + 256]
                    OL[b, :, mi * 1024 + j * 128:mi * 1024 + (j + 1) * 128] = o[b * 4352 + 256:(b + 1) * 4352]
        wts = c_weights(prm)
        maps = []
        for i in range(8):
            m = c_inputs(i, X, XC, OL, OC, mod, modc, prm)
            m.update(wts)
            maps.append(m)
        res = _run(ncC, maps)
        Xn = np.empty_like(X)
        XCn = np.empty_like(XC)
        for i in range(8):
            b, q = i // 4, i % 4
            xo = res[i]["xoT"]
            Xn[b, q * 1024:(q + 1) * 1024] = xo[:, :1024].T
            XCn[b, q * 64:(q + 1) * 64] = xo[:, 1024:].T
        X, XC = Xn, XCn
    return X.astype(f32)
```

```python
import numpy as np
import concourse.bass as bass
import concourse.mybir as mybir

F32 = mybir.dt.float32
BF16 = mybir.dt.bfloat16
AF = mybir.ActivationFunctionType
ALU = mybir.AluOpType
AX = mybir.AxisListType

ENGS = ("pe", "act", "dve", "pool", "sp")


class T:
    __slots__ = ("name", "last_w", "readers", "sem", "semcnt", "excl")

    def __init__(self, name):
        self.name = name
        self.last_w = None
        self.readers = []
        self.sem = None
        self.semcnt = 0
        self.excl = False


class Op:
    __slots__ = ("eng", "fn", "waits", "signal", "dma_tile", "idx", "inc")

    def __init__(self, eng, fn):
        self.eng = eng
        self.fn = fn
        self.waits = []
        self.signal = False
        self.dma_tile = None


class Sched:
    def __init__(self, same_engine_sync=True):
        self.ops = {e: [] for e in ENGS}
        self.seen = {e: {} for e in ENGS}
        self.same = same_engine_sync
        self.dma_tiles = []
        self.final_waits = []

    def _dep(self, op, d):
        if d is None:
            return
        if d[0] == 'e':
            _, eng, idx = d
            if eng == op.eng:
                if eng == "pe" or not self.same or eng == "sp":
                    return
                if idx >= len(self.ops[eng]):
                    return
            key = ('e', eng)
            cnt = idx
            self.ops[eng][idx].signal = True
        else:
            _, tile, cnt = d
            cnt = tile.semcnt
            d = ('d', tile, cnt)
            key = ('d', id(tile))
        seen = self.seen[op.eng]
        if seen.get(key, -1) >= cnt:
            return
        seen[key] = cnt
        op.waits.append(d)

    def op(self, eng, fn, reads=(), writes=()):
        o = Op(eng, fn)
        o.idx = len(self.ops[eng])
        for t in reads:
            self._dep(o, t.last_w)
            if t.excl:
                for r in t.readers:
                    if r[0] == 'e' and r[1] != eng:
                        self._dep(o, r)
        for t in writes:
            self._dep(o, t.last_w)
            for r in t.readers:
                self._dep(o, r)
        self.ops[eng].append(o)
        me = ('e', eng, o.idx)
        for t in reads:
            t.readers.append(me)
        for t in writes:
            t.last_w = me
            t.readers = []
        return o

    def dma(self, eng, fn, reads=(), writes=(), sem_tile=None, inc=16):
        o = Op(eng, fn)
        o.idx = len(self.ops[eng])
        for t in reads:
            self._dep(o, t.last_w)
        for t in writes:
            self._dep(o, t.last_w)
            for r in t.readers:
                self._dep(o, r)
        st = sem_tile or (writes[0] if writes else reads[0])
        if st.sem is None:
            st.sem = True
            self.dma_tiles.append(st)
        st.semcnt += inc
        o.dma_tile = st
        o.inc = inc
        self.ops[eng].append(o)
        me = ('d', st, st.semcnt)
        for t in reads:
            t.readers.append(me)
        for t in writes:
            t.last_w = me
            t.readers = []
        return o

    def barrier(self):
        last = {e: len(self.ops[e]) - 1 for e in ENGS if e != "sp" and self.ops[e]}
        dts = [(t, t.semcnt) for t in self.dma_tiles]
        for e in ENGS:
            o = Op(e, None)
            o.idx = len(self.ops[e])
            for pe_, idx in last.items():
                if pe_ != e:
                    self._dep(o, ('e', pe_, idx))
            for t, c in dts:
                self._dep(o, ('d', t, c))
            self.ops[e].append(o)

    def finish(self, eng, tiles):
        def nopfn(e):
            return None
        o = Op(eng, None)
        o.idx = len(self.ops[eng])
        for t in tiles:
            self._dep(o, t.last_w)
            for r in t.readers:
                self._dep(o, r)
        self.ops[eng].append(o)

    def emit(self, nc, stack):
        esem = {}
        for e in ENGS:
            if any(o.signal for o in self.ops[e]):
                esem[e] = stack.enter_context(nc.semaphore("s_" + e))
        for t in self.dma_tiles:
            t.sem = stack.enter_context(nc.semaphore("d_" + t.name))
        cnt_at = {}
        for e in ENGS:
            c = 0
            for o in self.ops[e]:
                if o.signal:
                    c += 1
                    cnt_at[(e, o.idx)] = c
        block = stack.enter_context(nc.Block())

        def run(e, handle):
            for o in self.ops[e]:
                for d in o.waits:
                    if d[0] == 'e':
                        handle.wait_ge(esem[d[1]], cnt_at[(d[1], d[2])])
                    else:
                        handle.wait_ge(d[1].sem, d[2])
                if o.fn is None:
                    continue
                ins = o.fn(handle)
                if o.dma_tile is not None:
                    ins.then_inc(o.dma_tile.sem, o.inc)
                elif o.signal:
                    ins.then_inc(esem[e], 1)

        @block.tensor
        def _(h):
            run("pe", h)

        @block.scalar
        def _(h):
            run("act", h)

        @block.vector
        def _(h):
            run("dve", h)

        @block.gpsimd
        def _(h):
            run("pool", h)

        @block.sync
        def _(h):
            run("sp", h)


from contextlib import ExitStack
import numpy as np
from concourse.bass_utils import run_bass_kernel_spmd

D = 2048
KC = 16
EPS = 1e-6


class Ctx:
    def __init__(self):
        self.nc = bass.Bass("TRN2", target_bir_lowering=False)
        self.S = Sched()
        self.st = ExitStack()
        self.n = 0

    def sb(self, shape, dt=F32, name=None):
        self.n += 1
        name = name or f"sb{self.n}"
        t = self.st.enter_context(self.nc.sbuf_tensor(name, list(shape), dt))
        return t, T(name)

    def ps(self, shape=(128, 512), dt=F32, name=None):
        self.n += 1
        name = name or f"ps{self.n}"
        t = self.st.enter_context(self.nc.psum_tensor(name, list(shape), dt))
        tt = T(name)
        tt.excl = True
        return t, tt

    def dram_in(self, name, shape, dt=F32):
        return self.nc.dram_tensor(name, list(shape), dt, kind="ExternalInput").ap()

    def dram_out(self, name, shape, dt=F32):
        return self.nc.dram_tensor(name, list(shape), dt, kind="ExternalOutput").ap()

    def finish(self, out_tiles):
        self.S.finish("sp", out_tiles)
        self.S.emit(self.nc, self.st)
        self.st.close()
        return self.nc


def consts(C):
    ones, To = C.sb([128, 128], F32, "ones")
    C.S.op("pool", lambda e: e.memset(ones[:], 1.0), writes=[To])
    return ones, To


def norm_mod(C, xt, Tx, n, ones, To, eff, sh, Tv, h, Th, pss, Tps, scr):
    S = C.S
    for c in range(KC):
        sq, Tsq = scr['sq'][c % 2]
        S.op("act", lambda e, c=c, sq=sq: e.activation(out=sq[:, :n], in_=xt[:, c, :n], func=AF.Square),
             reads=[Tx], writes=[Tsq])
        S.op("pe", lambda e, c=c, sq=sq: e.matmul(pss[:, :n], lhsT=ones[:], rhs=sq[:, :n],
                                                  start=(c == 0), stop=(c == KC - 1)),
             reads=[Tsq, To], writes=[Tps])
    rstd, Tr = scr['rstd']
    S.op("act", lambda e: e.activation(out=rstd[:, :n], in_=pss[:, :n], func=AF.Sqrt, scale=1.0 / D, bias=scr['eps'][0][:, 0:1]),
         reads=[Tps, scr['eps'][1]], writes=[Tr])
    S.op("dve", lambda e: e.reciprocal(out=rstd[:, :n], in_=rstd[:, :n]), reads=[Tr], writes=[Tr])
    for c in range(KC):
        tmp, Tt = scr['tmp'][c % 2]
        S.op("dve", lambda e, c=c, tmp=tmp: e.tensor_tensor(out=tmp[:, :n], in0=xt[:, c, :n], in1=rstd[:, :n], op=ALU.mult),
             reads=[Tx, Tr], writes=[Tt])
        S.op("act", lambda e, c=c, tmp=tmp: e.activation(out=h[:, c, :n], in_=tmp[:, :n], func=AF.Identity,
                                                        scale=eff[:, c:c + 1], bias=sh[:, c:c + 1]),
             reads=[Tt, Tv], writes=[Th])


def norm_mod2(C, xt, Tx, n, ones, To, segs, Tv, h, Th, pss, Tps, scr):
    S = C.S
    for c in range(KC):
        sq, Tsq = scr['sq'][c % 2]
        S.op("act", lambda e, c=c, sq=sq: e.activation(out=sq[:, :n], in_=xt[:, c, :n], func=AF.Square), reads=[Tx], writes=[Tsq])
        S.op("pe", lambda e, c=c, sq=sq: e.matmul(pss[:, :n], lhsT=ones[:], rhs=sq[:, :n], start=(c == 0), stop=(c == KC - 1)),
             reads=[Tsq, To], writes=[Tps])
    rstd, Tr = scr['rstd']
    S.op("act", lambda e: e.activation(out=rstd[:, :n], in_=pss[:, :n], func=AF.Sqrt, scale=1.0 / D, bias=scr['eps'][0][:, 0:1]),
         reads=[Tps, scr['eps'][1]], writes=[Tr])
    S.op("dve", lambda e: e.reciprocal(out=rstd[:, :n], in_=rstd[:, :n]), reads=[Tr], writes=[Tr])
    for c in range(KC):
        tmp, Tt = scr['tmp'][c % 2]
        S.op("dve", lambda e, c=c, tmp=tmp: e.tensor_tensor(out=tmp[:, :n], in0=xt[:, c, :n], in1=rstd[:, :n], op=ALU.mult),
             reads=[Tx, Tr], writes=[Tt])
        for (c0, c1, eff, sh) in segs:
            S.op("act", lambda e, c=c, tmp=tmp, c0=c0, c1=c1, eff=eff, sh=sh: e.activation(out=h[:, c, c0:c1], in_=tmp[:, c0:c1], func=AF.Identity,
                                                                                          scale=eff[:, c:c + 1], bias=sh[:, c:c + 1]),
                 reads=[Tt, Tv], writes=[Th])


def make_scr(C, w=512):
    scr = {}
    scr['sq'] = [C.sb([128, w], F32) for _ in range(2)]
    scr['tmp'] = [C.sb([128, w], F32) for _ in range(2)]
    scr['rstd'] = C.sb([128, w], F32)
    scr['eps'] = C.sb([128, 1], F32, "epsc")
    C.S.op("pool", lambda e: e.memset(scr['eps'][0][:], EPS), writes=[scr['eps'][1]])
    return scr


NMIX = 6208
A_TILES = [(0, 512, 0), (512, 512, 0), (1024, 64, 1)]
NT_A = 1088


def build_A():
    C = Ctx()
    nc, S = C.nc, C.S
    xT = C.dram_in("xT", [D, NT_A])
    vecs = C.dram_in("vecs", [128, 5, 16])
    wmix = C.dram_in("wmix", [D, NMIX])
    pT = C.dram_out("pT", [NMIX, NT_A])
    ones, To = consts(C)
    scr = make_scr(C)
    vt, Tv = C.sb([128, 5, 16], F32, "vecs_sb")
    eff, _ = C.sb([128, 2, 16], F32, "eff")
    S.dma("sp", lambda e: e.dma_start(out=vt[:], in_=vecs[:, :, :]), writes=[Tv])
    for s, (isc) in enumerate((1, 3)):
        S.op("dve", lambda e, s=s, isc=isc: e.scalar_tensor_tensor(out=eff[:, s, :], in0=vt[:, isc, :], scalar=1.0, in1=vt[:, 0, :],
                                                                   op0=ALU.add, op1=ALU.mult), reads=[Tv], writes=[Tv])
    h, Th = C.sb([128, KC, NT_A], BF16, "h")
    xv = xT.rearrange("(c p) t -> p c t", p=128)
    pss, Tps = C.ps(name="pss")
    xbufs = [C.sb([128, KC, 512], F32) for _ in range(2)]
    for ti, (t0, n, vs) in enumerate(A_TILES):
        xt, Tx = xbufs[ti % 2]
        S.dma("sp", lambda e, xt=xt, t0=t0, n=n: e.dma_start(out=xt[:, :, :n], in_=xv[:, :, t0:t0 + n]), writes=[Tx])
        hv = h[:, :, t0:t0 + n]
        norm_mod(C, xt, Tx, n, ones, To, eff[:, vs, :], vt[:, 2 + 2 * vs, :], Tv, hv, Th, pss, Tps, scr)
    wv = wmix.rearrange("(c p) n -> p c n", p=128)
    wb = [C.sb([128, KC, 512], BF16) for _ in range(3)]
    pbs = [C.ps() for _ in range(4)]
    obs = [C.sb([128, 512], F32) for _ in range(4)]
    nslab = (NMIX + 511) // 512
    g = 0
    for s in range(nslab):
        c0 = s * 512
        cw = min(512, NMIX - c0)
        w, Tw = wb[s % 3]
        S.dma("pool", lambda e, w=w, c0=c0, cw=cw: e.dma_start(out=w[:, :, :cw], in_=wv[:, :, c0:c0 + cw]), writes=[Tw])
        for (t0, n, vs) in A_TILES:
            for oc in range((cw + 127) // 128):
                m = min(128, cw - oc * 128)
                pb, Tp = pbs[g % 4]
                ob, Tob = obs[g % 4]
                for k in range(KC):
                    S.op("pe", lambda e, pb=pb, w=w, k=k, oc=oc, m=m, t0=t0, n=n: e.matmul(
                        pb[:m, :n], lhsT=w[:, k, oc * 128:oc * 128 + m], rhs=h[:, k, t0:t0 + n],
                        start=(k == 0), stop=(k == KC - 1)), reads=[Tw, Th], writes=[Tp])
                ev = "act" if g % 2 == 0 else "dve"
                if ev == "act":
                    S.op("act", lambda e, pb=pb, ob=ob, m=m, n=n: e.copy(out=ob[:m, :n], in_=pb[:m, :n]), reads=[Tp], writes=[Tob])
                else:
                    S.op("dve", lambda e, pb=pb, ob=ob, m=m, n=n: e.tensor_copy(out=ob[:m, :n], in_=pb[:m, :n]), reads=[Tp], writes=[Tob])
                r0 = c0 + oc * 128
                S.dma("sp", lambda e, ob=ob, r0=r0, m=m, t0=t0, n=n: e.dma_start(out=pT[r0:r0 + m, t0:t0 + n], in_=ob[:m, :n]),
                      reads=[Tob])
                g += 1
    return C.finish([t for _, t in obs])


POOLMAP = {}
FP32R = False


def _mm(C, out, lhsT, rhs, R, W, start=True, stop=True):
    if FP32R and lhsT.dtype == F32 and rhs.dtype == F32:
        lhsT = lhsT.bitcast(mybir.dt.float32r)
        rhs = rhs.bitcast(mybir.dt.float32r)
    return C.S.op("pe", lambda e: e.matmul(out, lhsT=lhsT, rhs=rhs, start=start, stop=stop), reads=R, writes=W)


def _tr(C, out, in_, ident, R, W):
    return C.S.op("pe", lambda e: e.transpose(out, in_, ident), reads=R, writes=W)


def _act(C, out, in_, func, R, W, scale=None, bias=None, accum=None):
    kw = {}
    if scale is not None:
        kw["scale"] = scale
    if bias is not None:
        kw["bias"] = bias
    if accum is not None:
        kw["accum_out"] = accum
    return C.S.op("act", lambda e: e.activation(out=out, in_=in_, func=func, **kw), reads=R, writes=W)


def _tt(C, eng, out, in0, in1, op, R, W):
    eng = POOLMAP.get(eng, eng)
    return C.S.op(eng, lambda e: e.tensor_tensor(out=out, in0=in0, in1=in1, op=op), reads=R, writes=W)


def _ts(C, eng, out, in0, s1, s2, op0, op1, R, W):
    eng = POOLMAP.get(eng, eng)
    if op1 is None:
        return C.S.op(eng, lambda e: e.tensor_scalar(out=out, in0=in0, scalar1=s1, scalar2=None, op0=op0), reads=R, writes=W)
    return C.S.op(eng, lambda e: e.tensor_scalar(out=out, in0=in0, scalar1=s1, scalar2=s2, op0=op0, op1=op1), reads=R, writes=W)


def _stt(C, out, in0, scalar, in1, op0, op1, R, W):
    return C.S.op("dve", lambda e: e.scalar_tensor_tensor(out=out, in0=in0, scalar=scalar, in1=in1, op0=op0, op1=op1), reads=R, writes=W)


def _sc(C, out, in_, scale_ap, R, W):
    return C.S.op("act", lambda e: e.activation(out=out, in_=in_, func=AF.Identity, scale=scale_ap), reads=R, writes=W)


def _cp(C, eng, out, in_, R, W):
    if eng == "act":
        return C.S.op("act", lambda e: e.copy(out=out, in_=in_), reads=R, writes=W)
    return C.S.op(eng, lambda e: e.tensor_copy(out=out, in_=in_), reads=R, writes=W)


class Ring:
    def __init__(self, items):
        self.items = items
        self.i = 0

    def get(self):
        it = self.items[self.i % len(self.items)]
        self.i += 1
        return it


TB = 4352
NCH = 34
NTOK = 2 * TB
CI, CLO, CUP, CSLO, CSUP, CTLO, CTUP, CRM = [i * 128 for i in range(8)]
NCST = 8 * 128
MUL, ADD, SUB, MAX, MIN = ALU.mult, ALU.add, ALU.subtract, ALU.max, ALU.min


def b_consts():
    p = np.arange(128)[:, None]
    f = np.arange(128)[None, :]
    NEG = -30000.0
    cst = np.zeros((128, NCST), np.float32)
    cst[:, CI:CI + 128] = (p == f)
    cst[:, CLO:CLO + 128] = np.where(f <= p, 0.0, NEG)
    cst[:, CUP:CUP + 128] = np.where(f >= p, 0.0, NEG)
    cst[:, CSLO:CSLO + 128] = (f < p)
    cst[:, CSUP:CSUP + 128] = (f > p)
    cst[:, CTLO:CTLO + 128] = (p >= f)
    cst[:, CTUP:CTUP + 128] = (p <= f)
    rm = np.zeros((128, 128), np.float32)
    for base in (0, 64):
        for m in range(32):
            rm[base + m + 32, base + m] = -1.0
            rm[base + m, base + m + 32] = 1.0
    cst[:, CRM:CRM + 128] = rm
    t = np.arange(4096)
    row = (t // 64).astype(np.float32)
    col = (t % 64).astype(np.float32)
    inv = (10000.0 ** (-np.arange(32, dtype=np.float32) / 32)).astype(np.float32)
    cosF = np.zeros((128, 4096), np.float32)
    sinF = np.zeros((128, 4096), np.float32)
    for pp in range(128):
        pos = row if pp < 64 else col
        ang = (pos * inv[pp % 32]).astype(np.float32)
        cosF[pp] = np.cos(ang)
        sinF[pp] = np.sin(ang)
    return cst, cosF, sinF


def build_B(which=("gdn", "ssd", "attn"), batches=(0, 1)):
    extra_fns = {}
    C = Ctx()
    nc, S = C.nc, C.S
    names = ["gq", "gk", "gv", "sx", "sB", "sC", "aq", "ak", "av"]
    din = {n: C.dram_in(n, [128, NTOK]) for n in names}
    gates = C.dram_in("gates", [128, 2 * NCH, 8])
    pvd = C.dram_in("pv", [128, 40])
    cstd = C.dram_in("cst", [128, NCST])
    cosd = C.dram_in("cosF", [128, 4096])
    sind = C.dram_in("sinF", [128, 4096])
    outs = {n: C.dram_out(n, [NTOK, 128]) for n in ("oa", "ob", "oc")}

    cst, Tc = C.sb([128, NCST], F32, "cst_sb")
    S.dma("sp", lambda e: e.dma_start(out=cst[:], in_=cstd[:, :]), writes=[Tc])
    pv, Tpv = C.sb([128, 40], F32, "pv_sb")
    S.dma("sp", lambda e: e.dma_start(out=pv[:], in_=pvd[:, :]), writes=[Tpv])
    ident = cst[:, CI:CI + 128]
    ones, To = consts(C)
    epsc, Te = C.sb([128, 1], F32, "epsc")
    S.op("pool", lambda e: e.memset(epsc[:], EPS), writes=[Te])
    onec, T1 = C.sb([128, 1], F32, "onec")
    S.op("pool", lambda e: e.memset(onec[:], 1.0), writes=[T1])

    slabs = [None] + [C.sb([128, TB], F32, f"slab{i}") for i in range(1, 8)]
    slabs[0] = slabs[6]
    gt, Tg = C.sb([128, NCH, 8], F32, "gates_sb")
    banks = [C.ps(name=f"bank{i}") for i in range(8)]
    small = [(banks[bi][0][:, 0:128], banks[bi][1]) for bi in range(8)]
    PS = Ring(small)
    wide = Ring([(banks[6 + i][0], banks[6 + i][1]) for i in range(2)])

    def ringsb(n, shape=(128, 128), dt=F32):
        return Ring([C.sb(list(shape), dt) for _ in range(n)])

    def conv_silu(raw, Traw, dst, Tdst, wcol, bias_ap=None):
        for (s0, s1) in ((0, 256), (256, TB)):
            S.op("act", lambda e, s0=s0, s1=s1: e.activation(out=dst[:, s0:s1], in_=raw[:, s0:s1], func=AF.Identity,
                                                             scale=wcol[:, 1:2], **({} if bias_ap is None else {"bias": bias_ap})),
                 reads=[Traw, Tpv], writes=[Tdst])
            _stt(C, dst[:, s0 + 1:s1], raw[:, s0:s1 - 1], wcol[:, 0:1], dst[:, s0 + 1:s1], MUL, ADD, [Traw, Tpv, Tdst], [Tdst])
            _stt(C, dst[:, s0:s1 - 1], raw[:, s0 + 1:s1], wcol[:, 2:3], dst[:, s0:s1 - 1], MUL, ADD, [Traw, Tpv, Tdst], [Tdst])
        _act(C, dst[:, :], dst[:, :], AF.Silu, [Tdst], [Tdst])

    sqr = ringsb(2, (128, 512))

    def l2norm(dst, Tdst, mul):
        for c0 in range(0, TB, 512):
            n = min(512, TB - c0)
            sq, Tsq = sqr.get()
            _act(C, sq[:, :n], dst[:, c0:c0 + n], AF.Square, [Tdst], [Tsq])
            pw, Tpw = wide.get()
            _mm(C, pw[:, :n], ones[:], sq[:, :n], [Tsq, To], [Tpw])
            _act(C, sq[:, :n], pw[:, :n], AF.Sqrt, [Tpw, Te], [Tsq], bias=epsc[:, 0:1])
            S.op("dve", lambda e, sq=sq, n=n: e.reciprocal(out=sq[:, :n], in_=sq[:, :n]), reads=[Tsq], writes=[Tsq])
            _stt(C, dst[:, c0:c0 + n], dst[:, c0:c0 + n], float(mul), sq[:, :n], MUL, MUL, [Tdst, Tsq], [Tdst])

    def to_tm(src, Tsrc, dst, Tdst, chunks=range(NCH)):
        for ch in chunks:
            pt, Tp = PS.get()
            _tr(C, pt, src[:, ch * 128:(ch + 1) * 128], ident, [Tsrc, Tc], [Tp])
            _cp(C, "act" if ch % 2 else "dve", dst[:, ch * 128:(ch + 1) * 128], pt, [Tp], [Tdst])

    def load(name, b, dst, Tdst):
        S.dma("sp", lambda e: e.dma_start(out=dst[:, :], in_=din[name][:, b * TB:(b + 1) * TB]), writes=[Tdst])

    def store(oname, b, src, Tsrc):
        ov = outs[oname].rearrange("(c p) d -> p c d", p=128)
        sv = src[:, :].rearrange("p (c d) -> p c d", d=128)
        for c0 in range(0, NCH, 6):
            c1 = min(NCH, c0 + 6)
            S.dma("sp", lambda e, c0=c0, c1=c1: e.dma_start(out=ov[:, b * NCH + c0:b * NCH + c1, :], in_=sv[:, c0:c1, :]),
                  reads=[Tsrc])

    tabcache = {}
    sbcache = {}

    def sbc(shape, name):
        if name not in sbcache:
            sbcache[name] = C.sb(shape, F32, name)
        return sbcache[name]

    def tab(name):
        if name not in tabcache:
            tabcache[name] = C.sb([128, NCH], F32, name)
        return tabcache[name]

    def cum_tabs(gsrc_ap, Tsrc, d, names):
        tri = cst[:, CTUP:CTUP + 128] if d == 0 else cst[:, CTLO:CTLO + 128]
        r = {}
        pc, Tpc = PS.get()
        _mm(C, pc[:, :NCH], tri, gsrc_ap, [Tc, Tsrc], [Tpc])
        r['gc'] = tab(names + "gc")
        _cp(C, "dve", r['gc'][0][:, :], pc[:, :NCH], [Tpc], [r['gc'][1]])
        pl, Tpl = PS.get()
        _mm(C, pl[:, :NCH], ones[:], gsrc_ap, [To, Tsrc], [Tpl])
        r['gl'] = tab(names + "gl")
        _cp(C, "dve", r['gl'][0][:, :], pl[:, :NCH], [Tpl], [r['gl'][1]])
        r['ngc'] = tab(names + "ngc")
        _ts(C, "dve", r['ngc'][0][:, :], r['gc'][0][:, :], -1.0, None, MUL, None, [r['gc'][1]], [r['ngc'][1]])
        r['eg'] = tab(names + "eg")
        _act(C, r['eg'][0][:, :], r['gc'][0][:, :], AF.Exp, [r['gc'][1]], [r['eg'][1]])
        r['el'] = tab(names + "el")
        _act(C, r['el'][0][:, :], r['gl'][0][:, :], AF.Exp, [r['gl'][1]], [r['el'][1]])
        r['kd'] = tab(names + "kd")
        _tt(C, "dve", r['kd'][0][:, :], r['gl'][0][:, :], r['gc'][0][:, :], SUB, [r['gl'][1], r['gc'][1]], [r['kd'][1]])
        _act(C, r['kd'][0][:, :], r['kd'][0][:, :], AF.Exp, [r['kd'][1]], [r['kd'][1]])
        return r

    R = {k: ringsb(3) for k in ("diag", "t1", "t2", "vb", "kbg", "vnew")}
    R.update({k: ringsb(4) for k in ("Dm", "DTm", "EG", "Nm", "NmT", "P", "PT", "XT")})
    R.update({k: ringsb(5) for k in ("u", "wT", "attnT", "qdT", "kdec")})

    def lockstep(gens):
        gens = list(gens)
        while gens:
            nxt = []
            for g in gens:
                try:
                    next(g)
                    nxt.append(g)
                except StopIteration:
                    pass
            gens = nxt

    def grow_mats(gc_tab, ch, d, want_D, out, want_EG=True):
        gc, Tgc = gc_tab['gc']
        ngc, Tngc = gc_tab['ngc']
        LO = cst[:, CLO:CLO + 128]
        UP = cst[:, CUP:CUP + 128]
        negm, negmT = (LO, UP) if d == 0 else (UP, LO)
        dg, Tdg = R["diag"].get()
        _sc(C, dg[:, :], ident, gc[:, ch:ch + 1], [Tc, Tgc], [Tdg])
        yield
        pg, Tpg = PS.get()
        _mm(C, pg, ones[:], dg[:, :], [To, Tdg], [Tpg])
        yield
        t2, Tt2 = R["t2"].get()
        _tt(C, "dve", t2[:, :], pg, negmT, ADD, [Tpg, Tc], [Tt2])
        if want_D:
            t1, Tt1 = R["t1"].get()
            _stt(C, t1[:, :], pg, -1.0, negm, MUL, ADD, [Tpg, Tc], [Tt1])
        yield
        DTm, TDT = R["DTm"].get()
        _act(C, DTm[:, :], t2[:, :], AF.Exp, [Tt2, Tngc], [TDT], bias=ngc[:, ch:ch + 1])
        out['DTm'] = (DTm, TDT)
        if want_D:
            Dm, TD = R["Dm"].get()
            _act(C, Dm[:, :], t1[:, :], AF.Exp, [Tt1, Tgc], [TD], bias=gc[:, ch:ch + 1])
            out['Dm'] = (Dm, TD)
        if want_EG:
            EG, TEG = R["EG"].get()
            _act(C, EG[:, :], pg, AF.Exp, [Tpg], [TEG])
            out['EG'] = (EG, TEG)
        yield

    def gdn(b):
        raw, Traw = slabs[0]
        qf, Tq = slabs[1]
        kf, Tk = slabs[2]
        vf, Tv = slabs[3]
        ktm, Tktm = slabs[4]
        vtm, Tvtm = slabs[5]
        obs = [slabs[6], slabs[7]]
        S.dma("sp", lambda e: e.dma_start(out=gt[:], in_=gates[:, b * NCH:(b + 1) * NCH, :]), writes=[Tg])
        for i, (nm, (dst, Td)) in enumerate((("gq", slabs[1]), ("gk", slabs[2]), ("gv", slabs[3]))):
            load(nm, b, raw, Traw)
            conv_silu(raw, Traw, dst, Td, pv[:, 3 * i:3 * i + 3])
        l2norm(qf, Tq, 128 ** -0.5)
        l2norm(kf, Tk, 1.0)
        to_tm(kf, Tk, ktm, Tktm)
        to_tm(vf, Tv, vtm, Tvtm)
        nA, TnA = sbc([128, 2], "gdn_nA")
        _act(C, nA[:, :], pv[:, 18:20], AF.Exp, [Tpv], [TnA])
        _ts(C, "dve", nA[:, :], nA[:, :], -1.0, None, MUL, None, [TnA], [TnA])
        tabs = []
        betas = []
        for d in range(2):
            g, Tgd = tab(f"gdn_g{d}")
            _act(C, g[:, :], gt[:, :, d], AF.Exp, [Tg, Tpv], [Tgd], bias=pv[:, 20 + d:21 + d])
            _act(C, g[:, :], g[:, :], AF.Ln, [Tgd, T1], [Tgd], bias=onec[:, 0:1])
            _ts(C, "dve", g[:, :], g[:, :], nA[:, d:d + 1], None, MUL, None, [Tgd, TnA], [Tgd])
            tb = cum_tabs(g[:, :], Tgd, d, f"gdn{d}")
            be, Tbe = tab(f"gdn_beta{d}")
            _act(C, be[:, :], gt[:, :, 2 + d], AF.Sigmoid, [Tg], [Tbe])
            nb_, Tnb = tab(f"gdn_nbeta{d}")
            _ts(C, "dve", nb_[:, :], be[:, :], -1.0, None, MUL, None, [Tbe], [Tnb])
            bg, Tbg = tab(f"gdn_bg{d}")
            _tt(C, "dve", bg[:, :], be[:, :], tb['eg'][0][:, :], MUL, [Tbe, tb['eg'][1]], [Tbg])
            tb['beta'] = (be, Tbe)
            tb['nbeta'] = (nb_, Tnb)
            tb['bg'] = (bg, Tbg)
            tabs.append(tb)
        Sst = [sbc([128, 128], f"gdnS{d}") for d in range(2)]
        for d in range(2):
            S.op("pool", lambda e, d=d: e.memset(Sst[d][0][:], 0.0), writes=[Sst[d][1]])

        def pre(d, ch, m):
            tb = tabs[d]
            cs = slice(ch * 128, (ch + 1) * 128)
            gm = {}
            yield from grow_mats(tb, ch, d, True, gm)
            Dm, TD = gm['Dm']
            DTm, TDT = gm['DTm']
            EG, TEG = gm['EG']
            strict = cst[:, CSLO:CSLO + 128] if d == 0 else cst[:, CSUP:CSUP + 128]
            _tt(C, "pool", Dm[:, :], Dm[:, :], strict, MUL, [TD, Tc], [TD])
            pk, Tpk = PS.get()
            _mm(C, pk, kf[:, cs], kf[:, cs], [Tk], [Tpk])
            yield
            Nm, TN = R["Nm"].get()
            _stt(C, Nm[:, :], pk, tb['nbeta'][0][:, ch:ch + 1], Dm[:, :], MUL, MUL, [Tpk, tb['nbeta'][1], TD], [TN])
            yield
            pt, Tpt = PS.get()
            _tr(C, pt, Nm[:, :], ident, [TN, Tc], [Tpt])
            yield
            NmT, TNT = R["NmT"].get()
            _cp(C, "act", NmT[:, :], pt, [Tpt], [TNT])
            yield
            XT, TX = R["XT"].get()
            _tt(C, "dve", XT[:, :], NmT[:, :], ident, ADD, [TNT, Tc], [TX])
            P, TP, PT, TPT = Nm, TN, NmT, TNT
            for s_ in range(1, 7):
                pp, Tpp = PS.get()
                _mm(C, pp, PT[:, :], P[:, :], [TPT, TP], [Tpp])
                if s_ < 6:
                    pq, Tpq = PS.get()
                    _mm(C, pq, P[:, :], PT[:, :], [TP, TPT], [Tpq])
                yield
                Pn, TPn = R["P"].get()
                _cp(C, "act", Pn[:, :], pp, [Tpp], [TPn])
                if s_ < 6:
                    PTn, TPTn = R["PT"].get()
                    _cp(C, "dve", PTn[:, :], pq, [Tpq], [TPTn])
                yield
                px, Tpx = PS.get()
                _mm(C, px, Pn[:, :], XT[:, :], [TPn, TX], [Tpx])
                yield
                _tt(C, "dve", XT[:, :], px, XT[:, :], ADD, [Tpx, TX], [TX])
                P, TP = Pn, TPn
                if s_ < 6:
                    PT, TPT = PTn, TPTn
                yield
            vb, Tvb = R["vb"].get()
            _sc(C, vb[:, :], vtm[:, cs], tb['beta'][0][:, ch:ch + 1], [Tvtm, tb['beta'][1]], [Tvb])
            kbg, Tkbg = R["kbg"].get()
            _sc(C, kbg[:, :], ktm[:, cs], tb['bg'][0][:, ch:ch + 1], [Tktm, tb['bg'][1]], [Tkbg])
            pa, Tpa = PS.get()
            _mm(C, pa, kf[:, cs], qf[:, cs], [Tk, Tq], [Tpa])
            yield
            pu, Tpu = PS.get()
            _mm(C, pu, XT[:, :], vb[:, :], [TX, Tvb], [Tpu])
            pw, Tpw = PS.get()
            _mm(C, pw, kbg[:, :], XT[:, :], [Tkbg, TX], [Tpw])
            attnT, TaT = R["attnT"].get()
            _tt(C, "dve", attnT[:, :], pa, DTm[:, :], MUL, [Tpa, TDT], [TaT])
            qdT, TqdT = R["qdT"].get()
            _tt(C, "pool", qdT[:, :], qf[:, cs], EG[:, :], MUL, [Tq, TEG], [TqdT])
            kdec, Tkd = R["kdec"].get()
            _sc(C, kdec[:, :], ktm[:, cs], tb['kd'][0][:, ch:ch + 1], [Tktm, tb['kd'][1]], [Tkd])
            yield
            u, Tu = R["u"].get()
            _cp(C, "act", u[:, :], pu, [Tpu], [Tu])
            wT, TwT = R["wT"].get()
            _cp(C, "dve", wT[:, :], pw, [Tpw], [TwT])
            m.update(u=(u, Tu), wT=(wT, TwT), attnT=(attnT, TaT), qdT=(qdT, TqdT), kdec=(kdec, Tkd))

        def seq(d, ch, m):
            tb = tabs[d]
            St, TS = Sst[d]
            ob, Tob = obs[d]
            cs = slice(ch * 128, (ch + 1) * 128)
            p1, Tp1 = PS.get()
            _mm(C, p1, m['wT'][0][:, :], St[:, :], [m['wT'][1], TS], [Tp1])
            p2, Tp2 = PS.get()
            _mm(C, p2, m['qdT'][0][:, :], St[:, :], [m['qdT'][1], TS], [Tp2], start=True, stop=False)
            yield
            vn, Tvn = R["vnew"].get()
            _tt(C, "dve", vn[:, :], m['u'][0][:, :], p1, SUB, [m['u'][1], Tp1], [Tvn])
            yield
            _mm(C, p2, m['attnT'][0][:, :], vn[:, :], [m['attnT'][1], Tvn], [Tp2], start=False, stop=True)
            p3, Tp3 = PS.get()
            _mm(C, p3, m['kdec'][0][:, :], vn[:, :], [m['kdec'][1], Tvn], [Tp3])
            yield
            _stt(C, St[:, :], St[:, :], tb['el'][0][:, ch:ch + 1], p3, MUL, ADD, [TS, tb['el'][1], Tp3], [TS])
            _cp(C, "act", ob[:, cs], p2, [Tp2], [Tob])

        order = [[0, 1] + list(range(2, NCH)), [1, 0] + list(range(NCH - 1, 1, -1))]
        pend = [None, None]
        for step in range(NCH + 1):
            cur = [None, None]
            gens = []
            if step < NCH:
                for d in range(2):
                    cur[d] = (order[d][step], {})
                    gens.append(pre(d, cur[d][0], cur[d][1]))
            if step > 0:
                for d in range(2):
                    gens.append(seq(d, pend[d][0], pend[d][1]))
            lockstep(gens)
            pend = cur
        S.counting = False
        _tt(C, "pool", obs[0][0][:, :], obs[0][0][:, :], obs[1][0][:, :], ADD, [obs[0][1], obs[1][1]], [obs[0][1]])
        store("oa", b, obs[0][0], obs[0][1])

    R.update({k: ringsb(4) for k in ("CBs",)})
    R.update({k: ringsb(5) for k in ("MT", "CdT")})
    R.update({k: ringsb(6, (128, 64)) for k in ("xdt", "xw")})

    def ssd(b):
        raw, Traw = slabs[0]
        xf, Txf = slabs[1]
        Bf, TBf = slabs[2]
        Cf, TCf = slabs[3]
        xtm, Txtm = slabs[4]
        Btm, TBtm = slabs[5]
        ybs = [slabs[6], slabs[7]]
        S.dma("sp", lambda e: e.dma_start(out=gt[:], in_=gates[:, b * NCH:(b + 1) * NCH, :]), writes=[Tg])
        for i, (nm, (dst, Td)) in enumerate((("sx", slabs[1]), ("sB", slabs[2]), ("sC", slabs[3]))):
            load(nm, b, raw, Traw)
            conv_silu(raw, Traw, dst, Td, pv[:, 9 + 3 * i:12 + 3 * i], bias_ap=pv[:, 22 + i:23 + i])
        to_tm(xf, Txf, xtm, Txtm)
        to_tm(Bf, TBf, Btm, TBtm)
        nA, TnA = sbc([128, 4], "ssd_nA")
        _act(C, nA[:, :], pv[:, 25:29], AF.Exp, [Tpv], [TnA])
        _ts(C, "dve", nA[:, :], nA[:, :], -1.0, None, MUL, None, [TnA], [TnA])
        tabs = {}
        for d in range(2):
            for hh in range(2):
                k = 2 * d + hh
                dt_, Tdt = tab(f"ssd_dt{k}")
                _act(C, dt_[:, :], gt[:, :, 4 + k], AF.Exp, [Tg, Tpv], [Tdt], bias=pv[:, 29 + k:30 + k])
                _act(C, dt_[:, :], dt_[:, :], AF.Ln, [Tdt, T1], [Tdt], bias=onec[:, 0:1])
                a_, Ta = tab(f"ssd_a{k}")
                _ts(C, "dve", a_[:, :], dt_[:, :], nA[:, k:k + 1], None, MUL, None, [Tdt, TnA], [Ta])
                tb = cum_tabs(a_[:, :], Ta, d, f"ssd{k}")
                tb['dt'] = (dt_, Tdt)
                tabs[(d, hh)] = tb
        hst = [sbc([128, 128], f"ssdH{d}") for d in range(2)]
        for d in range(2):
            S.op("pool", lambda e, d=d: e.memset(hst[d][0][:], 0.0), writes=[hst[d][1]])

        def step(d, ch):
            cs = slice(ch * 128, (ch + 1) * 128)
            H, TH = hst[d]
            yb, Tyb = ybs[d]
            pcb, Tpcb = PS.get()
            _mm(C, pcb, Bf[:, cs], Cf[:, cs], [TBf, TCf], [Tpcb])
            gms = [{}, {}]
            yield from grow_mats(tabs[(d, 0)], ch, d, False, gms[0])
            CBs, TCB = R["CBs"].get()
            _cp(C, "act", CBs[:, :], pcb, [Tpcb], [TCB])
            yield from grow_mats(tabs[(d, 1)], ch, d, False, gms[1])
            hd = []
            for hh in range(2):
                tb = tabs[(d, hh)]
                DTm, TDT = gms[hh]['DTm']
                EG, TEG = gms[hh]['EG']
                MT, TMT = R["MT"].get()
                _tt(C, "pool", MT[:, :], CBs[:, :], DTm[:, :], MUL, [TCB, TDT], [TMT])
                CdT, TCd = R["CdT"].get()
                _tt(C, "pool", CdT[:, :], Cf[:, cs], EG[:, :], MUL, [TCf, TEG], [TCd])
                xdt, Txd = R["xdt"].get()
                _ts(C, "dve", xdt[:, :], xtm[:, ch * 128 + hh * 64:ch * 128 + hh * 64 + 64], tb['dt'][0][:, ch:ch + 1], None, MUL, None,
                    [Txtm, tb['dt'][1]], [Txd])
                xw, Txw = R["xw"].get()
                _ts(C, "dve", xw[:, :], xdt[:, :], tb['kd'][0][:, ch:ch + 1], None, MUL, None, [Txd, tb['kd'][1]], [Txw])
                hd.append((MT, TMT, CdT, TCd, xdt, Txd, xw, Txw, tb))
            yield
            py, Tpy = PS.get()
            for hh in range(2):
                MT, TMT, CdT, TCd, xdt, Txd, xw, Txw, tb = hd[hh]
                hc = slice(hh * 64, hh * 64 + 64)
                _mm(C, py[:, hc], MT[:, :], xdt[:, :], [TMT, Txd], [Tpy], start=True, stop=False)
                _mm(C, py[:, hc], CdT[:, :], H[:, hc], [TCd, TH], [Tpy], start=False, stop=True)
            ph, Tph = PS.get()
            for hh in range(2):
                xw, Txw = hd[hh][6], hd[hh][7]
                hc = slice(hh * 64, hh * 64 + 64)
                _mm(C, ph[:, hc], Btm[:, cs], xw[:, :], [TBtm, Txw], [Tph])
            yield
            _cp(C, "act", yb[:, cs], py, [Tpy], [Tyb])
            for hh in range(2):
                tb = hd[hh][8]
                hc = slice(hh * 64, hh * 64 + 64)
                _stt(C, H[:, hc], H[:, hc], tb['el'][0][:, ch:ch + 1], ph[:, hc], MUL, ADD, [TH, tb['el'][1], Tph], [TH])

        order = [[0, 1] + list(range(2, NCH)), [1, 0] + list(range(NCH - 1, 1, -1))]
        for st_ in range(NCH):
            lockstep([step(d, order[d][st_]) for d in range(2)])
        yv = ybs[0][0][:, :].rearrange("p (c h q) -> p c h q", h=2, q=64)
        xv = xtm[:, :].rearrange("p (c h q) -> p c h q", h=2, q=64)
        for hh in range(2):
            _stt(C, yv[:, :, hh, :], xv[:, :, hh, :], pv[:, 33 + hh:34 + hh], yv[:, :, hh, :], MUL, ADD, [Txtm, Tpv, ybs[0][1]], [ybs[0][1]])
        _tt(C, "pool", ybs[0][0][:, :], ybs[0][0][:, :], ybs[1][0][:, :], ADD, [ybs[0][1], ybs[1][1]], [ybs[0][1]])
        store("ob", b, ybs[0][0], ybs[0][1])

    amask, Tam = C.sb([128, 384], F32, "amask")
    _cp(C, "pool", amask[:, 0:128], cst[:, CUP:CUP + 128], [Tc], [Tam])
    S.op("pool", lambda e: e.memset(amask[:, 128:256], 0.0), writes=[Tam])
    _cp(C, "pool", amask[:, 256:384], cst[:, CLO:CLO + 128], [Tc], [Tam])
    nsk, Tnsk = C.sb([128, 1], F32, "nsink")
    _ts(C, "dve", nsk[:, :], pv[:, 35:36], -1.0, None, MUL, None, [Tpv], [Tnsk])
    csr = ringsb(1, (128, 512))
    snr = ringsb(1, (128, 512))
    rtmp = sqr
    scr_ = ringsb(3, (128, 640))
    pTr = ringsb(11)
    sm = Ring([tuple(C.sb([128, 1], F32) for _ in range(5)) for _ in range(4)])
    PSW = Ring([(banks[i][0], banks[i][1]) for i in range(8)])
    SCALE = 128 ** -0.5

    def attn(b):
        qf, Tq = slabs[1]
        kf, Tk = slabs[2]
        vf, Tv = slabs[3]
        vtm, Tvtm = slabs[4]
        ob, Tob = slabs[5]
        load("aq", b, qf, Tq)
        load("ak", b, kf, Tk)
        load("av", b, vf, Tv)
        to_tm(vf, Tv, vtm, Tvtm)
        for p0 in range(0, 4096, 512):
            cs_, Tcs = csr.get()
            sn_, Tsn = snr.get()
            S.dma("sp", lambda e, cs_=cs_, p0=p0: e.dma_start(out=cs_[:, :], in_=cosd[:, p0:p0 + 512]), writes=[Tcs])
            S.dma("sp", lambda e, sn_=sn_, p0=p0: e.dma_start(out=sn_[:, :], in_=sind[:, p0:p0 + 512]), writes=[Tsn])
            for (x, Tx) in ((qf, Tq), (kf, Tk)):
                xs = x[:, 256 + p0:256 + p0 + 512]
                pr, Tpr = wide.get()
                _mm(C, pr[:, :], cst[:, CRM:CRM + 128], xs, [Tc, Tx], [Tpr])
                tm_, Ttm = rtmp.get()
                _tt(C, "dve", tm_[:, :], pr[:, :], sn_[:, :], MUL, [Tpr, Tsn], [Ttm])
                _tt(C, "pool", xs, xs, cs_[:, :], MUL, [Tx, Tcs], [Tx])
                _tt(C, "dve", xs, xs, tm_[:, :], ADD, [Tx, Ttm], [Tx])
        def ablock(qc):
            qcs = slice(qc * 128, (qc + 1) * 128)
            sc, Tsc = scr_.get()
            mx, nm, rs, es, rd = [t for t in sm.get()]
            if qc < 2:
                W = 0
                kchunks = [0, 1]
            else:
                n = qc - 2
                lo, hi = max(n - 1, 0), min(n + 1, 31)
                W = (hi - lo + 1) * 128
                m0 = 0 if lo == n - 1 else 128
                pa, Tpa = PSW.get()
                _mm(C, pa[:, :W], qf[:, qcs], kf[:, 256 + lo * 128:256 + lo * 128 + W], [Tq, Tk], [Tpa])
                kchunks = [2 + lo + i for i in range(hi - lo + 1)] + [0, 1]
            pb_, Tpb = PSW.get()
            _mm(C, pb_[:, :256], qf[:, qcs], kf[:, 0:256], [Tq, Tk], [Tpb])
            yield
            if W:
                _tt(C, "dve", sc[:, :W], pa[:, :W], amask[:, m0:m0 + W], ADD, [Tpa, Tam], [Tsc])
            _cp(C, "act", sc[:, W:W + 256], pb_[:, :256], [Tpb], [Tsc])
            yield
            WT = W + 256
            S.op("dve", lambda e, sc=sc, mx=mx, WT=WT: e.tensor_reduce(out=mx[0][:, :], in_=sc[:, :WT], axis=AX.X, op=MAX), reads=[Tsc], writes=[mx[1]])
            yield
            _ts(C, "dve", nm[0][:, :], mx[0][:, :], -SCALE, nsk[:, 0:1], MUL, MIN, [mx[1], Tnsk], [nm[1]])
            yield
            _act(C, sc[:, :WT], sc[:, :WT], AF.Exp, [Tsc, nm[1]], [Tsc, rs[1]], scale=SCALE, bias=nm[0][:, 0:1], accum=rs[0][:, 0:1])
            _act(C, es[0][:, :], pv[:, 35:36], AF.Exp, [Tpv, nm[1]], [es[1]], bias=nm[0][:, 0:1])
            yield
            _tt(C, "dve", rd[0][:, :], rs[0][:, :], es[0][:, :], ADD, [rs[1], es[1]], [rd[1]])
            S.op("dve", lambda e, rd=rd: e.reciprocal(out=rd[0][:, :], in_=rd[0][:, :]), reads=[rd[1]], writes=[rd[1]])
            nk = len(kchunks)
            pts = []
            for i in range(nk):
                pt, Tpt = PS.get()
                _tr(C, pt, sc[:, i * 128:(i + 1) * 128], ident, [Tsc, Tc], [Tpt])
                yield
                pT, TpT = pTr.get()
                _cp(C, "act" if i % 2 else "dve", pT[:, :], pt, [Tpt], [TpT])
                pts.append((pT, TpT))
            yield
            po, Tpo = PS.get()
            for i, kc in enumerate(kchunks):
                pT, TpT = pts[i]
                _mm(C, po, pT[:, :], vtm[:, kc * 128:(kc + 1) * 128], [TpT, Tvtm], [Tpo], start=(i == 0), stop=(i == nk - 1))
            yield
            _act(C, ob[:, qcs], po, AF.Identity, [Tpo, rd[1]], [Tob], scale=rd[0][:, 0:1])

        for q0 in range(0, NCH, 2):
            lockstep([ablock(q0), ablock(q0 + 1)])
        store("oc", b, ob, Tob)

    extra_fns = dict(ssd=ssd, attn=attn)
    fns = dict(gdn=gdn)
    fns.update(extra_fns)
    for nm in which:
        for b in batches:
            fns[nm](b)
    return C.finish([t for _, t in slabs[1:]])


NT_C = 1092
NGATE = 8192
DFF = 5632
FC = 44
P1_TILES = [(0, 342, [(0, 342, 0)]), (342, 342, [(0, 342, 0)]), (684, 408, [(0, 342, 0), (342, 408, 1)])]
P2_TILES = [(1, 343, [(0, 342, 0, 0)]), (343, 685, [(0, 342, 0, 342)]), (685, 1091, [(0, 340, 0, 684), (342, 406, 1, 1024)])]
NW = 410


def build_C():
    C = Ctx()
    nc, S = C.nc, C.S
    xT = C.dram_in("xT", [D, NT_C])
    oT = C.dram_in("oT", [3072, NT_C])
    hmd = C.dram_in("hmask", [128, 4])
    vecs = C.dram_in("vecs", [128, 16, 16])
    pvd = C.dram_in("pvc", [128, 185])
    wg = C.dram_in("wg", [D, NGATE])
    wbr = C.dram_in("wbr", [3072, D])
    wout = C.dram_in("wout", [D, D])
    wup = C.dram_in("wup", [D, 2 * DFF])
    wdn = C.dram_in("wdn", [DFF, D])
    xo = C.dram_out("xoT", [D, 1088])
    ones, To = consts(C)
    scr = make_scr(C, NW)
    vt, Tv = C.sb([128, 16, 16], F32, "vecs_sb")
    S.dma("sp", lambda e: e.dma_start(out=vt[:], in_=vecs[:, :, :]), writes=[Tv])
    pv, Tpv = C.sb([128, 185], F32, "pvc_sb")
    S.dma("sp", lambda e: e.dma_start(out=pv[:], in_=pvd[:, :]), writes=[Tpv])
    hm, Thm = C.sb([128, 4], F32, "hm_sb")
    S.dma("sp", lambda e: e.dma_start(out=hm[:], in_=hmd[:, :]), writes=[Thm])
    ef, Tef = C.sb([128, 2, 4, 16], F32, "eff")
    for s in range(2):
        b0 = 4 + 6 * s
        S.op("dve", lambda e, s=s, b0=b0: e.scalar_tensor_tensor(out=ef[:, s, 0, :], in0=vt[:, b0 + 1, :], scalar=1.0, in1=vt[:, 0, :], op0=ADD, op1=MUL),
             reads=[Tv], writes=[Tef])
        _tt(C, "dve", ef[:, s, 1, :], vt[:, b0 + 2, :], vt[:, 1, :], MUL, [Tv], [Tef])
        S.op("dve", lambda e, s=s, b0=b0: e.scalar_tensor_tensor(out=ef[:, s, 2, :], in0=vt[:, b0 + 4, :], scalar=1.0, in1=vt[:, 2, :], op0=ADD, op1=MUL),
             reads=[Tv], writes=[Tef])
        _tt(C, "dve", ef[:, s, 3, :], vt[:, b0 + 5, :], vt[:, 3, :], MUL, [Tv], [Tef])
    xs, Txs = C.sb([128, KC, NT_C], F32, "xres")
    xv = xT.rearrange("(c p) t -> p c t", p=128)
    for c0 in range(0, KC, 4):
        S.dma("sp", lambda e, c0=c0: e.dma_start(out=xs[:, c0:c0 + 4, :], in_=xv[:, c0:c0 + 4, :]), writes=[Txs])
    A16, TA16 = C.sb([128, KC, NW], BF16, "A16")
    ACT_, TACT = C.sb([128, FC, NW], BF16, "ACTb")
    F32A, TF32 = C.sb([128, KC, NW], F32, "F32A")
    M16, TM16 = A16, TA16
    wsl = Ring([C.sb([128, KC, 512], BF16) for _ in range(2)])
    pss, Tps = C.ps(name="pss")
    PB = Ring([C.ps() for _ in range(6)])
    st4 = [C.sb([128, NW], F32) for _ in range(4)]
    st_in = Ring(st4)
    tmpr = Ring([C.sb([128, NW], F32) for _ in range(2)])
    rsm, Trsm = scr["rstd"]
    ov = oT.rearrange("(c p) t -> p c t", p=128)

    def wload(dview, c0, cw, kcn):
        w, Tw = wsl.get()
        kl = kcn
        S.dma("pool", lambda e: e.dma_start(out=w[:, :kl, :cw], in_=dview[:, :kl, c0:c0 + cw]), writes=[Tw])
        return w, Tw

    wgv = wg.rearrange("(c p) n -> p c n", p=128)
    wbv = [wbr[br * 1024:(br + 1) * 1024, :].rearrange("(c p) n -> p c n", p=128) for br in range(3)]
    wov = wout.rearrange("(c p) n -> p c n", p=128)
    wuv = wup.rearrange("(c p) n -> p c n", p=128)
    wdv = [wdn[fg * 11 * 128:(fg + 1) * 11 * 128, :].rearrange("(c p) n -> p c n", p=128) for fg in range(4)]

    def stats(src_fn, Tsrc, nchunks, n, div):
        for c in range(nchunks):
            sq, Tsq = scr['sq'][c % 2]
            _act(C, sq[:, :n], src_fn(c), AF.Square, [Tsrc], [Tsq])
            _mm(C, pss[:, :n], ones[:], sq[:, :n], [Tsq, To], [Tps], start=(c == 0), stop=(c == nchunks - 1))
        _act(C, rsm[:, :n], pss[:, :n], AF.Sqrt, [Tps, scr['eps'][1]], [Trsm], scale=1.0 / div, bias=scr['eps'][0][:, 0:1])
        S.op("dve", lambda e: e.reciprocal(out=rsm[:, :n], in_=rsm[:, :n]), reads=[Trsm], writes=[Trsm])

    def gemm_chunk(w, Tw, wc, kcn, rhs_fn, Trhs, n):
        pb, Tp = PB.get()
        for k in range(kcn):
            _mm(C, pb[:, :n], w[:, k, wc:wc + 128], rhs_fn(k), [Tw, Trhs], [Tp], start=(k == 0), stop=(k == kcn - 1))
        return pb, Tp

    for (t0, n, segs1) in P1_TILES:
        norm_mod2(C, xs[:, :, t0:t0 + n], Txs, n, ones, To, [(c0, c1, ef[:, vs, 0, :], vt[:, 4 + 6 * vs, :]) for (c0, c1, vs) in segs1],
                  Tef, A16, TA16, pss, Tps, scr)
        hfn = lambda k: A16[:, k, :n]
        for sgrp in range(2):
            w, Tw = wload(wgv, sgrp * 512, 512, KC)
            for cc in range(4):
                c = sgrp * 4 + cc
                oin, Toin = st_in.get()
                S.dma("sp", lambda e, oin=oin, c=c, t0=t0, n=n: e.dma_start(out=oin[:, :n], in_=ov[:, c, t0:t0 + n]), writes=[Toin])
                stats(lambda _c, oin=oin: oin[:, :n], Toin, 1, n, 128.0)
                pb, Tp = gemm_chunk(w, Tw, cc * 128, KC, hfn, TA16, n)
                sg, Tsg = tmpr.get()
                _act(C, sg[:, :n], pb[:, :n], AF.Silu, [Tp], [Tsg])
                _tt(C, "dve", oin[:, :n], oin[:, :n], rsm[:, :n], MUL, [Toin, Trsm], [Toin])
                _stt(C, ACT_[:, c, :n], oin[:, :n], pv[:, 0:1], sg[:, :n], MUL, MUL, [Toin, Tpv, Tsg], [TACT])
        for sgrp in range(2):
            w, Tw = wload(wgv, 1024 + sgrp * 512, 512, KC)
            for cc in range(4):
                c = sgrp * 4 + cc
                oin, Toin = st4[cc]
                S.dma("sp", lambda e, oin=oin, c=c, t0=t0, n=n: e.dma_start(out=oin[:, :n], in_=ov[:, 8 + c, t0:t0 + n]), writes=[Toin])
                pb, Tp = gemm_chunk(w, Tw, cc * 128, KC, hfn, TA16, n)
                sg, Tsg = tmpr.get()
                _act(C, sg[:, :n], pb[:, :n], AF.Silu, [Tp], [Tsg])
                _tt(C, "dve", oin[:, :n], oin[:, :n], sg[:, :n], MUL, [Toin, Tsg], [Toin])
            for c_ in range(4):
                sq, Tsq = scr['sq'][c_ % 2]
                _act(C, sq[:, :n], st4[c_][0][:, :n], AF.Square, [st4[c_][1]], [Tsq])
                _mm(C, pss[:, :n], ones[:], sq[:, :n], [Tsq, To], [Tps], start=(c_ == 0), stop=(c_ == 3))
            _act(C, rsm[:, :n], pss[:, :n], AF.Sqrt, [Tps, scr['eps'][1]], [Trsm], scale=1.0 / 512.0, bias=scr['eps'][0][:, 0:1])
            S.op("dve", lambda e, n=n: e.reciprocal(out=rsm[:, :n], in_=rsm[:, :n]), reads=[Trsm], writes=[Trsm])
            for cc in range(4):
                c = sgrp * 4 + cc
                oin, Toin = st4[cc]
                _tt(C, "dve", oin[:, :n], oin[:, :n], rsm[:, :n], MUL, [Toin, Trsm], [Toin])
                _ts(C, "dve", ACT_[:, 8 + c, :n], oin[:, :n], pv[:, 1 + c:2 + c], None, MUL, None, [Toin, Tpv], [TACT])
        for c in range(8):
            oin, Toin = st_in.get()
            S.dma("sp", lambda e, oin=oin, c=c, t0=t0, n=n: e.dma_start(out=oin[:, :n], in_=ov[:, 16 + c, t0:t0 + n]), writes=[Toin])
            _cp(C, "act", ACT_[:, 16 + c, :n], oin[:, :n], [Toin], [TACT])
        for og in range(4):
            for br in range(3):
                wb_, Twb = wload(wbv[br], og * 512, 512, 8)
                wm_, Twm = wload(wgv, 2048 + br * 2048 + og * 512, 512, KC)
                for oc in range(4):
                    o = og * 4 + oc
                    p1, Tp1 = gemm_chunk(wb_, Twb, oc * 128, 8, lambda k, br=br: ACT_[:, br * 8 + k, :n], TACT, n)
                    p2, Tp2 = gemm_chunk(wm_, Twm, oc * 128, KC, hfn, TA16, n)
                    gt_, Tgt = tmpr.get()
                    _act(C, gt_[:, :n], p2[:, :n], AF.Sigmoid, [Tp2], [Tgt])
                    if br == 0:
                        _tt(C, "dve", F32A[:, o, :n], p1[:, :n], gt_[:, :n], MUL, [Tp1, Tgt], [TF32])
                    else:
                        _tt(C, "dve", gt_[:, :n], p1[:, :n], gt_[:, :n], MUL, [Tp1, Tgt], [Tgt])
                        _tt(C, "dve", F32A[:, o, :n], F32A[:, o, :n], gt_[:, :n], ADD, [TF32, Tgt], [TF32])
        for o in range(KC):
            _cp(C, "act", M16[:, o, :n], F32A[:, o, :n], [TF32], [TM16])
        for og in range(4):
            w, Tw = wload(wov, og * 512, 512, KC)
            for oc in range(4):
                o = og * 4 + oc
                pb, Tp = gemm_chunk(w, Tw, oc * 128, KC, lambda k: M16[:, k, :n], TM16, n)
                _cp(C, "act" if o % 2 else "dve", F32A[:, o, :n], pb[:, :n], [Tp], [TF32])
        stats(lambda c: F32A[:, c, :n], TF32, KC, n, float(D))
        for c in range(KC):
            tm_, Ttm = tmpr.get()
            _tt(C, "dve", tm_[:, :n], F32A[:, c, :n], rsm[:, :n], MUL, [TF32, Trsm], [Ttm])
            for (c0, c1, vs) in segs1:
                _stt(C, xs[:, c, t0 + c0:t0 + c1], tm_[:, c0:c1], ef[:, vs, 1, c:c + 1], xs[:, c, t0 + c0:t0 + c1], MUL, ADD, [Ttm, Tef, Txs], [Txs])

    cw_ = pv[:, 9:9 + 132]
    for (a, b, segs2) in P2_TILES:
        n2 = b - a + 2
        ni = b - a
        nsegs = [(i0, i1 + 2, ef[:, vs, 2, :], vt[:, 4 + 6 * vs + 3, :]) for (i0, i1, vs, _o) in segs2]
        norm_mod2(C, xs[:, :, a - 1:b + 1], Txs, n2, ones, To, nsegs, Tef, A16, TA16, pss, Tps, scr)
        for (lc, hi_) in ((0, 0), (1025, 1), (1026, 2), (1091, 3)):
            j = lc - (a - 1)
            if 0 <= j < n2:
                _ts(C, "dve", A16[:, :, j:j + 1], A16[:, :, j:j + 1], hm[:, hi_:hi_ + 1], None, MUL, None, [TA16, Thm], [TA16])
        hfn = lambda k: A16[:, k, :n2]
        for fg in range(11):
            wu, Twu = wload(wuv, fg * 512, 512, KC)
            wgt, Twgt = wload(wuv, DFF + fg * 512, 512, KC)
            for fc in range(4):
                f = fg * 4 + fc
                pu, Tpu = gemm_chunk(wu, Twu, fc * 128, KC, hfn, TA16, n2)
                pg, Tpg = gemm_chunk(wgt, Twgt, fc * 128, KC, hfn, TA16, n2)
                gc, Tgc = tmpr.get()
                _act(C, gc[:, :ni], pg[:, 1:1 + ni], AF.Identity, [Tpg, Tpv], [Tgc], scale=pv[:, 9 + 3 * f + 1:9 + 3 * f + 2], bias=pv[:, 141 + f:142 + f])
                _stt(C, gc[:, :ni], pg[:, 0:ni], pv[:, 9 + 3 * f:9 + 3 * f + 1], gc[:, :ni], MUL, ADD, [Tpg, Tpv, Tgc], [Tgc])
                _stt(C, gc[:, :ni], pg[:, 2:2 + ni], pv[:, 9 + 3 * f + 2:9 + 3 * f + 3], gc[:, :ni], MUL, ADD, [Tpg, Tpv, Tgc], [Tgc])
                _act(C, gc[:, :ni], gc[:, :ni], AF.Silu, [Tgc], [Tgc])
                _tt(C, "dve", ACT_[:, f, :ni], pu[:, 1:1 + ni], gc[:, :ni], MUL, [Tpu, Tgc], [TACT])
        for og in range(4):
            pbs = [PB.get() for _ in range(4)]
            for fg in range(4):
                w, Tw = wload(wdv[fg], og * 512, 512, 11)
                for oc in range(4):
                    pb, Tp = pbs[oc]
                    for k in range(11):
                        _mm(C, pb[:, :ni], w[:, k, oc * 128:(oc + 1) * 128], ACT_[:, fg * 11 + k, :ni], [Tw, TACT], [Tp],
                            start=(fg == 0 and k == 0), stop=(fg == 3 and k == 10))
            for oc in range(4):
                o = og * 4 + oc
                _cp(C, "act" if o % 2 else "dve", F32A[:, o, :ni], pbs[oc][0][:, :ni], [pbs[oc][1]], [TF32])
        stats(lambda c: F32A[:, c, :ni], TF32, KC, ni, float(D))
        for c in range(KC):
            _tt(C, "dve", F32A[:, c, :ni], F32A[:, c, :ni], rsm[:, :ni], MUL, [TF32, Trsm], [TF32])
            for (i0, i1, vs, _o) in segs2:
                _stt(C, F32A[:, c, i0:i1], F32A[:, c, i0:i1], ef[:, vs, 3, c:c + 1], xs[:, c, a + i0:a + i1], MUL, ADD, [TF32, Tef, Txs], [TF32])
        xov = xo.rearrange("(c p) t -> p c t", p=128)
        for (i0, i1, vs, oc0) in segs2:
            for c0 in range(0, KC, 4):
                S.dma("sp", lambda e, c0=c0, i0=i0, i1=i1, oc0=oc0: e.dma_start(out=xov[:, c0:c0 + 4, oc0:oc0 + (i1 - i0)], in_=F32A[:, c0:c0 + 4, i0:i1]), reads=[TF32])
    return C.finish([TF32])


MCOLS = 1536


def build_M():
    C = Ctx()
    nc, S = C.nc, C.S
    wada = C.dram_in("wada", [4 * D, MCOLS])
    bada = C.dram_in("bada", [3, 4 * MCOLS])
    cT = C.dram_in("cT", [128, 16, 3])
    out = C.dram_out("modT", [3, 4 * MCOLS])
    ct, Tct = C.sb([128, 16, 3], F32, "ct_sb")
    S.dma("sp", lambda e: e.dma_start(out=ct[:], in_=cT[:, :, :]), writes=[Tct])
    bt, Tbt = C.sb([3, 4 * MCOLS], F32, "bt_sb")
    S.dma("sp", lambda e: e.dma_start(out=bt[:], in_=bada[:, :]), writes=[Tbt])
    _act(C, ct[:, :, :], ct[:, :, :], AF.Silu, [Tct], [Tct])
    ot, Tot = C.sb([3, 4 * MCOLS], F32, "ot_sb")
    wb = Ring([C.sb([128, KC, 768], F32) for _ in range(2)])
    PB = Ring([C.ps() for _ in range(4)])
    for l in range(4):
        wv = wada[l * D:(l + 1) * D, :].rearrange("(c p) n -> p c n", p=128)
        for hf in range(2):
            w, Tw = wb.get()
            for c0 in range(0, KC, 4):
                S.dma("sp", lambda e, w=w, wv=wv, hf=hf, c0=c0: e.dma_start(out=w[:, c0:c0 + 4, :], in_=wv[:, c0:c0 + 4, hf * 768:(hf + 1) * 768]), writes=[Tw])
            for (g0, gw) in ((0, 512), (512, 256)):
                pb, Tp = PB.get()
                for k in range(KC):
                    _mm(C, pb[:3, :gw], ct[:, k, :], w[:, k, g0:g0 + gw], [Tct, Tw], [Tp], start=(k == 0), stop=(k == KC - 1))
                o0 = l * MCOLS + hf * 768 + g0
                _tt(C, "dve", ot[:, o0:o0 + gw], pb[:3, :gw], bt[:, o0:o0 + gw], ADD, [Tp, Tbt], [Tot])
    S.dma("sp", lambda e: e.dma_start(out=out[:, :], in_=ot[:, :]), reads=[Tot])
    return C.finish([Tot])


IN_SPLITS = (3072, 1024, 16, 16, 1024, 1536, 32, 1024, 512, 6144)
OFFS = np.cumsum((0,) + IN_SPLITS)
MIX_PARTS = (0, 2, 3, 5, 6, 7, 8)
MIXCOLS = np.concatenate([np.arange(OFFS[i], OFFS[i + 1]) for i in MIX_PARTS])
_mo = np.cumsum([0] + [IN_SPLITS[i] for i in MIX_PARTS])
M_QKV, M_A, M_B, M_XBC, M_DT, M_Q, M_KV = [int(v) for v in _mo[:7]]
GATECOLS = np.concatenate([np.arange(OFFS[i], OFFS[i + 1]) for i in (1, 4, 9)])
def b_inputs(PT, j, prm):
    g = j // 4
    sl = lambda r0: np.ascontiguousarray(PT[r0:r0 + 128])
    d = {}
    d["gq"] = sl(M_QKV + j * 128)
    d["gk"] = sl(M_QKV + 1024 + j * 128)
    d["gv"] = sl(M_QKV + 2048 + j * 128)
    d["sx"] = sl(M_XBC + j * 128)
    d["sB"] = sl(M_XBC + 1024 + g * 128)
    d["sC"] = sl(M_XBC + 1024 + 256 + g * 128)
    d["aq"] = sl(M_Q + j * 128)
    d["ak"] = sl(M_KV + g * 128)
    d["av"] = sl(M_KV + 256 + g * 128)
    rows = [M_A + j, M_A + 8 + j, M_B + j, M_B + 8 + j,
            M_DT + 2 * j, M_DT + 2 * j + 1, M_DT + 16 + 2 * j, M_DT + 16 + 2 * j + 1]
    gt = PT[rows]
    d["gates"] = np.ascontiguousarray(gt.reshape(8, 68, 128).transpose(2, 1, 0))
    pv = np.zeros((128, 40), np.float32)
    cw = prm["gdn_conv_w"]
    for i in range(3):
        pv[:, 3 * i:3 * i + 3] = cw[:, i * 1024 + j * 128:i * 1024 + (j + 1) * 128].T
    sw = prm["ssm_conv_w"]
    sb = prm["ssm_conv_b"]
    for i, c0 in enumerate((j * 128, 1024 + g * 128, 1280 + g * 128)):
        pv[:, 9 + 3 * i:12 + 3 * i] = sw[:, c0:c0 + 128].T
        pv[:, 22 + i] = sb[c0:c0 + 128]
    pv[:, 18] = prm["gdn_a_log"][0, j]; pv[:, 19] = prm["gdn_a_log"][1, j]
    pv[:, 20] = prm["gdn_dt_bias"][0, j]; pv[:, 21] = prm["gdn_dt_bias"][1, j]
    k = 0
    for dd in range(2):
        for hh in range(2):
            pv[:, 25 + k] = prm["ssm_a_log"][dd, 2 * j + hh]
            pv[:, 29 + k] = prm["ssm_dt_bias"][dd, 2 * j + hh]
            k += 1
    pv[:, 33] = prm["ssm_d"][2 * j]; pv[:, 34] = prm["ssm_d"][2 * j + 1]
    pv[:, 35] = prm["attn_sink"][j]
    d["pv"] = pv
    return d


def vec16(v):
    return np.ascontiguousarray(np.asarray(v, np.float32).reshape(16, 128).T)


def halo_slice(arr, lo, hi):
    T = arr.shape[0]
    out = np.zeros((hi - lo, arr.shape[1]), arr.dtype)
    m = np.zeros(hi - lo, np.float32)
    a, b = max(lo, 0), min(hi, T)
    out[a - lo:b - lo] = arr[a:b]
    m[a - lo:b - lo] = 1.0
    return out, m


def c_inputs(i, X, XC, OL, OC, mod, modc, prm):
    b, q = i // 4, i % 4
    xl, ml = halo_slice(X[b], q * 1024 - 1, (q + 1) * 1024 + 1)
    xc, mc = halo_slice(XC[b], q * 64 - 1, (q + 1) * 64 + 1)
    ol, _ = halo_slice(OL[b], q * 1024 - 1, (q + 1) * 1024 + 1)
    oc, _ = halo_slice(OC[b], q * 64 - 1, (q + 1) * 64 + 1)
    d = {}
    d["xT"] = np.ascontiguousarray(np.concatenate([xl, xc], 0).T)
    d["oT"] = np.ascontiguousarray(np.concatenate([ol, oc], 0).T)
    mm_ = np.concatenate([ml, mc])
    d["hmask"] = np.ascontiguousarray(np.broadcast_to(mm_[[0, 1025, 1026, 1091]][None, :], (128, 4))).astype(np.float32)
    vs = [prm["g_pre1"], prm["g_post1"], prm["g_pre2"], prm["g_post2"]]
    for m in (mod[b], modc):
        vs += [m[k * 2048:(k + 1) * 2048] for k in range(6)]
    d["vecs"] = np.ascontiguousarray(np.stack([vec16(v) for v in vs], 1))
    pv = np.zeros((128, 185), np.float32)
    pv[:, 0] = prm["gdn_norm"]
    pv[:, 1:9] = prm["ssm_norm"].reshape(8, 128).T
    pv[:, 9:141] = prm["ffn_conv_w"].reshape(3, 44, 128).transpose(2, 1, 0).reshape(128, 132)
    pv[:, 141:185] = prm["ffn_conv_b"].reshape(44, 128).T
    d["pvc"] = pv
    return d


def c_weights(W):
    return dict(wg=np.ascontiguousarray(W["w_in"][:, GATECOLS]),
                wbr=np.ascontiguousarray(np.concatenate([W["w_branch_a"], W["w_branch_b"], W["w_branch_c"]], 0)),
                wout=W["w_out"], wup=W["w_up"], wdn=W["w_down"])


def _run(nc, in_maps):
    res = run_bass_kernel_spmd(nc, in_maps, core_ids=list(range(8)))
    return res.results


def kernel(**inp):
    inp = {k: np.asarray(v) for k, v in inp.items()}
    x, ctx, c, c_ctx = inp["x"], inp["ctx"], inp["c"], inp["c_ctx"]
    f32 = np.float32
    ncM = build_M()
    cT = np.ascontiguousarray(np.stack([c[0], c[1], c_ctx], 0).reshape(3, 16, 128).transpose(2, 1, 0)).astype(f32)
    maps = []
    for i in range(8):
        cs = slice(i * 1536, (i + 1) * 1536)
        maps.append(dict(wada=np.ascontiguousarray(inp["w_ada"][:, :, cs].reshape(4 * 2048, 1536)),
                         bada=np.ascontiguousarray(np.broadcast_to(inp["b_ada"][:, cs].reshape(1, 4 * 1536), (3, 4 * 1536))).astype(f32),
                         cT=cT))
    res = _run(ncM, maps)
    mod_all = np.concatenate([r["modT"].reshape(3, 4, 1536).transpose(1, 0, 2) for r in res], -1)
    ncA, ncB, ncC = build_A(), build_B(), build_C()
    cst, cosF, sinF = b_consts()
    X = np.array(x, f32)
    XC = np.array(ctx, f32)
    pnames = ["gdn_conv_w", "gdn_a_log", "gdn_dt_bias", "gdn_norm", "ssm_conv_w", "ssm_conv_b", "ssm_a_log", "ssm_dt_bias", "ssm_d", "ssm_norm",
              "attn_sink", "g_pre1", "g_post1", "g_pre2", "g_post2", "ffn_conv_w", "ffn_conv_b", "w_in", "w_branch_a", "w_branch_b", "w_branch_c",
              "w_out", "w_up", "w_down"]
    for l in range(4):
        prm = {k: inp[k][l] for k in pnames}
        mod, modc = mod_all[l, 0:2], mod_all[l, 2]
        wmix = np.ascontiguousarray(prm["w_in"][:, MIXCOLS])
        maps = []
        for i in range(8):
            b, q = i // 4, i % 4
            xs = np.concatenate([X[b, q * 1024:(q + 1) * 1024], XC[b, q * 64:(q + 1) * 64]], 0)
            vecs = np.stack([vec16(prm["g_pre1"]), vec16(mod[b, 2048:4096]), vec16(mod[b, 0:2048]), vec16(modc[2048:4096]), vec16(modc[0:2048])], 1)
            maps.append(dict(xT=np.ascontiguousarray(xs.T), vecs=np.ascontiguousarray(vecs), wmix=wmix))
        res = _run(ncA, maps)
        PT = np.empty((6208, 8704), f32)
        for i in range(8):
            b, q = i // 4, i % 4
            PT[:, b * 4352 + 256 + q * 1024:b * 4352 + 256 + (q + 1) * 1024] = res[i]["pT"][:, :1024]
            PT[:, b * 4352 + q * 64:b * 4352 + (q + 1) * 64] = res[i]["pT"][:, 1024:]
        maps = []
        for j in range(8):
            m = b_inputs(PT, j, prm)
            m.update(cst=cst, cosF=cosF, sinF=sinF)
            maps.append(m)
        res = _run(ncB, maps)
        OL = np.empty((2, 4096, 3072), f32)
        OC = np.empty((2, 256, 3072), f32)
        for j in range(8):
            for mi, nm in enumerate(("oa", "ob", "oc")):
                o = res[j][nm]
                for b in range(2):
                    OC[b, :, mi * 1024 + j * 128:mi * 1024 + (j + 1) * 128] = o[b * 4352:b * 4352 + 256]
                    OL[b, :, mi * 1024 + j * 128:mi * 1024 + (j + 1) * 128] = o[b * 4352 + 256:(b + 1) * 4352]
        wts = c_weights(prm)
        maps = []
        for i in range(8):
            m = c_inputs(i, X, XC, OL, OC, mod, modc, prm)
            m.update(wts)
            maps.append(m)
        res = _run(ncC, maps)
        Xn = np.empty_like(X)
        XCn = np.empty_like(XC)
        for i in range(8):
            b, q = i // 4, i % 4
            xo = res[i]["xoT"]
            Xn[b, q * 1024:(q + 1) * 1024] = xo[:, :1024].T
            XCn[b, q * 64:(q + 1) * 64] = xo[:, 1024:].T
        X, XC = Xn, XCn
    return X.astype(f32)
```
